# Optimizing a Trainium2 kernel written in Bass

```python
import math
import jax, jax.numpy as jnp
from jax import lax
import numpy as np

D_MODEL = 1024
BATCH = 8
SEQ = 2048
DEPTH = 4

HEAD_DIM = 64
A_Q_HEADS = 8
A_KV_HEADS = 2
A_WIDTH = A_Q_HEADS * HEAD_DIM
A_KV_WIDTH = A_KV_HEADS * HEAD_DIM
WINDOW = 128
BLOCK = 128
B_WIDTH = 256
B_BLOCKS = 4
B_BLOCK_DIM = B_WIDTH // B_BLOCKS
CONV_WIDTH = 4
LRU_C = 8.0
C_HEADS = 4
C_WIDTH = C_HEADS * HEAD_DIM
DECAY_RANK = 32
AICL_RANK = 32
VRES_RANK = 16
C_SHIFT_WIDTH = 3 * C_WIDTH + DECAY_RANK + AICL_RANK
GN_EPS = 64e-5

D_MIX = A_WIDTH + B_WIDTH + C_WIDTH
SPLIT_SIZES = (A_WIDTH, A_KV_WIDTH, A_KV_WIDTH, A_WIDTH,
               B_WIDTH, B_WIDTH,
               C_SHIFT_WIDTH, C_WIDTH)
D_IN = sum(SPLIT_SIZES)
LN_EPS = 1e-5
ALPHA = (2 * DEPTH) ** 0.25
BETA = (8 * DEPTH) ** -0.25

kernel_name = "hybrid_swa_rglru_rwkv7_deepnorm"


def _layer_norm(x, g, b):
    xf = x.astype(jnp.float32)
    mu = jnp.mean(xf, axis=-1, keepdims=True)
    var = jnp.mean(jnp.square(xf - mu), axis=-1, keepdims=True)
    return (xf - mu) * lax.rsqrt(var + LN_EPS) * g + b


def _swa_sinks(q, k, v, sinks):
    bsz, seq = q.shape[0], q.shape[1]
    nb = seq // BLOCK
    grp = A_Q_HEADS // A_KV_HEADS
    qb = q.astype(jnp.float32).reshape(bsz, nb, BLOCK, A_KV_HEADS, grp, HEAD_DIM) * (HEAD_DIM ** -0.5)

    def band(t):
        t = t.astype(jnp.float32).reshape(bsz, seq, A_KV_HEADS, HEAD_DIM)
        prev = jnp.pad(t, ((0, 0), (BLOCK, 0), (0, 0), (0, 0)))[:, :seq]
        prev = prev.reshape(bsz, nb, BLOCK, A_KV_HEADS, HEAD_DIM)
        cur = t.reshape(bsz, nb, BLOCK, A_KV_HEADS, HEAD_DIM)
        return jnp.concatenate([prev, cur], axis=2)

    kb, vb = band(k), band(v)
    s = jnp.einsum("bnqhgd,bnkhd->bnhgqk", qb, kb)
    qi = jnp.arange(BLOCK)[:, None]
    kj = jnp.arange(2 * BLOCK)[None, :]
    diff = qi + BLOCK - kj
    band_mask = (diff >= 0) & (diff < WINDOW)
    key_pos = jnp.arange(nb)[:, None] * BLOCK - BLOCK + jnp.arange(2 * BLOCK)[None, :]
    mask = band_mask[None] & (key_pos >= 0)[:, None, :]
    s = jnp.where(mask[None, :, None, None], s, -jnp.inf)
    sink = sinks.astype(jnp.float32).reshape(A_KV_HEADS, grp)[None, None, :, :, None, None]
    m = jnp.maximum(jnp.max(s, axis=-1, keepdims=True), sink)
    p = jnp.exp(s - m)
    denom = jnp.sum(p, axis=-1, keepdims=True) + jnp.exp(sink - m)
    o = jnp.einsum("bnhgqk,bnkhd->bnqhgd", p / denom, vb)
    return o.reshape(bsz, seq, A_WIDTH)


def _rg_lru(xb, conv_w, conv_b, wa, ba, wx, bx, lam):
    bsz, seq = xb.shape[0], xb.shape[1]
    xf = xb.astype(jnp.float32)
    xc = lax.conv_general_dilated(
        xf, conv_w.astype(jnp.float32).reshape(CONV_WIDTH, 1, B_WIDTH),
        window_strides=(1,), padding=[(CONV_WIDTH - 1, 0)],
        dimension_numbers=("NWC", "WIO", "NWC"), feature_group_count=B_WIDTH) + conv_b
    xh = xc.reshape(bsz, seq, B_BLOCKS, B_BLOCK_DIM)
    r = jax.nn.sigmoid(jnp.einsum("bsnd,nde->bsne", xh, wa).reshape(bsz, seq, B_WIDTH) + ba)
    i = jax.nn.sigmoid(jnp.einsum("bsnd,nde->bsne", xh, wx).reshape(bsz, seq, B_WIDTH) + bx)
    log_a = -LRU_C * r * jax.nn.softplus(-lam.astype(jnp.float32))
    a = jnp.exp(log_a)
    u = jnp.sqrt(-jnp.expm1(2.0 * log_a)) * (i * xc)

    def combine(e1, e2):
        a1, b1 = e1
        a2, b2 = e2
        return a1 * a2, a2 * b1 + b2

    _, h = lax.associative_scan(combine, (a, u), axis=1)
    return h


def _rwkv7(cs, mu, w0, w2, a0, a2, k_k, k_a, r_k, gn_w, gn_b, vres):
    bsz, seq = cs.shape[0], cs.shape[1]
    cs = cs.astype(jnp.float32)
    prev = jnp.pad(cs, ((0, 0), (1, 0), (0, 0)))[:, :seq]
    xs = cs + (prev - cs) * mu
    r, k, v, wl, al = jnp.split(xs, [C_WIDTH, 2 * C_WIDTH, 3 * C_WIDTH, 3 * C_WIDTH + DECAY_RANK], axis=-1)
    logw = -jax.nn.softplus(-(w0 + jnp.tanh(wl) @ w2)) - 0.5
    decay = jnp.exp(-jnp.exp(logw))
    a = jax.nn.sigmoid(a0 + al @ a2)
    v_own = v
    if vres is not None:
        v_first, v0, v1, v2 = vres
        v = v + (v_first - v) * jax.nn.sigmoid(v0 + (v @ v1) @ v2)

    def heads(t):
        return t.reshape(bsz, seq, C_HEADS, HEAD_DIM)

    kk = heads(k * k_k)
    kk = kk * lax.rsqrt(jnp.sum(jnp.square(kk), axis=-1, keepdims=True) + 1e-12)
    k = k * (1.0 + (a - 1.0) * k_a)
    rh, wh, kh, vh, ah = heads(r), heads(decay), heads(k), heads(v), heads(a)
    bh = kk * ah

    def step(state, inp):
        r_t, w_t, k_t, v_t, kk_t, b_t = inp
        sa = jnp.einsum("bhvk,bhk->bhv", state, -kk_t)
        state = (state * w_t[:, :, None, :] + sa[..., None] * b_t[:, :, None, :]
                 + v_t[..., None] * k_t[:, :, None, :])
        y_t = jnp.einsum("bhvk,bhk->bhv", state, r_t)
        return state, y_t

    tm = lambda t: jnp.swapaxes(t, 0, 1)
    s0 = jnp.zeros((bsz, C_HEADS, HEAD_DIM, HEAD_DIM), jnp.float32)
    _, y = lax.scan(step, s0, (tm(rh), tm(wh), tm(kh), tm(vh), tm(kk), tm(bh)))
    y = tm(y)
    ym = jnp.mean(y, axis=-1, keepdims=True)
    yv = jnp.mean(jnp.square(y - ym), axis=-1, keepdims=True)
    y = (y - ym) * lax.rsqrt(yv + GN_EPS) * gn_w.reshape(C_HEADS, HEAD_DIM) + gn_b.reshape(C_HEADS, HEAD_DIM)
    bonus = jnp.sum(rh * kh * r_k, axis=-1, keepdims=True) * vh
    return (y + bonus).reshape(bsz, seq, C_WIDTH), v_own


def setup_inputs(seed: int = 0) -> dict:
    key = jax.random.key(seed)
    ks = jax.random.split(key, 32)
    n = lambda i, shape: jax.random.normal(ks[i], shape, jnp.float32)
    u_lru = jax.random.uniform(ks[11], (DEPTH, B_WIDTH), jnp.float32, 0.9, 0.999)
    a_base = u_lru ** (1.0 / LRU_C)
    w0_base = jnp.linspace(-6.0, -1.0, C_WIDTH, dtype=jnp.float32)[None, :]
    return {
        "x": n(0, (BATCH, SEQ, D_MODEL)),
        "w_in": n(1, (DEPTH, D_MODEL, D_IN)) * D_MODEL ** -0.5,
        "w_out": n(2, (DEPTH, D_MIX, D_MODEL)) * (D_MIX ** -0.5) * BETA,
        "ln_g": 1.0 + 0.02 * n(3, (DEPTH, D_MODEL)),
        "ln_b": 0.02 * n(4, (DEPTH, D_MODEL)),
        "attn_sinks": 0.5 * n(5, (DEPTH, A_Q_HEADS)),
        "conv_w": n(6, (DEPTH, CONV_WIDTH, B_WIDTH)) * CONV_WIDTH ** -0.5,
        "conv_b": 0.02 * n(7, (DEPTH, B_WIDTH)),
        "lru_wa": n(8, (DEPTH, B_BLOCKS, B_BLOCK_DIM, B_BLOCK_DIM)) * B_BLOCK_DIM ** -0.5,
        "lru_ba": 0.02 * n(9, (DEPTH, B_WIDTH)),
        "lru_wx": n(10, (DEPTH, B_BLOCKS, B_BLOCK_DIM, B_BLOCK_DIM)) * B_BLOCK_DIM ** -0.5,
        "lru_bx": 0.02 * n(12, (DEPTH, B_WIDTH)),
        "lru_lambda": jnp.log(a_base) - jnp.log1p(-a_base),
        "rwkv_mu": jax.random.uniform(ks[13], (DEPTH, C_SHIFT_WIDTH), jnp.float32),
        "rwkv_w0": w0_base + 0.1 * n(14, (DEPTH, C_WIDTH)),
        "rwkv_w2": 0.1 * n(15, (DEPTH, DECAY_RANK, C_WIDTH)),
        "rwkv_a0": 0.1 * n(16, (DEPTH, C_WIDTH)),
        "rwkv_a2": 0.1 * n(17, (DEPTH, AICL_RANK, C_WIDTH)),
        "rwkv_kk": 0.85 + 0.02 * n(18, (DEPTH, C_WIDTH)),
        "rwkv_ka": 1.0 + 0.02 * n(19, (DEPTH, C_WIDTH)),
        "rwkv_rk": -0.04 + 0.02 * n(20, (DEPTH, C_HEADS, HEAD_DIM)),
        "rwkv_gn_w": 1.0 + 0.02 * n(21, (DEPTH, C_WIDTH)),
        "rwkv_gn_b": 0.02 * n(22, (DEPTH, C_WIDTH)),
        "rwkv_v0": 1.0 + 0.1 * n(23, (DEPTH - 1, C_WIDTH)),
        "rwkv_v1": n(24, (DEPTH - 1, C_WIDTH, VRES_RANK)) * C_WIDTH ** -0.5,
        "rwkv_v2": 0.1 * n(25, (DEPTH - 1, VRES_RANK, C_WIDTH)),
    }


def reference(x, w_in, w_out, ln_g, ln_b, attn_sinks, conv_w, conv_b, lru_wa, lru_ba,
              lru_wx, lru_bx, lru_lambda, rwkv_mu, rwkv_w0, rwkv_w2, rwkv_a0, rwkv_a2,
              rwkv_kk, rwkv_ka, rwkv_rk, rwkv_gn_w, rwkv_gn_b, rwkv_v0, rwkv_v1, rwkv_v2):
    split_idx = [int(c) for c in np.cumsum(SPLIT_SIZES)[:-1]]
    v_first = None
    for l in range(DEPTH):
        proj = jnp.einsum("bsd,de->bse", x, w_in[l])
        q, k, v, g_a, x_b, g_b, c_cols, g_c = jnp.split(proj, split_idx, axis=-1)
        y_a = _swa_sinks(q, k, v, attn_sinks[l])
        y_b = _rg_lru(x_b, conv_w[l], conv_b[l], lru_wa[l], lru_ba[l], lru_wx[l], lru_bx[l], lru_lambda[l])
        vres = None if l == 0 else (v_first, rwkv_v0[l - 1], rwkv_v1[l - 1], rwkv_v2[l - 1])
        y_c, v_own = _rwkv7(c_cols, rwkv_mu[l], rwkv_w0[l], rwkv_w2[l], rwkv_a0[l], rwkv_a2[l],
                            rwkv_kk[l], rwkv_ka[l], rwkv_rk[l], rwkv_gn_w[l], rwkv_gn_b[l], vres)
        if l == 0:
            v_first = v_own
        silu = lambda t: jax.nn.silu(t.astype(jnp.float32))
        y = jnp.concatenate([y_a * silu(g_a), y_b * silu(g_b), y_c * silu(g_c)], axis=-1)
        out = jnp.einsum("bse,ed->bsd", y, w_out[l])
        x = _layer_norm(ALPHA * x.astype(jnp.float32) + out, ln_g[l], ln_b[l]).astype(x.dtype)
    return x
```

```python
import math
from contextlib import ExitStack

import numpy as np
import concourse.bass as bass
import concourse.mybir as mybir
from concourse.bass_utils import run_bass_kernel_spmd

F32 = mybir.dt.float32
BF16 = mybir.dt.bfloat16
AF = mybir.ActivationFunctionType
ALU = mybir.AluOpType

T = 2048
TB = 512
NTB = T // TB
D = 1024
KC = 8
DEPTH = 4
CH = 64
SB = 256
NCH = SB // CH
NSB = T // SB
NSLOT = 8
NWCH = 32
ALPHA = (2 * DEPTH) ** 0.25
LN_EPS = 1e-5
GN_EPS = 64e-5
CW = math.exp(-0.5)
NEG = -30000.0
NDSEM = 40

PV_LNG, PV_LNB, PV_SINK, PV_CONVW, PV_CONVB, PV_BA, PV_BX, PV_LAM = 0, 8, 16, 20, 28, 30, 32, 34
PV_MU, PV_MUCW, PV_W0, PV_A0, PV_KK, PV_KA, PV_RK, PV_GNW, PV_GNB, PV_V0 = 36, 42, 43, 45, 47, 49, 51, 53, 55, 57
NPV = 64
DV_OMM, DV_SE, DV_LC, DV_LC2, DV_OKA = 0, 7, 11, 13, 15
DV_NBA, DV_NBX, DV_NW0, DV_NA0, DV_NV0, DV_EPSLN, DV_EPSGN, DV_EPSKK = 17, 19, 21, 23, 25, 27, 28, 29
NDV = 32
SW_LRU, SW_WA2, SW_V1, SW_V2, NSW = 0, 512, 768, 800, 1056
CB_ID, CB_ID2, CB_MP, CB_MC, CB_RM, CB_ML, CB_BO, CB_CM, CB_ON, NCB = 0, 128, 192, 704, 1216, 1344, 1408, 1536, 2048, 2112


class Op:
    __slots__ = ("idx", "eng", "fn", "dma", "deps", "odeps", "signal", "semval", "dsem", "dval", "dguard",
                 "cost", "start", "finish", "pos", "barrier")


SCHED = True
PRIO_BLEV = True
BLEV_PE_W = 0.55
CSCALE = {"pe": 1.0, "act": 1.0, "dve": 1.0, "pool": 1.0, "sp": 1.0}
XLAT = 0.5


class Prog:
    ENG = ("pe", "act", "dve", "pool", "sp")

    def __init__(self):
        self.ops = []
        self.lastw = {}
        self.readers = {}
        self.fence = {}
        self.since_fence = {e: [] for e in self.ENG}

    def add(self, eng, fn, r=(), w=(), dma=False, cost=0.3):
        op = Op()
        op.idx = len(self.ops)
        op.eng = eng
        op.fn = fn
        op.dma = dma
        op.signal = False
        op.semval = 0
        op.cost = cost
        op.barrier = False
        deps = {}
        ps_r = [k for k in r if k[0] == "PS"]
        if ps_r:
            w = list(w) + [k for k in ps_r if k not in w]

        def dep(d):
            if d is not None:
                deps[d.idx] = d

        for k in r:
            dep(self.lastw.get(k))
        for k in w:
            if k not in self.lastw and k not in self.readers:
                for d in self.fence.values():
                    dep(d)
            dep(self.lastw.get(k))
            for d in self.readers.get(k, ()):
                dep(d)
        deps.pop(op.idx, None)
        op.odeps = list(deps.values())
        op.deps = [d for d in op.odeps if d.dma or dma or not (d.eng == eng and eng == "pe")]
        for k in w:
            self.lastw[k] = op
            self.readers[k] = []
        for k in r:
            self.readers.setdefault(k, []).append(op)
        self.ops.append(op)
        if not dma and fn is not None:
            for k in list(r) + list(w):
                if isinstance(k[0], str) and "_L" in k[0]:
                    self.since_fence[eng].append(op)
                    break
        return op

    def set_fence(self):
        for e in self.ENG:
            prev = self.since_fence[e]
            if not prev:
                continue
            b = Op()
            b.idx = len(self.ops)
            b.eng = e
            b.fn = None
            b.dma = False
            b.signal = False
            b.semval = 0
            b.cost = 0.0
            b.barrier = True
            b.odeps = list(prev)
            b.deps = []
            self.ops.append(b)
            self.fence[e] = b
            self.since_fence[e] = [b]

    def schedule(self):
        ops = self.ops
        per_eng = {e: [] for e in self.ENG}
        if not SCHED:
            for op in ops:
                op.start = float(op.idx)
                op.pos = len(per_eng[op.eng])
                per_eng[op.eng].append(op)
            return per_eng, list(ops)
        succ = [[] for _ in ops]
        indeg = [0] * len(ops)
        cst = [op.cost * (1.0 if op.dma else CSCALE[op.eng]) for op in ops]
        for op in ops:
            indeg[op.idx] = len(op.odeps)
            for d in op.odeps:
                succ[d.idx].append(op)
        blev = [0.0] * len(ops)
        for op in reversed(ops):
            b_ = 0.0
            for s_ in succ[op.idx]:
                v = blev[s_.idx] + (0.0 if (s_.eng == op.eng and not op.dma) else XLAT)
                if v > b_:
                    b_ = v
            blev[op.idx] = b_ + cst[op.idx] * (BLEV_PE_W if op.eng == "pe" else 1.0)
        ready = {e: [] for e in self.ENG}
        eng_t = {e: 0.0 for e in self.ENG}

        def push(op):
            rt = 0.0
            for d in op.odeps:
                t = d.finish + (0.0 if (d.eng == op.eng and not d.dma) else XLAT)
                if t > rt:
                    rt = t
            ready[op.eng].append((rt, (-blev[op.idx] if PRIO_BLEV else op.idx), op))

        for op in ops:
            if indeg[op.idx] == 0:
                push(op)
        order = []
        for _ in range(len(ops)):
            best = None
            for e in self.ENG:
                rl = ready[e]
                if not rl:
                    continue
                te = eng_t[e]
                c = min(rl, key=lambda x: (x[0] if x[0] > te else te, x[1]))
                st = c[0] if c[0] > te else te
                if best is None or (st, c[1]) < (best[0], best[1][1]):
                    best = (st, c, e)
            st, c, e = best
            ready[e].remove(c)
            op = c[2]
            op.start = st
            if op.dma:
                eng_t[e] = st + 0.06
                op.finish = st + op.cost
            else:
                op.finish = st + cst[op.idx]
                eng_t[e] = op.finish
            op.pos = len(per_eng[e])
            per_eng[e].append(op)
            order.append(op)
            for s_ in succ[op.idx]:
                indeg[s_.idx] -= 1
                if indeg[s_.idx] == 0:
                    push(s_)
        self.est_us = max(eng_t.values())
        return per_eng, order

    def emit(self, nc, block, esems, dsems):
        per_eng, order = self.schedule()
        def real_before(d):
            lst = per_eng[d.eng]
            i = d.pos - 1
            while i >= 0:
                c = lst[i]
                if c.fn is not None and not c.dma:
                    return c
                i -= 1
            return None

        for op in order:
            if op.barrier:
                op.deps = []
                continue
            lastdep = {}
            keep = []
            for d in op.deps:
                if d.barrier:
                    d = real_before(d)
                    if d is None:
                        continue
                if d.dma:
                    keep.append(d)
                else:
                    cur = lastdep.get(d.eng)
                    if cur is None or d.pos > cur.pos:
                        lastdep[d.eng] = d
            for d in lastdep.values():
                if d.eng == op.eng and d.pos > op.pos:
                    raise RuntimeError("scheduler order violation")
                d.signal = True
                keep.append(d)
            op.deps = keep
        cnt = {e: 0 for e in self.ENG}
        dval = [0] * len(dsems)
        rrs = {"pool": 0, "sp": 0}
        half = len(dsems) // 2
        for op in order:
            if op.dma:
                if op.eng == "pool":
                    s = rrs["pool"] % half
                    rrs["pool"] += 1
                else:
                    s = half + rrs["sp"] % (len(dsems) - half)
                    rrs["sp"] += 1
                op.dsem = s
                op.dguard = dval[s]
                dval[s] += 16
                op.dval = dval[s]
        for e in self.ENG:
            for op in per_eng[e]:
                if (not op.dma) and op.signal:
                    cnt[e] += 1
                    op.semval = cnt[e]
        self.stats = dict(cnt)

        def run(eng_name, e):
            seen = {}

            def wait(key, sem, val):
                if val <= 0 or seen.get(key, 0) >= val:
                    return
                seen[key] = val
                e.wait_ge(sem, val)

            for op in per_eng[eng_name]:
                for d in op.deps:
                    if d.dma:
                        wait(("d", d.dsem), dsems[d.dsem], d.dval)
                    else:
                        wait(("e", d.eng), esems[d.eng], d.semval)
                if op.dma:
                    wait(("d", op.dsem), dsems[op.dsem], op.dguard)
                if op.fn is None:
                    continue
                ins = op.fn(e)
                if op.dma:
                    ins.then_inc(dsems[op.dsem], 16)
                elif op.signal:
                    ins.then_inc(esems[op.eng], 1)

        @block.sync
        def _(e):
            run("sp", e)

        @block.gpsimd
        def _(e):
            run("pool", e)

        @block.tensor
        def _(e):
            run("pe", e)

        @block.scalar
        def _(e):
            run("act", e)

        @block.vector
        def _(e):
            run("dve", e)


class Tile:
    def __init__(self, t, name):
        self.t = t
        self.name = name

    def k(self, *idx):
        return (self.name,) + tuple(idx)


def _chunk_cols(w, cols):
    out = np.zeros((KC, 128, 128), np.float32)
    out[:, :, : len(cols)] = w[:, cols].reshape(KC, 128, len(cols))
    return np.ascontiguousarray(out.transpose(1, 0, 2))


def _prep_weights(inp):
    w_in = np.asarray(inp["w_in"], np.float32)
    w_out = np.asarray(inp["w_out"], np.float32)
    ws = np.zeros((DEPTH, NWCH, 128, KC, 128), np.float32)
    ar = np.arange
    for l in range(DEPTH):
        ch = []
        for cb in range(2):
            ch.append(_chunk_cols(w_in[l], 1536 + 128 * cb + ar(128)))
        for cb in range(2):
            ch.append(_chunk_cols(w_in[l], 1280 + 128 * cb + ar(128)))
        for c in range(4):
            ch.append(_chunk_cols(w_in[l], 128 * c + ar(128)))
        for g in range(2):
            kc_ = 512 + 64 * g + ar(64)
            ch.append(_chunk_cols(w_in[l], np.concatenate([kc_, kc_])))
        ch.append(_chunk_cols(w_in[l], 640 + ar(128)))

        def outp2(row0):
            res = []
            for j in range(2):
                o = np.zeros((128, KC, 128), np.float32)
                for kc in range(2):
                    for mm in range(4):
                        m = 4 * j + mm
                        o[:, kc * 4 + mm, :] = w_out[l][row0 + kc * 128:row0 + (kc + 1) * 128, m * 128:(m + 1) * 128]
                res.append(o)
            return res

        for c in range(4):
            ch.append(_chunk_cols(w_in[l], 768 + 128 * c + ar(128)))
        for j in range(4):
            o = np.zeros((128, KC, 128), np.float32)
            for kc in range(4):
                for mm in range(2):
                    m = 2 * j + mm
                    o[:, kc * 2 + mm, :] = w_out[l][kc * 128:(kc + 1) * 128, m * 128:(m + 1) * 128]
            ch.append(o)
        ch += outp2(512)
        for hp in range(2):
            ch.append(_chunk_cols(w_in[l], 2624 + 128 * hp + ar(128)))
        ch.append(_chunk_cols(w_in[l], 2560 + ar(64)))
        for hp in range(2):
            ch.append(_chunk_cols(w_in[l], 2304 + 128 * hp + ar(128)))
        for hp in range(2):
            ch.append(_chunk_cols(w_in[l], 1792 + 128 * hp + ar(128)))
            ch.append(_chunk_cols(w_in[l], 2048 + 128 * hp + ar(128)))
        ch += outp2(768)
        assert len(ch) == NWCH
        ws[l] = np.stack(ch)
    return ws


def _prep_small(inp):
    f = lambda k: np.asarray(inp[k], np.float32)
    pv = np.zeros((DEPTH, 128, NPV), np.float32)
    sw = np.zeros((DEPTH, 128, NSW), np.float32)
    p = np.arange(128)
    for l in range(DEPTH):
        for kc in range(KC):
            pv[l, :, PV_LNG + kc] = f("ln_g")[l, kc * 128 + p]
            pv[l, :, PV_LNB + kc] = f("ln_b")[l, kc * 128 + p]
        for c in range(4):
            pv[l, :, PV_SINK + c] = f("attn_sinks")[l, 2 * c + p // 64]
        for j in range(4):
            for cb in range(2):
                pv[l, :, PV_CONVW + j * 2 + cb] = f("conv_w")[l, j, cb * 128 + p]
        for cb in range(2):
            pv[l, :, PV_CONVB + cb] = f("conv_b")[l, cb * 128 + p]
            pv[l, :, PV_BA + cb] = f("lru_ba")[l, cb * 128 + p]
            pv[l, :, PV_BX + cb] = f("lru_bx")[l, cb * 128 + p]
            pv[l, :, PV_LAM + cb] = f("lru_lambda")[l, cb * 128 + p]
        mu = f("rwkv_mu")[l]
        for q in range(3):
            for hp in range(2):
                pv[l, :, PV_MU + q * 2 + hp] = mu[q * 256 + hp * 128 + p]
        pv[l, :64, PV_MUCW] = mu[768:832]
        for hp in range(2):
            s = hp * 128 + p
            pv[l, :, PV_W0 + hp] = f("rwkv_w0")[l, s]
            pv[l, :, PV_A0 + hp] = f("rwkv_a0")[l, s]
            pv[l, :, PV_KK + hp] = f("rwkv_kk")[l, s]
            pv[l, :, PV_KA + hp] = f("rwkv_ka")[l, s]
            pv[l, :, PV_RK + hp] = f("rwkv_rk")[l].reshape(256)[s]
            pv[l, :, PV_GNW + hp] = f("rwkv_gn_w")[l, s]
            pv[l, :, PV_GNB + hp] = f("rwkv_gn_b")[l, s]
            if l > 0:
                pv[l, :, PV_V0 + hp] = f("rwkv_v0")[l - 1, s]
        wa, wx = f("lru_wa")[l], f("lru_wx")[l]
        for ax, wmat in enumerate((wa, wx)):
            for cb in range(2):
                for i in range(2):
                    c0 = SW_LRU + (ax * 2 + cb) * 128 + i * 64
                    sw[l, i * 64:(i + 1) * 64, c0:c0 + 64] = wmat[2 * cb + i]
        sw[l, 0:32, SW_WA2:SW_WA2 + 256] = f("rwkv_w2")[l]
        sw[l, 32:64, SW_WA2:SW_WA2 + 256] = f("rwkv_a2")[l]
        if l > 0:
            v1 = f("rwkv_v1")[l - 1]
            for hp in range(2):
                sw[l, :, SW_V1 + hp * 16:SW_V1 + hp * 16 + 16] = v1[hp * 128:(hp + 1) * 128, :]
            sw[l, 0:16, SW_V2:SW_V2 + 256] = f("rwkv_v2")[l - 1]
    return pv, sw


def _consts():
    cb = np.zeros((128, NCB), np.float32)
    p = np.arange(128)[:, None]
    q = np.arange(128)[None, :]
    cb[:, CB_ID:CB_ID + 128] = (p == q)
    cb[:, CB_ID2:CB_ID2 + 64] = ((p % 64) == np.arange(64)[None, :])
    mp = np.where(q < p, 0.0, NEG)
    mc = np.where(q >= p, 0.0, NEG)
    cb[:, CB_MP:CB_MP + 512] = np.tile(mp, (1, 4))
    cb[:, CB_MC:CB_MC + 512] = np.tile(mc, (1, 4))
    s = (np.arange(128) % 64)[:, None]
    t = np.arange(64)[None, :]
    cb[:, CB_RM:CB_RM + 64] = (t > s)
    cb[:, CB_RM + 64:CB_RM + 128] = (t >= s)
    cb[:, CB_ML:CB_ML + 64] = (t < s)
    cb[:, CB_BO:CB_BO + 128] = ((p // 64) == (q // 64))
    cb[:, CB_CM:CB_CM + 512] = ((np.arange(512) % 64) != 0)[None, :]
    cb[:, CB_ON:CB_ON + 64] = 1.0
    cf = np.zeros((128, 256), np.float32)
    cf[:, 0:128] = ((p // 64) == (q // 64))
    cf[:, 128:256] = 1.0
    return cb, cf


E_START, E_CHAIN = 6, 2
WARM_P4, WARM_CH = 0, 0
WARM_BURST = 0


def build(layers, stage=9):
    nc = bass.Bass("TRN2", target_bir_lowering=False)
    NLY = len(layers)
    x_in = nc.dram_tensor("xT", [D, T], F32, kind="ExternalInput").ap()
    wst = nc.dram_tensor("wst", [NLY, NWCH, 128, KC, 128], F32, kind="ExternalInput").ap()
    pv_d = nc.dram_tensor("pv", [NLY, 128, NPV], F32, kind="ExternalInput").ap()
    sw_d = nc.dram_tensor("sw", [NLY, 128, NSW], F32, kind="ExternalInput").ap()
    cb_d = nc.dram_tensor("cb", [128, NCB], F32, kind="ExternalInput").ap()
    cf_d = nc.dram_tensor("cf", [128, 256], F32, kind="ExternalInput").ap()
    y_out = nc.dram_tensor("yT", [D, T], F32, kind="ExternalOutput").ap()

    P = Prog()
    es = ExitStack()

    lyr = {"i": 0}

    def sb(name, shape, dt, stack=None):
        if stack is None:
            stack = es
        else:
            name = "%s_L%d" % (name, lyr["i"])
        return Tile(stack.enter_context(nc.sbuf_tensor(name, shape, dt)), name)

    X32 = sb("X32", [128, KC, T], F32)
    XBF = sb("XBF", [128, KC, T], BF16)
    Y = sb("Y", [128, 4, T], BF16)
    VF = sb("VF", [128, 2, T], BF16)
    WR = sb("WR", [128, NSLOT, KC, 128], BF16)
    PVT = sb("PVT", [128, NLY, NPV], F32)
    DVT = sb("DVT", [128, NDV], F32)
    SMW = sb("SMW", [128, NSW], BF16)
    CBT = sb("CBT", [128, NCB], BF16)
    CFT = sb("CFT", [128, 256], F32)
    PS = Tile(es.enter_context(nc.psum_tensor("PS", [128, 8, 512], F32)), "PS")

    ident = CBT.t[:, CB_ID:CB_ID + 128]
    bones_bf = CBT.t[:, CB_BO:CB_BO + 128]
    ones_bf64 = CBT.t[:, CB_ON:CB_ON + 64]
    bones32 = CFT.t[:, 0:128]
    ones32 = CFT.t[:, 128:256]
    KCB = [CBT.k()]
    KCF = [CFT.k()]
    KPV = [PVT.k()]
    KDV = [DVT.k()]
    KSW = [SMW.k()]

    def fs(ap):
        n = 1
        for d_ in ap.shape[1:]:
            n *= int(d_)
        return n

    def mm(out, lhsT, rhs, start, stop, r, w, tp=None):
        if tp is None:
            fn = lambda e: e.matmul(out, lhsT=lhsT, rhs=rhs, start=start, stop=stop)
        else:
            fn = lambda e: e.matmul(out, lhsT=lhsT, rhs=rhs, start=start, stop=stop, tile_position=tp)
        m_, n_ = fs(lhsT), fs(rhs)
        if lhsT.dtype == F32:
            c = 0.06 + n_ / 700.0
        elif m_ <= 64 and n_ <= 128:
            c = 0.03 + n_ / 6000.0
        elif n_ >= 512:
            c = 0.225
        else:
            c = 0.04 + n_ / 1200.0
        P.add("pe", fn, r, w, cost=c)

    def act(out, in_, func, r, w, scale=None, bias=None):
        kw = {}
        if scale is not None:
            kw["scale"] = scale
        if bias is not None:
            kw["bias"] = bias
        P.add("act", lambda e: e.activation(out=out, in_=in_, func=func, **kw), r, w, cost=0.22 + fs(out) / 1100.0)

    def ecost(eng, out):
        if eng == "pool":
            return 0.3 + fs(out) / 450.0
        return 0.12 + fs(out) / 900.0

    def tt(out, in0, in1, op, r, w, eng="dve"):
        P.add(eng, lambda e: e.tensor_tensor(out=out, in0=in0, in1=in1, op=op), r, w, cost=ecost(eng, out))

    def ts(out, in0, s1, s2, op0, op1, r, w, eng="dve"):
        P.add(eng, lambda e: e.tensor_scalar(out=out, in0=in0, scalar1=s1, scalar2=s2, op0=op0, op1=op1), r, w,
              cost=ecost(eng, out))

    def stt(out, in0, sc, in1, op0, op1, r, w):
        P.add("dve", lambda e: e.scalar_tensor_tensor(out=out, in0=in0, scalar=sc, in1=in1, op0=op0, op1=op1), r, w,
              cost=ecost("dve", out))

    def cp(out, in_, r, w, eng="dve"):
        if eng == "act":
            P.add("act", lambda e: e.activation(out=out, in_=in_, func=AF.Copy), r, w, cost=0.22 + fs(out) / 1100.0)
        else:
            P.add(eng, lambda e: e.tensor_copy(out=out, in_=in_), r, w, cost=ecost(eng, out))

    def scan(out, d0, d1, init, r, w):
        P.add("dve", lambda e: e.tensor_tensor_scan(out=out, data0=d0, data1=d1, initial=init,
                                                    op0=ALU.mult, op1=ALU.add), r, w, cost=0.1 + fs(out) / 450.0)

    def recip(out, in_, r, w):
        P.add("dve", lambda e: e.reciprocal(out=out, in_=in_), r, w, cost=0.1 + fs(out) / 120.0)

    def memset(ap, val, w, eng="dve"):
        P.add(eng, lambda e: e.memset(ap, val), (), w, cost=ecost(eng, ap))

    def dma(out, in_, r, w, q="sp"):
        P.add(q, lambda e: e.dma_start(out=out, in_=in_), r, w, dma=True, cost=2.5)

    one_ap = CFT.t[:, 128:129]

    def sigm(out, in_, negbias, r, w):
        act(out, in_, AF.Exp, r, w, scale=-1.0, bias=negbias)
        act(out, out, AF.Ln, list(w) + KCF, w, bias=one_ap)
        act(out, out, AF.Exp, w, w, scale=-1.0)

    def rpow(out, in_, p, r, w, bias=None):
        act(out, in_, AF.Ln, r, w, bias=bias)
        act(out, out, AF.Exp, w, w, scale=p)

    def psk(*banks):
        return [PS.k(b) for b in banks]

    def r3(ap, inner):
        return ap.rearrange("p (a b) -> p a b", b=inner)

    wstate = {"loaded": 0}
    total_chunks = NLY * NWCH

    def wload_upto(n):
        while wstate["loaded"] < min(n, total_chunks):
            g = wstate["loaded"]
            li_, i = divmod(g, NWCH)
            s = g % NSLOT
            dma(WR.t[:, s, :, :], wst[li_, i, :, :, :], (), [WR.k(s)], q="pool")
            wstate["loaded"] += 1

    def wdone(g):
        wload_upto(g + 1 + NSLOT)

    def wslot(g):
        s = g % NSLOT
        return WR.t[:, s, :, :], WR.k(s)

    dma(CBT.t[:, :], cb_d[:, :], (), KCB, q="pool")
    dma(CFT.t[:, :], cf_d[:, :], (), KCF)
    dma(PVT.t[:, :, :], pv_d.rearrange("l p c -> p l c"), (), KPV)
    for tb in range(NTB):
        for kc in range(KC):
            dma(X32.t[:, kc, tb * TB:(tb + 1) * TB], x_in[kc * 128:(kc + 1) * 128, tb * TB:(tb + 1) * TB],
                (), [X32.k(kc, tb)])
    wload_upto(NSLOT)
    for tb in range(NTB):
        for kc in range(KC):
            sl = slice(tb * TB, (tb + 1) * TB)
            cp(XBF.t[:, kc, sl], X32.t[:, kc, sl], [X32.k(kc, tb)], [XBF.k(kc, tb)],
               eng="act" if (kc + tb) % 2 == 0 else "dve")

    pbank = {"i": 0}

    def next_bank(cands=(0, 1)):
        b = cands[pbank["i"] % len(cands)]
        pbank["i"] += 1
        return b

    def proj(g, tb, bank):
        wap, wk = wslot(g)
        for kc in range(KC):
            mm(PS.t[:, bank, :], wap[:, kc, :], XBF.t[:, kc, tb * TB:(tb + 1) * TB],
               kc == 0, kc == KC - 1, [wk, XBF.k(kc, tb)], psk(bank))

    for li, lay in enumerate(layers):
        gbase = li * NWCH
        lyr["i"] = li

        def pv(col, n=1, rows=slice(0, 128)):
            return PVT.t[rows, li, col:col + n]

        def dv(col, n=1, rows=slice(0, 128)):
            return DVT.t[rows, col:col + n]

        dma(SMW.t[:, :], sw_d[li, :, :], (), KSW, q="pool")
        ts(dv(DV_OMM, 7), pv(PV_MU, 7), -1.0, 1.0, ALU.mult, ALU.add, KPV, KDV)
        act(dv(DV_SE, 4), pv(PV_SINK, 4), AF.Exp, KPV, KDV)
        act(dv(DV_LC, 2), pv(PV_LAM, 2), AF.Exp, KPV, KDV, scale=-1.0)
        ts(dv(DV_LC, 2), dv(DV_LC, 2), 1.0, None, ALU.add, ALU.bypass, KDV, KDV)
        act(dv(DV_LC, 2), dv(DV_LC, 2), AF.Ln, KDV, KDV)
        ts(dv(DV_LC2, 2), dv(DV_LC, 2), -16.0, None, ALU.mult, ALU.bypass, KDV, KDV)
        ts(dv(DV_LC, 2), dv(DV_LC, 2), -8.0, None, ALU.mult, ALU.bypass, KDV, KDV)
        ts(dv(DV_OKA, 2), pv(PV_KA, 2), -1.0, 1.0, ALU.mult, ALU.add, KPV, KDV)
        for dcol, pcol in ((DV_NBA, PV_BA), (DV_NBX, PV_BX), (DV_NW0, PV_W0), (DV_NA0, PV_A0), (DV_NV0, PV_V0)):
            ts(dv(dcol, 2), pv(pcol, 2), -1.0, None, ALU.mult, ALU.bypass, KPV, KDV)
        memset(dv(DV_EPSLN), LN_EPS, KDV)
        memset(dv(DV_EPSGN), GN_EPS, KDV)
        memset(dv(DV_EPSKK), 1e-12, KDV)

        P.set_fence()
        ybs = ExitStack()
        YB = sb("YB", [128, 2, T], BF16, ybs)
        with ExitStack() as pha:
          if stage >= 1:
            QT = sb("QT", [128, 4, T], BF16, pha)
            KD = sb("KD", [128, 2, T], BF16, pha)
            VT = sb("VT", [128, 16, 128], BF16, pha)
            with ExitStack() as ph:
                XBs = [sb("XB", [128, T + 4], F32, ph)] * 2
                XCs = [sb("XC", [128, T], F32, ph)] * 2
                XCBs = [sb("XCB", [128, T], BF16, ph)] * 2
                BRs = [sb("BR", [128, TB], F32, ph)] * 2
                BIs = [sb("BI", [128, TB], F32, ph)] * 2
                BMs = [sb("BM", [128, TB], F32, ph)] * 2
                BHs = [[BMs[0]] * 2] * 2
                BHC = Tile(DVT.t, "BHC_col")
                memset(XBs[0].t[:, 0:4], 0.0, [XBs[0].k("pad")])
                for cb in range(2):
                    g = gbase + 0 + cb
                    for tb in range(NTB):
                        b = next_bank()
                        proj(g, tb, b)
                        act(YB.t[:, cb, tb * TB:(tb + 1) * TB], PS.t[:, b, :], AF.Silu, psk(b), [YB.k(cb, tb)])
                    wdone(g)
                def gen_B(cb):
                    XB, XC, XCB, BR, BI, BM, BH = XBs[cb], XCs[cb], XCBs[cb], BRs[cb], BIs[cb], BMs[cb], BHs[cb]
                    gb = 2 + 2 * cb
                    for tb in range(NTB):
                        rk = [XB.k(tb)] + ([XB.k(tb - 1)] if tb > 0 else [XB.k("pad")])
                        o = XC.t[:, tb * TB:(tb + 1) * TB]
                        ts(o, XB.t[:, 4 + tb * TB:4 + (tb + 1) * TB], pv(PV_CONVW + 3 * 2 + cb), pv(PV_CONVB + cb),
                           ALU.mult, ALU.add, rk + KPV, [XC.k(tb)])
                        for j in range(3):
                            s0 = 4 + tb * TB - 3 + j
                            stt(o, XB.t[:, s0:s0 + TB], pv(PV_CONVW + j * 2 + cb), o, ALU.mult, ALU.add,
                                rk + KPV + [XC.k(tb)], [XC.k(tb)])
                        cp(XCB.t[:, tb * TB:(tb + 1) * TB], o, [XC.k(tb)], [XCB.k(tb)], eng="pool")
                        yield
                    for tb in range(NTB):
                        sl = slice(tb * TB, (tb + 1) * TB)
                        ca = SW_LRU + (0 * 2 + cb) * 128
                        cx = SW_LRU + (1 * 2 + cb) * 128
                        mm(PS.t[:, gb, :], SMW.t[:, ca:ca + 128], XCB.t[:, sl], True, True, KSW + [XCB.k(tb)], psk(gb))
                        mm(PS.t[:, gb + 1, :], SMW.t[:, cx:cx + 128], XCB.t[:, sl], True, True, KSW + [XCB.k(tb)],
                           psk(gb + 1))
                        sigm(BR.t[:, :], PS.t[:, gb, :], dv(DV_NBA + cb), psk(gb) + KDV, [BR.k()])
                        yield
                        sigm(BI.t[:, :], PS.t[:, gb + 1, :], dv(DV_NBX + cb), psk(gb + 1) + KDV, [BI.k()])
                        yield
                        act(BM.t[:, :], BR.t[:, :], AF.Exp, [BR.k()] + KDV, [BM.k()], scale=dv(DV_LC2 + cb))
                        act(BR.t[:, :], BR.t[:, :], AF.Exp, [BR.k()] + KDV, [BR.k()], scale=dv(DV_LC + cb))
                        yield
                        act(BM.t[:, :], BM.t[:, :], AF.Ln, [BM.k()] + KCF, [BM.k()], scale=-1.0, bias=one_ap)
                        act(BM.t[:, :], BM.t[:, :], AF.Exp, [BM.k()], [BM.k()], scale=0.5)
                        tt(BI.t[:, :], BI.t[:, :], XC.t[:, sl], ALU.mult, [BI.k(), XC.k(tb)], [BI.k()])
                        yield
                        tt(BI.t[:, :], BI.t[:, :], BM.t[:, :], ALU.mult, [BI.k(), BM.k()], [BI.k()])
                        hcur = BH[0]
                        init = 0.0 if tb == 0 else BHC.t[:, 30:31]
                        scan(hcur.t[:, :], BR.t[:, :], BI.t[:, :], init,
                             [BR.k(), BI.k()] + ([BHC.k()] if tb > 0 else []), [hcur.k()])
                        if tb + 1 < NTB:
                            cp(BHC.t[:, 30:31], hcur.t[:, TB - 1:TB], [hcur.k()], [BHC.k()])
                        yv = YB.t[:, cb, sl]
                        tt(yv, yv, hcur.t[:, :], ALU.mult, [hcur.k(), YB.k(cb, tb)], [YB.k(cb, tb)])
                        yield

                for cb in range(2):
                    g = gbase + 2 + cb
                    XB = XBs[cb]
                    for tb in range(NTB):
                        b = next_bank()
                        proj(g, tb, b)
                        cp(XB.t[:, 4 + tb * TB:4 + (tb + 1) * TB], PS.t[:, b, :], psk(b), [XB.k(tb)],
                           eng="dve" if tb % 2 == 0 else "act")
                    wdone(g)
                    for _ in gen_B(cb):
                        pass

                for c in range(4):
                    g = gbase + 4 + c
                    for tb in range(NTB):
                        b = next_bank()
                        proj(g, tb, b)
                        cp(QT.t[:, c, tb * TB:(tb + 1) * TB], PS.t[:, b, :], psk(b), [QT.k(c, tb)],
                           eng="act" if tb % 2 == 0 else "dve")
                    wdone(g)
                for gi in range(2):
                    g = gbase + 8 + gi
                    for tb in range(NTB):
                        b = next_bank()
                        proj(g, tb, b)
                        cp(KD.t[:, gi, tb * TB:(tb + 1) * TB], PS.t[:, b, :], psk(b), [KD.k(gi, tb)],
                           eng="act" if tb % 2 == 0 else "dve")
                    wdone(g)
                g = gbase + 10
                wap, wk = wslot(g)
                for tb in range(NTB):
                    b = next_bank()
                    for nn in range(4):
                        n = tb * 4 + nn
                        for kc in range(KC):
                            mm(PS.t[:, b, nn * 128:(nn + 1) * 128], XBF.t[:, kc, n * 128:(n + 1) * 128], wap[:, kc, :],
                               kc == 0, kc == KC - 1, [wk, XBF.k(kc, tb)], psk(b))
                    cp(VT.t[:, tb * 4:(tb + 1) * 4, :], r3(PS.t[:, b, :], 128), psk(b), [VT.k(tb)], eng="act")
                wdone(g)
            P.set_fence()
            with ExitStack() as ph:
                PT = sb("PT", [128, 2, 2, 1024], BF16, ph)
                A1 = sb("A1", [128, 512], F32, ph)
                A2 = sb("A2", [128, 512], F32, ph)
                A3 = sb("A3", [128, 512], F32, ph)
                for c in range(4):
                    g = gbase + 11 + c
                    for tb in range(NTB):
                        b = next_bank()
                        proj(g, tb, b)
                        act(Y.t[:, c, tb * TB:(tb + 1) * TB], PS.t[:, b, :], AF.Silu, psk(b), [Y.k(c, tb)])
                    wdone(g)
                def attn_scores(n):
                    tbq = n // 4
                    jlist = [(0, n - 1), (1, n)] if n > 0 else [(1, n)]
                    for jj, kb in jlist:
                        tbk = kb // 4
                        mcol = CB_MP if jj == 0 else CB_MC
                        for par in range(2):
                            bank = 2 + jj * 2 + par
                            mm(PS.t[:, bank, :], ident, CBT.t[:, mcol:mcol + 512], True, False, KCB, psk(bank))
                        for gq in range(2):
                            for cg in range(2):
                                for par in range(2):
                                    bank = 2 + jj * 2 + par
                                    c = gq * 2 + cg
                                    pos = gq * 2 + cg
                                    mm(PS.t[:, bank, pos * 128:(pos + 1) * 128],
                                       KD.t[par * 64:(par + 1) * 64, gq, kb * 128:(kb + 1) * 128],
                                       QT.t[par * 64:(par + 1) * 64, c, n * 128:(n + 1) * 128],
                                       False, (gq == 1 and cg == 1), [KD.k(gq, tbk), QT.k(c, tbq)], psk(bank))
                        b0 = 2 + jj * 2
                        act(PT.t[:, n % 2, jj, :], PS.t[:, b0:b0 + 2, :].rearrange("p a b -> p (a b)"), AF.Exp,
                            psk(b0, b0 + 1), [PT.k(n % 2, jj)], scale=0.125)

                def attn_pv(n):
                    tbq = n // 4
                    jlist = [(0, n - 1), (1, n)] if n > 0 else [(1, n)]
                    for (obank, use_ones) in ((6, False), (7, True)):
                        for gq in range(2):
                            for par in range(2):
                                for ji, (jj, kb) in enumerate(jlist):
                                    if use_ones:
                                        lhsT, rk = ones_bf64, KCB
                                    else:
                                        lhsT, rk = VT.t[:, kb, gq * 64:(gq + 1) * 64], [VT.k(kb // 4)]
                                    c0 = par * 512 + gq * 256
                                    mm(PS.t[par * 64:(par + 1) * 64, obank, gq * 256:(gq + 1) * 256],
                                       lhsT, PT.t[:, n % 2, jj, c0:c0 + 256],
                                       ji == 0, ji == len(jlist) - 1, rk + [PT.k(n % 2, jj)], psk(obank),
                                       tp=(0, 64 * par))
                    tt(r3(A1.t[:, :], 128), r3(PS.t[:, 7, :], 128),
                       dv(DV_SE, 4).unsqueeze(2).to_broadcast([128, 4, 128]), ALU.add, psk(7) + KDV, [A1.k()])
                    rpow(A2.t[:, :], A1.t[:, :], -1.0, [A1.k()], [A2.k()])
                    tt(A3.t[:, :], PS.t[:, 6, :], A2.t[:, :], ALU.mult, psk(6) + [A2.k()], [A3.k()])
                    yv = Y.t[:, 0:4, n * 128:(n + 1) * 128]
                    yk = [Y.k(c, tbq) for c in range(4)]
                    tt(yv, yv, r3(A3.t[:, :], 128), ALU.mult, [A3.k()] + yk, yk)

                attn_scores(0)
                for n in range(16):
                    if n + 1 < 16:
                        attn_scores(n + 1)
                    attn_pv(n)
        for j in (range(4) if stage >= 2 else ()):
            g = gbase + 15 + j
            wap, wk = wslot(g)
            wapB, wkB = wslot(gbase + 19 + j // 2)
            for mm_ in range(2):
                m = 2 * j + mm_
                for tb in range(NTB):
                    b = next_bank()
                    for kc in range(2):
                        mm(PS.t[:, b, :], wapB[:, kc * 4 + (m % 4), :], YB.t[:, kc, tb * TB:(tb + 1) * TB],
                           kc == 0, False, [wkB, YB.k(kc, tb)], psk(b))
                    for kc in range(4):
                        mm(PS.t[:, b, :], wap[:, kc * 2 + mm_, :], Y.t[:, kc, tb * TB:(tb + 1) * TB],
                           False, kc == 3, [wk, Y.k(kc, tb)], psk(b))
                    xs = X32.t[:, m, tb * TB:(tb + 1) * TB]
                    stt(xs, xs, ALPHA, PS.t[:, b, :], ALU.mult, ALU.add, psk(b) + [X32.k(m, tb)], [X32.k(m, tb)])
            wdone(g)
        if stage >= 2:
            wdone(gbase + 19)
            wdone(gbase + 20)
        ybs.close()

        P.set_fence()
        with ExitStack() as ph:
          if stage >= 4:
            def f32t(name):
                return sb(name, [128, SB], F32, ph)

            def bft(name, shape):
                return sb(name, shape, BF16, ph)

            two = range(2)
            ET = [{n: f32t("c%s%d" % (n, h)) for n in ("TMP", "RS", "KS", "SG", "GG", "AA", "EG", "ENG", "EGM")}
                  for h in two]
            PTMP, PT1 = f32t("cPTMP"), f32t("cPT1")
            GNT = [{"YS": ET[h]["SG"], "YQ": ET[h]["GG"], "MEAN": ET[h]["AA"], "VAR": ET[h]["TMP"]} for h in two]
            VS = [f32t("cVS0"), f32t("cVS1")]
            XSW = sb("cXSW", [64, SB], F32, ph)
            TWAL = bft("cTWAL", [64, SB])
            VBp = [bft("cVB%d" % i, [128, 2, SB]) for i in two]
            VSB = bft("cVSB", [128, 2, SB])
            P1B = bft("cP1B", [16, SB])
            SQBh = [bft("cSQB%d" % h, [128, SB]) for h in two]
            RKBh = [bft("cRKB%d" % h, [128, SB]) for h in two]
            CAR = sb("cCAR", [128, 8], F32, ph)
            BONp = [[f32t("cBON%d%d" % (i, h)) for h in two] for i in two]
            ARp = [[bft("cAR%d%d" % (i, h), [128, NCH, 128]) for h in two] for i in two]
            BT = [bft("cBT%d" % h, [128, SB]) for h in two]
            KT = [bft("cKT%d" % h, [128, SB]) for h in two]
            BHt = [bft("cBHt%d" % h, [128, SB]) for h in two]
            KHt = [bft("cKHt%d" % h, [128, SB]) for h in two]
            NTMR = [bft("cNTMR%d" % h, [128, NCH, 128]) for h in two]
            MKT = [bft("cMKT%d" % h, [128, NCH, 128]) for h in two]
            NK = [[bft("cNK%d%d" % (h, i), [128, NCH, 64]) for i in two] for h in two]
            NTQ = [[bft("cNTQ%d%d" % (h, i), [128, NCH, 128]) for i in two] for h in two]
            QF = [bft("cQF%d" % h, [128, NCH, 64]) for h in two]
            TOK = [bft("cTOK%d" % h, [128, NCH, 4, 64]) for h in two]
            MVB = [bft("cMVB%d" % h, [128, NCH, 64]) for h in two]
            WT = [bft("cWT%d" % h, [128, NCH, 64]) for h in two]
            UT = [sb("cUT%d" % h, [128, NCH, 64], F32, ph) for h in two]
            UB = [bft("cUB%d" % h, [128, 64]) for h in two]
            HB = [[bft("cHB%d%d" % (h, i), [128, 64]) for i in two] for h in two]
            H32 = [sb("cH32%d" % h, [128, 64], F32, ph) for h in two]
            GCp = [[sb("cGC%d%d" % (i, h), [128, NCH], F32, ph) for h in two] for i in two]
            hpar = [0, 0]

            for hp in range(2):
                g = gbase + 21 + hp
                for tb in range(NTB):
                    b = next_bank()
                    proj(g, tb, b)
                    act(Y.t[:, 2 + hp, tb * TB:(tb + 1) * TB], PS.t[:, b, :], AF.Silu, psk(b), [Y.k(2 + hp, tb)])
                wdone(g)
            g_cw = gbase + 23
            g_v = [gbase + 24, gbase + 25]
            g_r = [gbase + 26, gbase + 28]
            g_k = [gbase + 27, gbase + 29]
            for hp in range(2):
                memset(H32[hp].t[:, :], 0.0, [H32[hp].k()])
                memset(HB[hp][0].t[:, :], 0.0, [HB[hp][0].k()])
            memset(CAR.t[:, :], 0.0, [CAR.k(c) for c in range(8)])

            def projC(g, sbi, bank, col0, rows=128):
                wap_, wk_ = wslot(g)
                tbx_ = (sbi * SB) // TB
                for kc in range(KC):
                    mm(PS.t[0:rows, bank, col0:col0 + SB], wap_[:, kc, 0:rows], XBF.t[:, kc, sbi * SB:(sbi + 1) * SB],
                       kc == 0, kc == KC - 1, [wk_, XBF.k(kc, tbx_)], psk(bank))

            def tshift(bank, col0, rows, mucol, ommcol, carcol, out, TMP):
                rs = slice(0, rows)
                src = PS.t[rs, bank, col0:col0 + SB]
                act(TMP.t[rs, :], src, AF.Identity, psk(bank) + KDV, [TMP.k()], scale=dv(ommcol, 1, rs))
                stt(out.t[rs, 1:SB], PS.t[rs, bank, col0:col0 + SB - 1], pv(mucol, 1, rs), TMP.t[rs, 1:SB],
                    ALU.mult, ALU.add, psk(bank) + KPV + [TMP.k()], [out.k()])
                stt(out.t[rs, 0:1], CAR.t[rs, carcol:carcol + 1], pv(mucol, 1, rs), TMP.t[rs, 0:1],
                    ALU.mult, ALU.add, [CAR.k(carcol)] + KPV + [TMP.k()], [out.k()])
                cp(CAR.t[rs, carcol:carcol + 1], PS.t[rs, bank, col0 + SB - 1:col0 + SB], psk(bank), [CAR.k(carcol)])

            rmaskb = CBT.t[:, CB_RM:CB_RM + 128].unsqueeze(1).to_broadcast([128, NCH, 128])
            masklb = CBT.t[:, CB_ML:CB_ML + 64].unsqueeze(1).to_broadcast([128, NCH, 64])
            ident2 = CBT.t[:, CB_ID2:CB_ID2 + 64]
            ident2b = ident2.unsqueeze(1).to_broadcast([128, NCH, 64])
            HS = [slice(0, 64), slice(64, 128)]

            def jh():
                for j in range(NCH):
                    for h in range(2):
                        yield j, h, HS[h], slice(j * CH, (j + 1) * CH)

            def prologue(sbi):
                tbx = (sbi * SB) // TB
                ssl = slice(sbi * SB, (sbi + 1) * SB)
                VB = VBp[sbi % 2]
                TMP, T1 = PTMP, PT1
                projC(g_cw, sbi, 6, 0, rows=64)
                tshift(6, 0, 64, PV_MUCW, DV_OMM + 6, 6, XSW, TMP)
                yield
                act(XSW.t[0:32, :], XSW.t[0:32, :], AF.Exp, [XSW.k()], [XSW.k()], scale=-2.0)
                act(XSW.t[0:32, :], XSW.t[0:32, :], AF.Ln, [XSW.k()] + KCF, [XSW.k()], bias=one_ap[0:32, :])
                act(XSW.t[0:32, :], XSW.t[0:32, :], AF.Exp, [XSW.k()], [XSW.k()], scale=-1.0)
                ts(TWAL.t[0:32, :], XSW.t[0:32, :], 2.0, -1.0, ALU.mult, ALU.add, [XSW.k()], [TWAL.k()])
                cp(TWAL.t[32:64, :], XSW.t[32:64, :], [XSW.k()], [TWAL.k()], eng="pool")
                yield
                for hp in range(2):
                    projC(g_v[hp], sbi, 7, hp * SB)
                    tshift(7, hp * SB, 128, PV_MU + 4 + hp, DV_OMM + 4 + hp, 4 + hp, VS[hp], TMP)
                    yield
                if lay == 0:
                    for hp in range(2):
                        cp(VB.t[:, hp, :], VS[hp].t[:, :], [VS[hp].k()], [VB.k(hp)], eng="pool")
                        cp(VF.t[:, hp, ssl], VS[hp].t[:, :], [VS[hp].k()], [VF.k(hp, tbx)], eng="pool")
                else:
                    for hp in range(2):
                        cp(VSB.t[:, hp, :], VS[hp].t[:, :], [VS[hp].k()], [VSB.k(hp)], eng="pool")
                    for hp in range(2):
                        mm(PS.t[0:16, 6, 256:512], SMW.t[:, SW_V1 + hp * 16:SW_V1 + hp * 16 + 16], VSB.t[:, hp, :],
                           hp == 0, hp == 1, KSW + [VSB.k(hp)], psk(6))
                    cp(P1B.t[:, :], PS.t[0:16, 6, 256:512], psk(6), [P1B.k()], eng="act")
                    for hp in range(2):
                        mm(PS.t[:, 6, 0:SB], SMW.t[0:16, SW_V2 + hp * 128:SW_V2 + hp * 128 + 128], P1B.t[0:16, :],
                           True, True, KSW + [P1B.k()], psk(6))
                        sigm(T1.t[:, :], PS.t[:, 6, 0:SB], dv(DV_NV0 + hp), psk(6) + KDV, [T1.k()])
                        tt(TMP.t[:, :], VF.t[:, hp, ssl], VS[hp].t[:, :], ALU.subtract,
                           [VF.k(hp, tbx), VS[hp].k()], [TMP.k()])
                        tt(TMP.t[:, :], TMP.t[:, :], T1.t[:, :], ALU.mult, [TMP.k(), T1.k()], [TMP.k()])
                        tt(VB.t[:, hp, :], VS[hp].t[:, :], TMP.t[:, :], ALU.add, [VS[hp].k(), TMP.k()], [VB.k(hp)])
                        yield

            def stage_E(sbi, hp):
                par = sbi % 2
                VB, AR, GC, BON = VBp[par], ARp[par], GCp[par], BONp[par]
                SQB, RKB = SQBh[hp], RKBh[hp]
                e_ = ET[hp]
                TMP, RS, KS, SG, GG, AA, EG, ENG, EGM = (e_[n] for n in
                                                         ("TMP", "RS", "KS", "SG", "GG", "AA", "EG", "ENG", "EGM"))
                RN, T1, BB, KK = TMP, TMP, SG, GG
                if True:
                    B0 = B1 = B2 = 6 + hp
                    projC(g_r[hp], sbi, B0, 0)
                    projC(g_k[hp], sbi, B0, SB)
                    tshift(B0, 0, 128, PV_MU + 0 + hp, DV_OMM + 0 + hp, 0 + hp, RS, TMP)
                    tshift(B0, SB, 128, PV_MU + 2 + hp, DV_OMM + 2 + hp, 2 + hp, KS, TMP)
                    yield
                    cz = SW_WA2 + hp * 128
                    mm(PS.t[:, B1, 0:SB], SMW.t[0:32, cz:cz + 128], TWAL.t[0:32, :], True, True, KSW + [TWAL.k()], psk(B1))
                    sigm(SG.t[:, :], PS.t[:, B1, 0:SB], dv(DV_NW0 + hp), psk(B1) + KDV, [SG.k()])
                    mm(PS.t[:, B2, 0:SB], SMW.t[32:64, cz:cz + 128], TWAL.t[32:64, :], True, True,
                       KSW + [TWAL.k()], psk(B2))
                    sigm(AA.t[:, :], PS.t[:, B2, 0:SB], dv(DV_NA0 + hp), psk(B2) + KDV, [AA.k()])
                    yield
                    scan(GG.t[:, :], CBT.t[:, CB_CM:CB_CM + SB], SG.t[:, :], 0.0, KCB + [SG.k()], [GG.k()])
                    act(EG.t[:, :], GG.t[:, :], AF.Exp, [GG.k()], [EG.k()], scale=-CW)
                    act(ENG.t[:, :], GG.t[:, :], AF.Exp, [GG.k()], [ENG.k()], scale=CW)
                    yield
                    tt(EGM.t[:, :], GG.t[:, :], SG.t[:, :], ALU.subtract, [GG.k(), SG.k()], [EGM.k()])
                    act(EGM.t[:, :], EGM.t[:, :], AF.Exp, [EGM.k()], [EGM.k()], scale=-CW)
                    cp(GC[hp].t[:, :].unsqueeze(2), r3(EG.t[:, :], CH)[:, :, CH - 1:CH], [EG.k()], [GC[hp].k()])
                    yield
                    ts(KK.t[:, :], KS.t[:, :], pv(PV_KK + hp), None, ALU.mult, ALU.bypass, [KS.k()] + KPV, [KK.k()])
                    tt(SQB.t[:, :], KK.t[:, :], KK.t[:, :], ALU.mult, [KK.k()], [SQB.k()], eng="pool")
                    mm(PS.t[:, B1, SB:2 * SB], bones_bf, SQB.t[:, :], True, True, KCB + [SQB.k()], psk(B1))
                    rpow(RN.t[:, :], PS.t[:, B1, SB:2 * SB], -0.5, psk(B1) + KDV, [RN.k()], bias=dv(DV_EPSKK))
                    yield
                    tt(KK.t[:, :], KK.t[:, :], RN.t[:, :], ALU.mult, [KK.k(), RN.k()], [KK.k()])
                    ts(T1.t[:, :], AA.t[:, :], pv(PV_KA + hp), dv(DV_OKA + hp), ALU.mult, ALU.add,
                       [AA.k()] + KPV + KDV, [T1.k()])
                    tt(KS.t[:, :], KS.t[:, :], T1.t[:, :], ALU.mult, [KS.k(), T1.k()], [KS.k()])
                    tt(BB.t[:, :], KK.t[:, :], AA.t[:, :], ALU.mult, [KK.k(), AA.k()], [BB.k()])
                    yield
                    tt(T1.t[:, :], RS.t[:, :], KS.t[:, :], ALU.mult, [RS.k(), KS.k()], [T1.k()])
                    ts(RKB.t[:, :], T1.t[:, :], pv(PV_RK + hp), None, ALU.mult, ALU.bypass, [T1.k()] + KPV, [RKB.k()])
                    mm(PS.t[:, B2, SB:2 * SB], bones_bf, RKB.t[:, :], True, True, KCB + [RKB.k()], psk(B2))
                    tt(BON[hp].t[:, :], PS.t[:, B2, SB:2 * SB], VB.t[:, hp, :], ALU.mult, psk(B2) + [VB.k(hp)],
                       [BON[hp].k()])
                    yield
                    stt(AR[hp].t[:, :, 0:64], r3(KK.t[:, :], CH), -1.0, r3(EGM.t[:, :], CH), ALU.mult, ALU.mult,
                        [KK.k(), EGM.k()], [AR[hp].k()])
                    tt(AR[hp].t[:, :, 64:128], r3(RS.t[:, :], CH), r3(EG.t[:, :], CH), ALU.mult, [RS.k(), EG.k()],
                       [AR[hp].k()])
                    yield
                    tt(BT[hp].t[:, :], BB.t[:, :], ENG.t[:, :], ALU.mult, [BB.k(), ENG.k()], [BT[hp].k()])
                    tt(KT[hp].t[:, :], KS.t[:, :], ENG.t[:, :], ALU.mult, [KS.k(), ENG.k()], [KT[hp].k()])
                    yield
                    gcb = GC[hp].t[:, :].unsqueeze(2).to_broadcast([128, NCH, CH])
                    tt(r3(BHt[hp].t[:, :], CH), r3(BT[hp].t[:, :], CH), gcb, ALU.mult, [BT[hp].k(), GC[hp].k()],
                       [BHt[hp].k()])
                    tt(r3(KHt[hp].t[:, :], CH), r3(KT[hp].t[:, :], CH), gcb, ALU.mult, [KT[hp].k(), GC[hp].k()],
                       [KHt[hp].k()])

            def stage_P2(sbi, hp):
                par = sbi % 2
                VB, AR, GC, BON = VBp[par], ARp[par], GCp[par], BONp[par]
                for j in range(NCH):
                    cs = slice(j * CH, (j + 1) * CH)
                    for q in range(4):
                        for h in range(2):
                            hs = HS[h]
                            src, sk = ((AR[hp].t[hs, j, 0:64], AR[hp].k()), (BHt[hp].t[hs, cs], BHt[hp].k()),
                                       (KHt[hp].t[hs, cs], KHt[hp].k()), (VB.t[hs, hp, cs], VB.k(hp)))[q]
                            bank = hp * 3 + j // 2
                            col = ((j % 2) * 4 + q) * 64
                            mm(PS.t[hs, bank, col:col + 64], src, ident2[hs, :], True, True, [sk] + KCB, psk(bank))
                for a_ in range(2):
                    bank = hp * 3 + a_
                    cp(TOK[hp].t[:, 2 * a_:2 * a_ + 2, :, :],
                       PS.t[:, bank, :].rearrange("p (a b c) -> p a b c", b=4, c=64), psk(bank), [TOK[hp].k()],
                       eng="act")

            def stage_P1(sbi, hp):
                par = sbi % 2
                VB, AR, GC, BON = VBp[par], ARp[par], GCp[par], BONp[par]
                B0, B1, B2 = hp * 3, hp * 3 + 1, hp * 3 + 2
                for j in range(NCH):
                    cs = slice(j * CH, (j + 1) * CH)
                    for h in range(2):
                        hs = HS[h]
                        mm(PS.t[hs, B0, j * 128:(j + 1) * 128], BT[hp].t[hs, cs], AR[hp].t[hs, j, :], True, True,
                           [BT[hp].k(), AR[hp].k()], psk(B0))
                    for h in range(2):
                        hs = HS[h]
                        mm(PS.t[hs, B1, j * 128:(j + 1) * 128], KT[hp].t[hs, cs], AR[hp].t[hs, j, :], True, True,
                           [KT[hp].k(), AR[hp].k()], psk(B1))
                    for h in range(2):
                        hs = HS[h]
                        mm(PS.t[hs, B2, j * 64:(j + 1) * 64], AR[hp].t[hs, j, 0:64], BT[hp].t[hs, cs], True, True,
                           [BT[hp].k(), AR[hp].k()], psk(B2))
                tt(NTMR[hp].t[:, :, :], r3(PS.t[:, B0, :], 128), rmaskb, ALU.mult, psk(B0) + KCB, [NTMR[hp].k()])
                tt(MKT[hp].t[:, :, :], r3(PS.t[:, B1, :], 128), rmaskb, ALU.mult, psk(B1) + KCB, [MKT[hp].k()])
                tt(NK[hp][0].t[:, :, :], r3(PS.t[:, B2, 0:SB], 64), masklb, ALU.mult, psk(B2) + KCB,
                   [NK[hp][0].k()])

            def stage_P3(sbi, hp):
                par = sbi % 2
                VB, AR, GC, BON = VBp[par], ARp[par], GCp[par], BONp[par]
                B2 = hp * 3 + 2
                for j, h, hs, cs in jh():
                    mm(PS.t[hs, B2, SB + j * 64:SB + (j + 1) * 64], MKT[hp].t[hs, j, 0:64], TOK[hp].t[hs, j, 3, :],
                       True, True, [MKT[hp].k(), TOK[hp].k()], psk(B2))
                cp(MVB[hp].t[:, :, :], r3(PS.t[:, B2, SB:2 * SB], 64), psk(B2), [MVB[hp].k()], eng="act")

            def stage_P4(sbi, hp, r):
                nk, ntq = NK[hp], NTQ[hp]
                B0, B1, B3 = hp * 3, hp * 3 + 1, hp * 3 + 2
                if r == 0:
                    for j in range(NCH):
                        for h in range(2):
                            hs = HS[h]
                            mm(PS.t[hs, B3, j * 64:(j + 1) * 64], NTMR[hp].t[hs, j, 0:64], nk[0].t[hs, j, :], True,
                               True, [NTMR[hp].k(), nk[0].k()], psk(B3))
                        for h in range(2):
                            hs = HS[h]
                            mm(PS.t[hs, B0, j * 128:j * 128 + 64], nk[0].t[hs, j, :], NTMR[hp].t[hs, j, 0:64], True,
                               True, [NTMR[hp].k(), nk[0].k()], psk(B0))
                    cp(nk[1].t[:, :, :], r3(PS.t[:, B3, 0:SB], 64), psk(B3), [nk[1].k()], eng="act")
                    cp(ntq[1].t[:, :, 0:64], r3(PS.t[:, B0, :], 128)[:, :, 0:64], psk(B0), [ntq[1].k()], eng="act")
                    tt(ntq[1].t[:, :, 64:128], NTMR[hp].t[:, :, 0:64], ident2b, ALU.add, [NTMR[hp].k()] + KCB,
                       [ntq[1].k()])
                elif r < 5:
                    cur, nxt = r % 2, 1 - (r % 2)
                    ca = 0 if r % 2 == 0 else SB
                    bb = B0 if r % 2 == 0 else B1
                    for j in range(NCH):
                        for h in range(2):
                            hs = HS[h]
                            mm(PS.t[hs, B3, ca + j * 64:ca + (j + 1) * 64], ntq[cur].t[hs, j, 0:64],
                               nk[cur].t[hs, j, :], True, True, [ntq[cur].k(), nk[cur].k()], psk(B3))
                        for h in range(2):
                            hs = HS[h]
                            if r < 4:
                                mm(PS.t[hs, bb, j * 128:(j + 1) * 128], nk[cur].t[hs, j, :], ntq[cur].t[hs, j, :],
                                   True, True, [ntq[cur].k(), nk[cur].k()], psk(bb))
                            else:
                                mm(PS.t[hs, bb, j * 128 + 64:(j + 1) * 128], nk[cur].t[hs, j, :],
                                   ntq[cur].t[hs, j, 64:128], True, True, [ntq[cur].k(), nk[cur].k()], psk(bb))
                    cp(nk[nxt].t[:, :, :], r3(PS.t[:, B3, ca:ca + SB], 64), psk(B3), [nk[nxt].k()], eng="act")
                    if r < 4:
                        cp(ntq[nxt].t[:, :, 0:64], r3(PS.t[:, bb, :], 128)[:, :, 0:64], psk(bb), [ntq[nxt].k()],
                           eng="act")
                    tt(ntq[nxt].t[:, :, 64:128], ntq[cur].t[:, :, 64:128], r3(PS.t[:, bb, :], 128)[:, :, 64:128],
                       ALU.add, [ntq[cur].k()] + psk(bb), [ntq[nxt].k()])
                else:
                    for j, h, hs, cs in jh():
                        mm(PS.t[hs, B3, j * 64:(j + 1) * 64], nk[1].t[hs, j, :], ntq[1].t[hs, j, 64:128], True, True,
                           [ntq[1].k(), nk[1].k()], psk(B3))
                    tt(QF[hp].t[:, :, :], ntq[1].t[:, :, 64:128], r3(PS.t[:, B3, 0:SB], 64), ALU.add,
                       [ntq[1].k()] + psk(B3), [QF[hp].k()])

            def stage_P5(sbi, hp):
                par = sbi % 2
                VB, AR, GC, BON = VBp[par], ARp[par], GCp[par], BONp[par]
                B0 = hp * 3
                for j in range(NCH):
                    for h in range(2):
                        hs = HS[h]
                        mm(PS.t[hs, B0, j * 64:(j + 1) * 64], TOK[hp].t[hs, j, 0, :], QF[hp].t[hs, j, :], True, True,
                           [TOK[hp].k(), QF[hp].k()], psk(B0))
                    for h in range(2):
                        hs = HS[h]
                        mm(PS.t[hs, B0, SB + j * 64:SB + (j + 1) * 64], QF[hp].t[hs, j, :], MVB[hp].t[hs, j, :],
                           True, True, [MVB[hp].k(), QF[hp].k()], psk(B0))
                cp(WT[hp].t[:, :, :], r3(PS.t[:, B0, 0:SB], 64), psk(B0), [WT[hp].k()], eng="dve")
                cp(UT[hp].t[:, :, :], r3(PS.t[:, B0, SB:2 * SB], 64), psk(B0), [UT[hp].k()], eng="dve")

            def chain_step(sbi, hp, j):
                par = sbi % 2
                VB, AR, GC, BON = VBp[par], ARp[par], GCp[par], BONp[par]
                B2 = hp * 3 + 2
                hcur = HB[hp][hpar[hp]]
                hnxt = HB[hp][1 - hpar[hp]]
                hpar[hp] = 1 - hpar[hp]
                tok, ub = TOK[hp], UB[hp]
                for h in range(2):
                    hs = HS[h]
                    mm(PS.t[hs, B2, 0:64], WT[hp].t[hs, j, :], hcur.t[hs, :], True, True, [WT[hp].k(), hcur.k()],
                       psk(B2))
                tt(ub.t[:, :], PS.t[:, B2, 0:64], UT[hp].t[:, j, :], ALU.add, psk(B2) + [UT[hp].k()], [ub.k()])
                for h in range(2):
                    hs = HS[h]
                    mm(PS.t[hs, B2, 64:128], tok.t[hs, j, 2, :], tok.t[hs, j, 3, :], True, False, [tok.k()], psk(B2))
                for h in range(2):
                    hs = HS[h]
                    mm(PS.t[hs, B2, 64:128], tok.t[hs, j, 1, :], ub.t[hs, :], False, True, [tok.k(), ub.k()], psk(B2))
                for h in range(2):
                    hs = HS[h]
                    yo = PS.t[hs, B2, 128 + j * 64:128 + (j + 1) * 64]
                    mm(yo, hcur.t[hs, :], AR[hp].t[hs, j, 64:128], True, False, [hcur.k(), AR[hp].k()], psk(B2))
                for h in range(2):
                    hs = HS[h]
                    yo = PS.t[hs, B2, 128 + j * 64:128 + (j + 1) * 64]
                    mm(yo, ub.t[hs, :], NTMR[hp].t[hs, j, 64:128], False, False, [ub.k(), NTMR[hp].k()], psk(B2))
                for h in range(2):
                    hs = HS[h]
                    yo = PS.t[hs, B2, 128 + j * 64:128 + (j + 1) * 64]
                    mm(yo, tok.t[hs, j, 3, :], MKT[hp].t[hs, j, 64:128], False, True, [tok.k(), MKT[hp].k()], psk(B2))
                stt(H32[hp].t[:, :], H32[hp].t[:, :], GC[hp].t[:, j:j + 1], PS.t[:, B2, 64:128], ALU.mult, ALU.add,
                    [H32[hp].k(), GC[hp].k()] + psk(B2), [H32[hp].k()])
                cp(hnxt.t[:, :], H32[hp].t[:, :], [H32[hp].k()], [hnxt.k()], eng="act")

            def stage_GN(sbi, hp):
                par = sbi % 2
                VB, AR, GC, BON = VBp[par], ARp[par], GCp[par], BONp[par]
                B1, B2 = hp * 3 + 1, hp * 3 + 2
                tbx = (sbi * SB) // TB
                ssl = slice(sbi * SB, (sbi + 1) * SB)
                YS, YQ, MEANc, VARc = (GNT[hp][n] for n in ("YS", "YQ", "MEAN", "VAR"))
                cp(YS.t[:, :], PS.t[:, B2, 128:128 + SB], psk(B2), [YS.k()], eng="act")
                act(YQ.t[:, :], PS.t[:, B2, 128:128 + SB], AF.Square, psk(B2), [YQ.k()])
                yield
                mm(PS.t[:, B1, 0:SB], bones32, YS.t[:, :], True, True, KCF + [YS.k()], psk(B1))
                mm(PS.t[:, B1, SB:2 * SB], bones32, YQ.t[:, :], True, True, KCF + [YQ.k()], psk(B1))
                act(MEANc.t[:, :], PS.t[:, B1, 0:SB], AF.Identity, psk(B1), [MEANc.k()], scale=1.0 / 64)
                act(VARc.t[:, :], PS.t[:, B1, 0:SB], AF.Square, psk(B1), [VARc.k()], scale=1.0 / 64)
                stt(VARc.t[:, :], PS.t[:, B1, SB:2 * SB], 1.0 / 64, VARc.t[:, :], ALU.mult, ALU.subtract,
                    psk(B1) + [VARc.k()], [VARc.k()])
                yield
                rpow(VARc.t[:, :], VARc.t[:, :], -0.5, [VARc.k()] + KDV, [VARc.k()], bias=dv(DV_EPSGN))
                yield
                tt(YS.t[:, :], YS.t[:, :], MEANc.t[:, :], ALU.subtract, [YS.k(), MEANc.k()], [YS.k()])
                tt(YS.t[:, :], YS.t[:, :], VARc.t[:, :], ALU.mult, [YS.k(), VARc.k()], [YS.k()])
                yield
                act(YS.t[:, :], YS.t[:, :], AF.Identity, [YS.k()] + KPV, [YS.k()],
                    scale=pv(PV_GNW + hp), bias=pv(PV_GNB + hp))
                tt(YS.t[:, :], YS.t[:, :], BON[hp].t[:, :], ALU.add, [YS.k(), BON[hp].k()], [YS.k()])
                yv = Y.t[:, 2 + hp, ssl]
                tt(yv, yv, YS.t[:, :], ALU.mult, [YS.k(), Y.k(2 + hp, tbx)], [Y.k(2 + hp, tbx)])

            def warm(n, bank=0):
                if n <= 0:
                    return

                def fn(e):
                    ins = None
                    for _ in range(n):
                        ins = e.matmul(PS.t[:, bank, :], lhsT=CBT.t[:, CB_ID:CB_ID + 128],
                                       rhs=CBT.t[:, CB_MP:CB_MP + 512], start=True, stop=True)
                    return ins
                P.add("pe", fn, KCB, psk(bank), cost=0.3 * n)

            def gen_Eall(sbi):
                for _ in prologue(sbi):
                    yield
                alive = [stage_E(sbi, 0), stage_E(sbi, 1)]
                while alive:
                    for g_ in list(alive):
                        try:
                            next(g_)
                        except StopIteration:
                            alive.remove(g_)
                        yield

            def gen_PC(sbi):
                warm(WARM_BURST)
                for hp in range(2):
                    stage_P2(sbi, hp)
                    yield
                for hp in range(2):
                    stage_P1(sbi, hp)
                    yield
                for hp in range(2):
                    stage_P3(sbi, hp)
                    yield
                for r in range(6):
                    for hp in range(2):
                        stage_P4(sbi, hp, r)
                        warm(WARM_P4)
                        yield
                for hp in range(2):
                    stage_P5(sbi, hp)
                    yield
                for j in range(NCH):
                    for hp in range(2):
                        chain_step(sbi, hp, j)
                        warm(WARM_CH)
                        yield

            def run_interleaved(*gens):
                alive = list(gens)
                while alive:
                    for g_ in list(alive):
                        try:
                            next(g_)
                        except StopIteration:
                            alive.remove(g_)

            def gen_GN(sbi):
                alive = [stage_GN(sbi, 0), stage_GN(sbi, 1)]
                while alive:
                    for g_ in list(alive):
                        try:
                            next(g_)
                        except StopIteration:
                            alive.remove(g_)
                        yield

            run_interleaved(gen_Eall(0))
            gg = iter(())
            for sbi in range(NSB):
                ge = gen_Eall(sbi + 1) if sbi + 1 < NSB else iter(())
                for i_, _ in enumerate(gen_PC(sbi)):
                    if i_ < E_START:
                        for _k in range(3):
                            next(gg, None)
                    else:
                        if i_ == E_START:
                            run_interleaved(gg)
                        for _k in range(E_CHAIN if i_ >= 20 else 1):
                            next(ge, None)
                run_interleaved(ge)
                gg = gen_GN(sbi)
            run_interleaved(gg)
            for g in range(gbase + 23, gbase + 30):
                wdone(g)

        P.set_fence()
        with ExitStack() as ph:
          if stage >= 5:
            SQ = [sb("LSQ0", [128, TB], F32, ph), sb("LSQ1", [128, TB], F32, ph)]
            MEAN = sb("LMEAN", [128, TB], F32, ph)
            RSTD = sb("LRSTD", [128, TB], F32, ph)
            L1 = sb("L1", [128, TB], F32, ph)

            def outproj_tb(tb):
                for j in range(2):
                    wap, wk = wslot(gbase + 30 + j)
                    for mm_ in range(4):
                        m = 4 * j + mm_
                        b = next_bank()
                        for kc in range(2):
                            mm(PS.t[:, b, :], wap[:, kc * 4 + mm_, :], Y.t[:, 2 + kc, tb * TB:(tb + 1) * TB],
                               kc == 0, kc == 1, [wk, Y.k(2 + kc, tb)], psk(b))
                        xs = X32.t[:, m, tb * TB:(tb + 1) * TB]
                        tt(xs, xs, PS.t[:, b, :], ALU.add, psk(b) + [X32.k(m, tb)], [X32.k(m, tb)])

            def ln_tb(tb):
                sl = slice(tb * TB, (tb + 1) * TB)
                for m in range(KC):
                    mm(PS.t[:, 2, :], ones32, X32.t[:, m, sl], m == 0, m == KC - 1, KCF + [X32.k(m, tb)], psk(2))
                for m in range(KC):
                    sq = SQ[m % 2]
                    act(sq.t[:, :], X32.t[:, m, sl], AF.Square, [X32.k(m, tb)], [sq.k()])
                    mm(PS.t[:, 3, :], ones32, sq.t[:, :], m == 0, m == KC - 1, KCF + [sq.k()], psk(3))
                act(MEAN.t[:, :], PS.t[:, 2, :], AF.Identity, psk(2), [MEAN.k()], scale=1.0 / D)
                act(L1.t[:, :], PS.t[:, 2, :], AF.Square, psk(2), [L1.k()], scale=1.0 / D)
                stt(L1.t[:, :], PS.t[:, 3, :], 1.0 / D, L1.t[:, :], ALU.mult, ALU.subtract, psk(3) + [L1.k()], [L1.k()])
                rpow(RSTD.t[:, :], L1.t[:, :], -0.5, [L1.k()] + KDV, [RSTD.k()], bias=dv(DV_EPSLN))
                tt(MEAN.t[:, :], MEAN.t[:, :], RSTD.t[:, :], ALU.mult, [MEAN.k(), RSTD.k()], [MEAN.k()])
                for m in range(KC):
                    xs = X32.t[:, m, sl]
                    tt(xs, xs, RSTD.t[:, :], ALU.mult, [X32.k(m, tb), RSTD.k()], [X32.k(m, tb)])
                    tt(xs, xs, MEAN.t[:, :], ALU.subtract, [X32.k(m, tb), MEAN.k()], [X32.k(m, tb)])
                    act(xs, xs, AF.Identity, [X32.k(m, tb)] + KPV, [X32.k(m, tb)],
                        scale=pv(PV_LNG + m), bias=pv(PV_LNB + m))
                    if li < NLY - 1:
                        cp(XBF.t[:, m, sl], xs, [X32.k(m, tb)], [XBF.k(m, tb)], eng="act")

            outproj_tb(0)
            for tb in range(NTB):
                if tb + 1 < NTB:
                    outproj_tb(tb + 1)
                ln_tb(tb)
            for j in range(2):
                wdone(gbase + 30 + j)

    outk = []
    for kc in range(KC):
        for tb in range(NTB):
            k = ("out", kc, tb)
            dma(y_out[kc * 128:(kc + 1) * 128, tb * TB:(tb + 1) * TB], X32.t[:, kc, tb * TB:(tb + 1) * TB],
                [X32.k(kc, tb)], [k])
            outk.append(k)
    P.add("sp", None, outk, (), cost=0.01)

    esems = {e: es.enter_context(nc.semaphore("sem_" + e)) for e in Prog.ENG}
    dsems = [es.enter_context(nc.semaphore("dsem%d" % i)) for i in range(NDSEM)]
    block = es.enter_context(nc.Block())
    P.emit(nc, block, esems, dsems)
    es.close()
    return nc, P


_CACHE = {}


def _get_program(layers):
    key = tuple(layers)
    if key not in _CACHE:
        _CACHE[key] = build(list(layers))
    return _CACHE[key]


def run_layers(x, inp, layers):
    ws = _prep_weights(inp)
    pvs, sws = _prep_small(inp)
    cb, cf = _consts()
    ls = list(layers)
    ws = np.ascontiguousarray(ws[ls])
    pvs = np.ascontiguousarray(pvs[ls])
    sws = np.ascontiguousarray(sws[ls])
    nc, _ = _get_program(ls)
    in_maps = []
    for b in range(8):
        in_maps.append({"xT": np.ascontiguousarray(x[b].T), "wst": ws, "pv": pvs, "sw": sws, "cb": cb, "cf": cf})
    res = run_bass_kernel_spmd(nc, in_maps, core_ids=list(range(8)))
    out = np.stack([np.asarray(res.results[b]["yT"]).T for b in range(8)])
    return np.ascontiguousarray(out.astype(np.float32))


def kernel(**inputs):
    x = np.asarray(inputs["x"], np.float32)
    return run_layers(x, inputs, range(DEPTH))
```

```python
import math
from contextlib import ExitStack

import numpy as np
import concourse.bass as bass
import concourse.mybir as mybir
from concourse.bass_utils import run_bass_kernel_spmd

F32 = mybir.dt.float32
BF16 = mybir.dt.bfloat16
AF = mybir.ActivationFunctionType
ALU = mybir.AluOpType

T = 2048
TB = 512
NTB = T // TB
D = 1024
KC = 8
DEPTH = 4
CH = 64
SB = 256
NCH = SB // CH
NSB = T // SB
NSLOT = 8
NWCH = 32
ALPHA = (2 * DEPTH) ** 0.25
LN_EPS = 1e-5
GN_EPS = 64e-5
CW = math.exp(-0.5)
NEG = -30000.0
NDSEM = 40

PV_LNG, PV_LNB, PV_SINK, PV_CONVW, PV_CONVB, PV_BA, PV_BX, PV_LAM = 0, 8, 16, 20, 28, 30, 32, 34
PV_MU, PV_MUCW, PV_W0, PV_A0, PV_KK, PV_KA, PV_RK, PV_GNW, PV_GNB, PV_V0 = 36, 42, 43, 45, 47, 49, 51, 53, 55, 57
NPV = 64
DV_OMM, DV_SE, DV_LC, DV_LC2, DV_OKA = 0, 7, 11, 13, 15
DV_NBA, DV_NBX, DV_NW0, DV_NA0, DV_NV0, DV_EPSLN, DV_EPSGN, DV_EPSKK = 17, 19, 21, 23, 25, 27, 28, 29
NDV = 32
SW_LRU, SW_WA2, SW_V1, SW_V2, NSW = 0, 512, 768, 800, 1056
CB_ID, CB_ID2, CB_MP, CB_MC, CB_RM, CB_ML, CB_BO, CB_CM, CB_ON, NCB = 0, 128, 192, 704, 1216, 1344, 1408, 1536, 2048, 2112


class Op:
    __slots__ = ("idx", "eng", "fn", "dma", "deps", "odeps", "signal", "semval", "dsem", "dval", "dguard",
                 "cost", "start", "finish", "pos", "barrier")


SCHED = True
PRIO_BLEV = True
BLEV_PE_W = 0.55
CSCALE = {"pe": 1.0, "act": 1.0, "dve": 1.0, "pool": 1.0, "sp": 1.0}
XLAT = 0.5


class Prog:
    ENG = ("pe", "act", "dve", "pool", "sp")

    def __init__(self):
        self.ops = []
        self.lastw = {}
        self.readers = {}
        self.fence = {}
        self.since_fence = {e: [] for e in self.ENG}

    def add(self, eng, fn, r=(), w=(), dma=False, cost=0.3):
        op = Op()
        op.idx = len(self.ops)
        op.eng = eng
        op.fn = fn
        op.dma = dma
        op.signal = False
        op.semval = 0
        op.cost = cost
        op.barrier = False
        deps = {}
        ps_r = [k for k in r if k[0] == "PS"]
        if ps_r:
            w = list(w) + [k for k in ps_r if k not in w]

        def dep(d):
            if d is not None:
                deps[d.idx] = d

        for k in r:
            dep(self.lastw.get(k))
        for k in w:
            if k not in self.lastw and k not in self.readers:
                for d in self.fence.values():
                    dep(d)
            dep(self.lastw.get(k))
            for d in self.readers.get(k, ()):
                dep(d)
        deps.pop(op.idx, None)
        op.odeps = list(deps.values())
        op.deps = [d for d in op.odeps if d.dma or dma or not (d.eng == eng and eng == "pe")]
        for k in w:
            self.lastw[k] = op
            self.readers[k] = []
        for k in r:
            self.readers.setdefault(k, []).append(op)
        self.ops.append(op)
        if not dma and fn is not None:
            for k in list(r) + list(w):
                if isinstance(k[0], str) and "_L" in k[0]:
                    self.since_fence[eng].append(op)
                    break
        return op

    def set_fence(self):
        for e in self.ENG:
            prev = self.since_fence[e]
            if not prev:
                continue
            b = Op()
            b.idx = len(self.ops)
            b.eng = e
            b.fn = None
            b.dma = False
            b.signal = False
            b.semval = 0
            b.cost = 0.0
            b.barrier = True
            b.odeps = list(prev)
            b.deps = []
            self.ops.append(b)
            self.fence[e] = b
            self.since_fence[e] = [b]

    def schedule(self):
        ops = self.ops
        per_eng = {e: [] for e in self.ENG}
        if not SCHED:
            for op in ops:
                op.start = float(op.idx)
                op.pos = len(per_eng[op.eng])
                per_eng[op.eng].append(op)
            return per_eng, list(ops)
        succ = [[] for _ in ops]
        indeg = [0] * len(ops)
        cst = [op.cost * (1.0 if op.dma else CSCALE[op.eng]) for op in ops]
        for op in ops:
            indeg[op.idx] = len(op.odeps)
            for d in op.odeps:
                succ[d.idx].append(op)
        blev = [0.0] * len(ops)
        for op in reversed(ops):
            b_ = 0.0
            for s_ in succ[op.idx]:
                v = blev[s_.idx] + (0.0 if (s_.eng == op.eng and not op.dma) else XLAT)
                if v > b_:
                    b_ = v
            blev[op.idx] = b_ + cst[op.idx] * (BLEV_PE_W if op.eng == "pe" else 1.0)
        ready = {e: [] for e in self.ENG}
        eng_t = {e: 0.0 for e in self.ENG}

        def push(op):
            rt = 0.0
            for d in op.odeps:
                t = d.finish + (0.0 if (d.eng == op.eng and not d.dma) else XLAT)
                if t > rt:
                    rt = t
            ready[op.eng].append((rt, (-blev[op.idx] if PRIO_BLEV else op.idx), op))

        for op in ops:
            if indeg[op.idx] == 0:
                push(op)
        order = []
        for _ in range(len(ops)):
            best = None
            for e in self.ENG:
                rl = ready[e]
                if not rl:
                    continue
                te = eng_t[e]
                c = min(rl, key=lambda x: (x[0] if x[0] > te else te, x[1]))
                st = c[0] if c[0] > te else te
                if best is None or (st, c[1]) < (best[0], best[1][1]):
                    best = (st, c, e)
            st, c, e = best
            ready[e].remove(c)
            op = c[2]
            op.start = st
            if op.dma:
                eng_t[e] = st + 0.06
                op.finish = st + op.cost
            else:
                op.finish = st + cst[op.idx]
                eng_t[e] = op.finish
            op.pos = len(per_eng[e])
            per_eng[e].append(op)
            order.append(op)
            for s_ in succ[op.idx]:
                indeg[s_.idx] -= 1
                if indeg[s_.idx] == 0:
                    push(s_)
        self.est_us = max(eng_t.values())
        return per_eng, order

    def emit(self, nc, block, esems, dsems):
        per_eng, order = self.schedule()
        def real_before(d):
            lst = per_eng[d.eng]
            i = d.pos - 1
            while i >= 0:
                c = lst[i]
                if c.fn is not None and not c.dma:
                    return c
                i -= 1
            return None

        for op in order:
            if op.barrier:
                op.deps = []
                continue
            lastdep = {}
            keep = []
            for d in op.deps:
                if d.barrier:
                    d = real_before(d)
                    if d is None:
                        continue
                if d.dma:
                    keep.append(d)
                else:
                    cur = lastdep.get(d.eng)
                    if cur is None or d.pos > cur.pos:
                        lastdep[d.eng] = d
            for d in lastdep.values():
                if d.eng == op.eng and d.pos > op.pos:
                    raise RuntimeError("scheduler order violation")
                d.signal = True
                keep.append(d)
            op.deps = keep
        cnt = {e: 0 for e in self.ENG}
        dval = [0] * len(dsems)
        rrs = {"pool": 0, "sp": 0}
        half = len(dsems) // 2
        for op in order:
            if op.dma:
                if op.eng == "pool":
                    s = rrs["pool"] % half
                    rrs["pool"] += 1
                else:
                    s = half + rrs["sp"] % (len(dsems) - half)
                    rrs["sp"] += 1
                op.dsem = s
                op.dguard = dval[s]
                dval[s] += 16
                op.dval = dval[s]
        for e in self.ENG:
            for op in per_eng[e]:
                if (not op.dma) and op.signal:
                    cnt[e] += 1
                    op.semval = cnt[e]
        self.stats = dict(cnt)

        def run(eng_name, e):
            seen = {}

            def wait(key, sem, val):
                if val <= 0 or seen.get(key, 0) >= val:
                    return
                seen[key] = val
                e.wait_ge(sem, val)

            for op in per_eng[eng_name]:
                for d in op.deps:
                    if d.dma:
                        wait(("d", d.dsem), dsems[d.dsem], d.dval)
                    else:
                        wait(("e", d.eng), esems[d.eng], d.semval)
                if op.dma:
                    wait(("d", op.dsem), dsems[op.dsem], op.dguard)
                if op.fn is None:
                    continue
                ins = op.fn(e)
                if op.dma:
                    ins.then_inc(dsems[op.dsem], 16)
                elif op.signal:
                    ins.then_inc(esems[op.eng], 1)

        @block.sync
        def _(e):
            run("sp", e)

        @block.gpsimd
        def _(e):
            run("pool", e)

        @block.tensor
        def _(e):
            run("pe", e)

        @block.scalar
        def _(e):
            run("act", e)

        @block.vector
        def _(e):
            run("dve", e)


class Tile:
    def __init__(self, t, name):
        self.t = t
        self.name = name

    def k(self, *idx):
        return (self.name,) + tuple(idx)


def _chunk_cols(w, cols):
    out = np.zeros((KC, 128, 128), np.float32)
    out[:, :, : len(cols)] = w[:, cols].reshape(KC, 128, len(cols))
    return np.ascontiguousarray(out.transpose(1, 0, 2))


def _prep_weights(inp):
    w_in = np.asarray(inp["w_in"], np.float32)
    w_out = np.asarray(inp["w_out"], np.float32)
    ws = np.zeros((DEPTH, NWCH, 128, KC, 128), np.float32)
    ar = np.arange
    for l in range(DEPTH):
        ch = []
        for cb in range(2):
            ch.append(_chunk_cols(w_in[l], 1536 + 128 * cb + ar(128)))
        for cb in range(2):
            ch.append(_chunk_cols(w_in[l], 1280 + 128 * cb + ar(128)))
        for c in range(4):
            ch.append(_chunk_cols(w_in[l], 128 * c + ar(128)))
        for g in range(2):
            kc_ = 512 + 64 * g + ar(64)
            ch.append(_chunk_cols(w_in[l], np.concatenate([kc_, kc_])))
        ch.append(_chunk_cols(w_in[l], 640 + ar(128)))

        def outp2(row0):
            res = []
            for j in range(2):
                o = np.zeros((128, KC, 128), np.float32)
                for kc in range(2):
                    for mm in range(4):
                        m = 4 * j + mm
                        o[:, kc * 4 + mm, :] = w_out[l][row0 + kc * 128:row0 + (kc + 1) * 128, m * 128:(m + 1) * 128]
                res.append(o)
            return res

        ch += outp2(512)
        for c in range(4):
            ch.append(_chunk_cols(w_in[l], 768 + 128 * c + ar(128)))
        for j in range(4):
            o = np.zeros((128, KC, 128), np.float32)
            for kc in range(4):
                for mm in range(2):
                    m = 2 * j + mm
                    o[:, kc * 2 + mm, :] = w_out[l][kc * 128:(kc + 1) * 128, m * 128:(m + 1) * 128]
            ch.append(o)
        for hp in range(2):
            ch.append(_chunk_cols(w_in[l], 2624 + 128 * hp + ar(128)))
        ch.append(_chunk_cols(w_in[l], 2560 + ar(64)))
        for hp in range(2):
            ch.append(_chunk_cols(w_in[l], 2304 + 128 * hp + ar(128)))
        for hp in range(2):
            ch.append(_chunk_cols(w_in[l], 1792 + 128 * hp + ar(128)))
            ch.append(_chunk_cols(w_in[l], 2048 + 128 * hp + ar(128)))
        ch += outp2(768)
        assert len(ch) == NWCH
        ws[l] = np.stack(ch)
    return ws


def _prep_small(inp):
    f = lambda k: np.asarray(inp[k], np.float32)
    pv = np.zeros((DEPTH, 128, NPV), np.float32)
    sw = np.zeros((DEPTH, 128, NSW), np.float32)
    p = np.arange(128)
    for l in range(DEPTH):
        for kc in range(KC):
            pv[l, :, PV_LNG + kc] = f("ln_g")[l, kc * 128 + p]
            pv[l, :, PV_LNB + kc] = f("ln_b")[l, kc * 128 + p]
        for c in range(4):
            pv[l, :, PV_SINK + c] = f("attn_sinks")[l, 2 * c + p // 64]
        for j in range(4):
            for cb in range(2):
                pv[l, :, PV_CONVW + j * 2 + cb] = f("conv_w")[l, j, cb * 128 + p]
        for cb in range(2):
            pv[l, :, PV_CONVB + cb] = f("conv_b")[l, cb * 128 + p]
            pv[l, :, PV_BA + cb] = f("lru_ba")[l, cb * 128 + p]
            pv[l, :, PV_BX + cb] = f("lru_bx")[l, cb * 128 + p]
            pv[l, :, PV_LAM + cb] = f("lru_lambda")[l, cb * 128 + p]
        mu = f("rwkv_mu")[l]
        for q in range(3):
            for hp in range(2):
                pv[l, :, PV_MU + q * 2 + hp] = mu[q * 256 + hp * 128 + p]
        pv[l, :64, PV_MUCW] = mu[768:832]
        for hp in range(2):
            s = hp * 128 + p
            pv[l, :, PV_W0 + hp] = f("rwkv_w0")[l, s]
            pv[l, :, PV_A0 + hp] = f("rwkv_a0")[l, s]
            pv[l, :, PV_KK + hp] = f("rwkv_kk")[l, s]
            pv[l, :, PV_KA + hp] = f("rwkv_ka")[l, s]
            pv[l, :, PV_RK + hp] = f("rwkv_rk")[l].reshape(256)[s]
            pv[l, :, PV_GNW + hp] = f("rwkv_gn_w")[l, s]
            pv[l, :, PV_GNB + hp] = f("rwkv_gn_b")[l, s]
            if l > 0:
                pv[l, :, PV_V0 + hp] = f("rwkv_v0")[l - 1, s]
        wa, wx = f("lru_wa")[l], f("lru_wx")[l]
        for ax, wmat in enumerate((wa, wx)):
            for cb in range(2):
                for i in range(2):
                    c0 = SW_LRU + (ax * 2 + cb) * 128 + i * 64
                    sw[l, i * 64:(i + 1) * 64, c0:c0 + 64] = wmat[2 * cb + i]
        sw[l, 0:32, SW_WA2:SW_WA2 + 256] = f("rwkv_w2")[l]
        sw[l, 32:64, SW_WA2:SW_WA2 + 256] = f("rwkv_a2")[l]
        if l > 0:
            v1 = f("rwkv_v1")[l - 1]
            for hp in range(2):
                sw[l, :, SW_V1 + hp * 16:SW_V1 + hp * 16 + 16] = v1[hp * 128:(hp + 1) * 128, :]
            sw[l, 0:16, SW_V2:SW_V2 + 256] = f("rwkv_v2")[l - 1]
    return pv, sw


def _consts():
    cb = np.zeros((128, NCB), np.float32)
    p = np.arange(128)[:, None]
    q = np.arange(128)[None, :]
    cb[:, CB_ID:CB_ID + 128] = (p == q)
    cb[:, CB_ID2:CB_ID2 + 64] = ((p % 64) == np.arange(64)[None, :])
    mp = np.where(q < p, 0.0, NEG)
    mc = np.where(q >= p, 0.0, NEG)
    cb[:, CB_MP:CB_MP + 512] = np.tile(mp, (1, 4))
    cb[:, CB_MC:CB_MC + 512] = np.tile(mc, (1, 4))
    s = (np.arange(128) % 64)[:, None]
    t = np.arange(64)[None, :]
    cb[:, CB_RM:CB_RM + 64] = (t > s)
    cb[:, CB_RM + 64:CB_RM + 128] = (t >= s)
    cb[:, CB_ML:CB_ML + 64] = (t < s)
    cb[:, CB_BO:CB_BO + 128] = ((p // 64) == (q // 64))
    cb[:, CB_CM:CB_CM + 512] = ((np.arange(512) % 64) != 0)[None, :]
    cb[:, CB_ON:CB_ON + 64] = 1.0
    cf = np.zeros((128, 256), np.float32)
    cf[:, 0:128] = ((p // 64) == (q // 64))
    cf[:, 128:256] = 1.0
    return cb, cf


E_START, E_CHAIN = 6, 2
WARM_P4, WARM_CH = 0, 0
WARM_BURST = 0


def build(layers, stage=9):
    nc = bass.Bass("TRN2", target_bir_lowering=False)
    NLY = len(layers)
    x_in = nc.dram_tensor("xT", [D, T], F32, kind="ExternalInput").ap()
    wst = nc.dram_tensor("wst", [NLY, NWCH, 128, KC, 128], F32, kind="ExternalInput").ap()
    pv_d = nc.dram_tensor("pv", [NLY, 128, NPV], F32, kind="ExternalInput").ap()
    sw_d = nc.dram_tensor("sw", [NLY, 128, NSW], F32, kind="ExternalInput").ap()
    cb_d = nc.dram_tensor("cb", [128, NCB], F32, kind="ExternalInput").ap()
    cf_d = nc.dram_tensor("cf", [128, 256], F32, kind="ExternalInput").ap()
    y_out = nc.dram_tensor("yT", [D, T], F32, kind="ExternalOutput").ap()

    P = Prog()
    es = ExitStack()

    lyr = {"i": 0}

    def sb(name, shape, dt, stack=None):
        if stack is None:
            stack = es
        else:
            name = "%s_L%d" % (name, lyr["i"])
        return Tile(stack.enter_context(nc.sbuf_tensor(name, shape, dt)), name)

    X32 = sb("X32", [128, KC, T], F32)
    XBF = sb("XBF", [128, KC, T], BF16)
    Y = sb("Y", [128, 4, T], BF16)
    VF = sb("VF", [128, 2, T], BF16)
    WR = sb("WR", [128, NSLOT, KC, 128], BF16)
    PVT = sb("PVT", [128, NLY, NPV], F32)
    DVT = sb("DVT", [128, NDV], F32)
    SMW = sb("SMW", [128, NSW], BF16)
    CBT = sb("CBT", [128, NCB], BF16)
    CFT = sb("CFT", [128, 256], F32)
    PS = Tile(es.enter_context(nc.psum_tensor("PS", [128, 8, 512], F32)), "PS")

    ident = CBT.t[:, CB_ID:CB_ID + 128]
    bones_bf = CBT.t[:, CB_BO:CB_BO + 128]
    ones_bf64 = CBT.t[:, CB_ON:CB_ON + 64]
    bones32 = CFT.t[:, 0:128]
    ones32 = CFT.t[:, 128:256]
    KCB = [CBT.k()]
    KCF = [CFT.k()]
    KPV = [PVT.k()]
    KDV = [DVT.k()]
    KSW = [SMW.k()]

    def fs(ap):
        n = 1
        for d_ in ap.shape[1:]:
            n *= int(d_)
        return n

    def mm(out, lhsT, rhs, start, stop, r, w, tp=None):
        if tp is None:
            fn = lambda e: e.matmul(out, lhsT=lhsT, rhs=rhs, start=start, stop=stop)
        else:
            fn = lambda e: e.matmul(out, lhsT=lhsT, rhs=rhs, start=start, stop=stop, tile_position=tp)
        m_, n_ = fs(lhsT), fs(rhs)
        if lhsT.dtype == F32:
            c = 0.06 + n_ / 700.0
        elif m_ <= 64 and n_ <= 128:
            c = 0.03 + n_ / 6000.0
        elif n_ >= 512:
            c = 0.225
        else:
            c = 0.04 + n_ / 1200.0
        P.add("pe", fn, r, w, cost=c)

    def act(out, in_, func, r, w, scale=None, bias=None):
        kw = {}
        if scale is not None:
            kw["scale"] = scale
        if bias is not None:
            kw["bias"] = bias
        P.add("act", lambda e: e.activation(out=out, in_=in_, func=func, **kw), r, w, cost=0.22 + fs(out) / 1100.0)

    def ecost(eng, out):
        if eng == "pool":
            return 0.3 + fs(out) / 450.0
        return 0.12 + fs(out) / 900.0

    def tt(out, in0, in1, op, r, w, eng="dve"):
        P.add(eng, lambda e: e.tensor_tensor(out=out, in0=in0, in1=in1, op=op), r, w, cost=ecost(eng, out))

    def ts(out, in0, s1, s2, op0, op1, r, w, eng="dve"):
        P.add(eng, lambda e: e.tensor_scalar(out=out, in0=in0, scalar1=s1, scalar2=s2, op0=op0, op1=op1), r, w,
              cost=ecost(eng, out))

    def stt(out, in0, sc, in1, op0, op1, r, w):
        P.add("dve", lambda e: e.scalar_tensor_tensor(out=out, in0=in0, scalar=sc, in1=in1, op0=op0, op1=op1), r, w,
              cost=ecost("dve", out))

    def cp(out, in_, r, w, eng="dve"):
        if eng == "act":
            P.add("act", lambda e: e.activation(out=out, in_=in_, func=AF.Copy), r, w, cost=0.22 + fs(out) / 1100.0)
        else:
            P.add(eng, lambda e: e.tensor_copy(out=out, in_=in_), r, w, cost=ecost(eng, out))

    def scan(out, d0, d1, init, r, w):
        P.add("dve", lambda e: e.tensor_tensor_scan(out=out, data0=d0, data1=d1, initial=init,
                                                    op0=ALU.mult, op1=ALU.add), r, w, cost=0.1 + fs(out) / 450.0)

    def recip(out, in_, r, w):
        P.add("dve", lambda e: e.reciprocal(out=out, in_=in_), r, w, cost=0.1 + fs(out) / 120.0)

    def memset(ap, val, w, eng="dve"):
        P.add(eng, lambda e: e.memset(ap, val), (), w, cost=ecost(eng, ap))

    def dma(out, in_, r, w, q="sp"):
        P.add(q, lambda e: e.dma_start(out=out, in_=in_), r, w, dma=True, cost=2.5)

    one_ap = CFT.t[:, 128:129]

    def sigm(out, in_, negbias, r, w):
        act(out, in_, AF.Exp, r, w, scale=-1.0, bias=negbias)
        act(out, out, AF.Ln, list(w) + KCF, w, bias=one_ap)
        act(out, out, AF.Exp, w, w, scale=-1.0)

    def rpow(out, in_, p, r, w, bias=None):
        act(out, in_, AF.Ln, r, w, bias=bias)
        act(out, out, AF.Exp, w, w, scale=p)

    def psk(*banks):
        return [PS.k(b) for b in banks]

    def r3(ap, inner):
        return ap.rearrange("p (a b) -> p a b", b=inner)

    wstate = {"loaded": 0}
    total_chunks = NLY * NWCH

    def wload_upto(n):
        while wstate["loaded"] < min(n, total_chunks):
            g = wstate["loaded"]
            li_, i = divmod(g, NWCH)
            s = g % NSLOT
            dma(WR.t[:, s, :, :], wst[li_, i, :, :, :], (), [WR.k(s)], q="pool")
            wstate["loaded"] += 1

    def wdone(g):
        wload_upto(g + 1 + NSLOT)

    def wslot(g):
        s = g % NSLOT
        return WR.t[:, s, :, :], WR.k(s)

    dma(CBT.t[:, :], cb_d[:, :], (), KCB, q="pool")
    dma(CFT.t[:, :], cf_d[:, :], (), KCF)
    dma(PVT.t[:, :, :], pv_d.rearrange("l p c -> p l c"), (), KPV)
    for tb in range(NTB):
        for kc in range(KC):
            dma(X32.t[:, kc, tb * TB:(tb + 1) * TB], x_in[kc * 128:(kc + 1) * 128, tb * TB:(tb + 1) * TB],
                (), [X32.k(kc, tb)])
    wload_upto(NSLOT)
    for tb in range(NTB):
        for kc in range(KC):
            sl = slice(tb * TB, (tb + 1) * TB)
            cp(XBF.t[:, kc, sl], X32.t[:, kc, sl], [X32.k(kc, tb)], [XBF.k(kc, tb)],
               eng="act" if (kc + tb) % 2 == 0 else "dve")

    pbank = {"i": 0}

    def next_bank(cands=(0, 1, 6, 7)):
        b = cands[pbank["i"] % len(cands)]
        pbank["i"] += 1
        return b

    def proj(g, tb, bank):
        wap, wk = wslot(g)
        for kc in range(KC):
            mm(PS.t[:, bank, :], wap[:, kc, :], XBF.t[:, kc, tb * TB:(tb + 1) * TB],
               kc == 0, kc == KC - 1, [wk, XBF.k(kc, tb)], psk(bank))

    for li, lay in enumerate(layers):
        gbase = li * NWCH
        lyr["i"] = li

        def pv(col, n=1, rows=slice(0, 128)):
            return PVT.t[rows, li, col:col + n]

        def dv(col, n=1, rows=slice(0, 128)):
            return DVT.t[rows, col:col + n]

        dma(SMW.t[:, :], sw_d[li, :, :], (), KSW, q="pool")
        ts(dv(DV_OMM, 7), pv(PV_MU, 7), -1.0, 1.0, ALU.mult, ALU.add, KPV, KDV)
        act(dv(DV_SE, 4), pv(PV_SINK, 4), AF.Exp, KPV, KDV)
        act(dv(DV_LC, 2), pv(PV_LAM, 2), AF.Exp, KPV, KDV, scale=-1.0)
        ts(dv(DV_LC, 2), dv(DV_LC, 2), 1.0, None, ALU.add, ALU.bypass, KDV, KDV)
        act(dv(DV_LC, 2), dv(DV_LC, 2), AF.Ln, KDV, KDV)
        ts(dv(DV_LC2, 2), dv(DV_LC, 2), -16.0, None, ALU.mult, ALU.bypass, KDV, KDV)
        ts(dv(DV_LC, 2), dv(DV_LC, 2), -8.0, None, ALU.mult, ALU.bypass, KDV, KDV)
        ts(dv(DV_OKA, 2), pv(PV_KA, 2), -1.0, 1.0, ALU.mult, ALU.add, KPV, KDV)
        for dcol, pcol in ((DV_NBA, PV_BA), (DV_NBX, PV_BX), (DV_NW0, PV_W0), (DV_NA0, PV_A0), (DV_NV0, PV_V0)):
            ts(dv(dcol, 2), pv(pcol, 2), -1.0, None, ALU.mult, ALU.bypass, KPV, KDV)
        memset(dv(DV_EPSLN), LN_EPS, KDV)
        memset(dv(DV_EPSGN), GN_EPS, KDV)
        memset(dv(DV_EPSKK), 1e-12, KDV)

        P.set_fence()
        with ExitStack() as pha:
          if stage >= 1:
            QT = sb("QT", [128, 4, T], BF16, pha)
            KD = sb("KD", [128, 2, T], BF16, pha)
            VT = sb("VT", [128, 16, 128], BF16, pha)
            with ExitStack() as ph:
                XBs = [sb("XB", [128, T + 4], F32, ph)] * 2
                XCs = [sb("XC", [128, T], F32, ph)] * 2
                XCBs = [sb("XCB", [128, T], BF16, ph)] * 2
                BRs = [sb("BR", [128, TB], F32, ph)] * 2
                BIs = [sb("BI", [128, TB], F32, ph)] * 2
                BMs = [sb("BM", [128, TB], F32, ph)] * 2
                BHs = [[sb("BH%d" % i, [128, TB], F32, ph) for i in range(2)]] * 2
                memset(XBs[0].t[:, 0:4], 0.0, [XBs[0].k("pad")])
                for cb in range(2):
                    g = gbase + 0 + cb
                    for tb in range(NTB):
                        b = next_bank()
                        proj(g, tb, b)
                        act(Y.t[:, cb, tb * TB:(tb + 1) * TB], PS.t[:, b, :], AF.Silu, psk(b), [Y.k(cb, tb)])
                    wdone(g)
                def gen_B(cb):
                    XB, XC, XCB, BR, BI, BM, BH = XBs[cb], XCs[cb], XCBs[cb], BRs[cb], BIs[cb], BMs[cb], BHs[cb]
                    gb = 2 + 2 * cb
                    for tb in range(NTB):
                        rk = [XB.k(tb)] + ([XB.k(tb - 1)] if tb > 0 else [XB.k("pad")])
                        o = XC.t[:, tb * TB:(tb + 1) * TB]
                        ts(o, XB.t[:, 4 + tb * TB:4 + (tb + 1) * TB], pv(PV_CONVW + 3 * 2 + cb), pv(PV_CONVB + cb),
                           ALU.mult, ALU.add, rk + KPV, [XC.k(tb)])
                        for j in range(3):
                            s0 = 4 + tb * TB - 3 + j
                            stt(o, XB.t[:, s0:s0 + TB], pv(PV_CONVW + j * 2 + cb), o, ALU.mult, ALU.add,
                                rk + KPV + [XC.k(tb)], [XC.k(tb)])
                        cp(XCB.t[:, tb * TB:(tb + 1) * TB], o, [XC.k(tb)], [XCB.k(tb)], eng="pool")
                        yield
                    for tb in range(NTB):
                        sl = slice(tb * TB, (tb + 1) * TB)
                        ca = SW_LRU + (0 * 2 + cb) * 128
                        cx = SW_LRU + (1 * 2 + cb) * 128
                        mm(PS.t[:, gb, :], SMW.t[:, ca:ca + 128], XCB.t[:, sl], True, True, KSW + [XCB.k(tb)], psk(gb))
                        mm(PS.t[:, gb + 1, :], SMW.t[:, cx:cx + 128], XCB.t[:, sl], True, True, KSW + [XCB.k(tb)],
                           psk(gb + 1))
                        sigm(BR.t[:, :], PS.t[:, gb, :], dv(DV_NBA + cb), psk(gb) + KDV, [BR.k()])
                        yield
                        sigm(BI.t[:, :], PS.t[:, gb + 1, :], dv(DV_NBX + cb), psk(gb + 1) + KDV, [BI.k()])
                        yield
                        act(BM.t[:, :], BR.t[:, :], AF.Exp, [BR.k()] + KDV, [BM.k()], scale=dv(DV_LC2 + cb))
                        act(BR.t[:, :], BR.t[:, :], AF.Exp, [BR.k()] + KDV, [BR.k()], scale=dv(DV_LC + cb))
                        yield
                        act(BM.t[:, :], BM.t[:, :], AF.Ln, [BM.k()] + KCF, [BM.k()], scale=-1.0, bias=one_ap)
                        act(BM.t[:, :], BM.t[:, :], AF.Exp, [BM.k()], [BM.k()], scale=0.5)
                        tt(BI.t[:, :], BI.t[:, :], XC.t[:, sl], ALU.mult, [BI.k(), XC.k(tb)], [BI.k()])
                        yield
                        tt(BI.t[:, :], BI.t[:, :], BM.t[:, :], ALU.mult, [BI.k(), BM.k()], [BI.k()])
                        hcur, hprev = BH[tb % 2], BH[(tb + 1) % 2]
                        init = 0.0 if tb == 0 else hprev.t[:, TB - 1:TB]
                        scan(hcur.t[:, :], BR.t[:, :], BI.t[:, :], init,
                             [BR.k(), BI.k()] + ([hprev.k()] if tb > 0 else []), [hcur.k()])
                        yv = Y.t[:, cb, sl]
                        tt(yv, yv, hcur.t[:, :], ALU.mult, [hcur.k(), Y.k(cb, tb)], [Y.k(cb, tb)])
                        yield

                for cb in range(2):
                    g = gbase + 2 + cb
                    XB = XBs[cb]
                    for tb in range(NTB):
                        b = next_bank()
                        proj(g, tb, b)
                        cp(XB.t[:, 4 + tb * TB:4 + (tb + 1) * TB], PS.t[:, b, :], psk(b), [XB.k(tb)],
                           eng="dve" if tb % 2 == 0 else "act")
                    wdone(g)
                    for _ in gen_B(cb):
                        pass

                for c in range(4):
                    g = gbase + 4 + c
                    for tb in range(NTB):
                        b = next_bank()
                        proj(g, tb, b)
                        cp(QT.t[:, c, tb * TB:(tb + 1) * TB], PS.t[:, b, :], psk(b), [QT.k(c, tb)],
                           eng="act" if tb % 2 == 0 else "dve")
                    wdone(g)
                for gi in range(2):
                    g = gbase + 8 + gi
                    for tb in range(NTB):
                        b = next_bank()
                        proj(g, tb, b)
                        cp(KD.t[:, gi, tb * TB:(tb + 1) * TB], PS.t[:, b, :], psk(b), [KD.k(gi, tb)],
                           eng="act" if tb % 2 == 0 else "dve")
                    wdone(g)
                g = gbase + 10
                wap, wk = wslot(g)
                for tb in range(NTB):
                    b = next_bank()
                    for nn in range(4):
                        n = tb * 4 + nn
                        for kc in range(KC):
                            mm(PS.t[:, b, nn * 128:(nn + 1) * 128], XBF.t[:, kc, n * 128:(n + 1) * 128], wap[:, kc, :],
                               kc == 0, kc == KC - 1, [wk, XBF.k(kc, tb)], psk(b))
                    cp(VT.t[:, tb * 4:(tb + 1) * 4, :], r3(PS.t[:, b, :], 128), psk(b), [VT.k(tb)], eng="act")
                wdone(g)
            P.set_fence()
            for j in (range(2) if stage >= 2 else ()):
                g = gbase + 11 + j
                wap, wk = wslot(g)
                for mm_ in range(4):
                    m = 4 * j + mm_
                    for tb in range(NTB):
                        b = next_bank()
                        for kc in range(2):
                            mm(PS.t[:, b, :], wap[:, kc * 4 + mm_, :], Y.t[:, kc, tb * TB:(tb + 1) * TB],
                               kc == 0, kc == 1, [wk, Y.k(kc, tb)], psk(b))
                        xs = X32.t[:, m, tb * TB:(tb + 1) * TB]
                        stt(xs, xs, ALPHA, PS.t[:, b, :], ALU.mult, ALU.add, psk(b) + [X32.k(m, tb)], [X32.k(m, tb)])
                wdone(g)
            with ExitStack() as ph:
                PT = sb("PT", [128, 2, 2, 1024], BF16, ph)
                A1 = sb("A1", [128, 512], F32, ph)
                A2 = sb("A2", [128, 512], F32, ph)
                A3 = sb("A3", [128, 512], F32, ph)
                for c in range(4):
                    g = gbase + 13 + c
                    for tb in range(NTB):
                        b = next_bank()
                        proj(g, tb, b)
                        act(Y.t[:, c, tb * TB:(tb + 1) * TB], PS.t[:, b, :], AF.Silu, psk(b), [Y.k(c, tb)])
                    wdone(g)
                def attn_scores(n):
                    tbq = n // 4
                    jlist = [(0, n - 1), (1, n)] if n > 0 else [(1, n)]
                    for jj, kb in jlist:
                        tbk = kb // 4
                        mcol = CB_MP if jj == 0 else CB_MC
                        for par in range(2):
                            bank = 2 + jj * 2 + par
                            mm(PS.t[:, bank, :], ident, CBT.t[:, mcol:mcol + 512], True, False, KCB, psk(bank))
                        for gq in range(2):
                            for cg in range(2):
                                for par in range(2):
                                    bank = 2 + jj * 2 + par
                                    c = gq * 2 + cg
                                    pos = gq * 2 + cg
                                    mm(PS.t[:, bank, pos * 128:(pos + 1) * 128],
                                       KD.t[par * 64:(par + 1) * 64, gq, kb * 128:(kb + 1) * 128],
                                       QT.t[par * 64:(par + 1) * 64, c, n * 128:(n + 1) * 128],
                                       False, (gq == 1 and cg == 1), [KD.k(gq, tbk), QT.k(c, tbq)], psk(bank))
                        b0 = 2 + jj * 2
                        act(PT.t[:, n % 2, jj, :], PS.t[:, b0:b0 + 2, :].rearrange("p a b -> p (a b)"), AF.Exp,
                            psk(b0, b0 + 1), [PT.k(n % 2, jj)], scale=0.125)

                def attn_pv(n):
                    tbq = n // 4
                    jlist = [(0, n - 1), (1, n)] if n > 0 else [(1, n)]
                    for (obank, use_ones) in ((6, False), (7, True)):
                        for gq in range(2):
                            for par in range(2):
                                for ji, (jj, kb) in enumerate(jlist):
                                    if use_ones:
                                        lhsT, rk = ones_bf64, KCB
                                    else:
                                        lhsT, rk = VT.t[:, kb, gq * 64:(gq + 1) * 64], [VT.k(kb // 4)]
                                    c0 = par * 512 + gq * 256
                                    mm(PS.t[par * 64:(par + 1) * 64, obank, gq * 256:(gq + 1) * 256],
                                       lhsT, PT.t[:, n % 2, jj, c0:c0 + 256],
                                       ji == 0, ji == len(jlist) - 1, rk + [PT.k(n % 2, jj)], psk(obank),
                                       tp=(0, 64 * par))
                    tt(r3(A1.t[:, :], 128), r3(PS.t[:, 7, :], 128),
                       dv(DV_SE, 4).unsqueeze(2).to_broadcast([128, 4, 128]), ALU.add, psk(7) + KDV, [A1.k()])
                    rpow(A2.t[:, :], A1.t[:, :], -1.0, [A1.k()], [A2.k()])
                    tt(A3.t[:, :], PS.t[:, 6, :], A2.t[:, :], ALU.mult, psk(6) + [A2.k()], [A3.k()])
                    yv = Y.t[:, 0:4, n * 128:(n + 1) * 128]
                    yk = [Y.k(c, tbq) for c in range(4)]
                    tt(yv, yv, r3(A3.t[:, :], 128), ALU.mult, [A3.k()] + yk, yk)

                attn_scores(0)
                for n in range(16):
                    if n + 1 < 16:
                        attn_scores(n + 1)
                    attn_pv(n)
        for j in (range(4) if stage >= 2 else ()):
            g = gbase + 17 + j
            wap, wk = wslot(g)
            for mm_ in range(2):
                m = 2 * j + mm_
                for tb in range(NTB):
                    b = next_bank()
                    for kc in range(4):
                        mm(PS.t[:, b, :], wap[:, kc * 2 + mm_, :], Y.t[:, kc, tb * TB:(tb + 1) * TB],
                           kc == 0, kc == 3, [wk, Y.k(kc, tb)], psk(b))
                    xs = X32.t[:, m, tb * TB:(tb + 1) * TB]
                    tt(xs, xs, PS.t[:, b, :], ALU.add, psk(b) + [X32.k(m, tb)], [X32.k(m, tb)])
            wdone(g)


        P.set_fence()
        with ExitStack() as ph:
          if stage >= 4:
            def f32t(name):
                return sb(name, [128, SB], F32, ph)

            def bft(name, shape):
                return sb(name, shape, BF16, ph)

            two = range(2)
            ET = [{n: f32t("c%s%d" % (n, h)) for n in ("TMP", "RS", "KS", "SG", "GG", "AA", "EG", "ENG", "EGM")}
                  for h in two]
            PTMP, PT1 = f32t("cPTMP"), f32t("cPT1")
            GNT = [{"YS": ET[h]["SG"], "YQ": ET[h]["GG"], "MEAN": ET[h]["AA"], "VAR": ET[h]["TMP"]} for h in two]
            VS = [f32t("cVS0"), f32t("cVS1")]
            XSW = sb("cXSW", [64, SB], F32, ph)
            TWAL = bft("cTWAL", [64, SB])
            VBp = [bft("cVB%d" % i, [128, 2, SB]) for i in two]
            VSB = bft("cVSB", [128, 2, SB])
            P1B = bft("cP1B", [16, SB])
            SQBh = [bft("cSQB%d" % h, [128, SB]) for h in two]
            RKBh = [bft("cRKB%d" % h, [128, SB]) for h in two]
            CAR = sb("cCAR", [128, 8], F32, ph)
            BONp = [[f32t("cBON%d%d" % (i, h)) for h in two] for i in two]
            ARp = [[bft("cAR%d%d" % (i, h), [128, NCH, 128]) for h in two] for i in two]
            BT = [bft("cBT%d" % h, [128, SB]) for h in two]
            KT = [bft("cKT%d" % h, [128, SB]) for h in two]
            BHt = [bft("cBHt%d" % h, [128, SB]) for h in two]
            KHt = [bft("cKHt%d" % h, [128, SB]) for h in two]
            NTMR = [bft("cNTMR%d" % h, [128, NCH, 128]) for h in two]
            MKT = [bft("cMKT%d" % h, [128, NCH, 128]) for h in two]
            NK = [[bft("cNK%d%d" % (h, i), [128, NCH, 64]) for i in two] for h in two]
            NTQ = [[bft("cNTQ%d%d" % (h, i), [128, NCH, 128]) for i in two] for h in two]
            QF = [bft("cQF%d" % h, [128, NCH, 64]) for h in two]
            TOK = [bft("cTOK%d" % h, [128, NCH, 4, 64]) for h in two]
            MVB = [bft("cMVB%d" % h, [128, NCH, 64]) for h in two]
            WT = [bft("cWT%d" % h, [128, NCH, 64]) for h in two]
            UT = [sb("cUT%d" % h, [128, NCH, 64], F32, ph) for h in two]
            UB = [bft("cUB%d" % h, [128, 64]) for h in two]
            HB = [[bft("cHB%d%d" % (h, i), [128, 64]) for i in two] for h in two]
            H32 = [sb("cH32%d" % h, [128, 64], F32, ph) for h in two]
            GCp = [[sb("cGC%d%d" % (i, h), [128, NCH], F32, ph) for h in two] for i in two]
            hpar = [0, 0]

            for hp in range(2):
                g = gbase + 21 + hp
                for tb in range(NTB):
                    b = next_bank()
                    proj(g, tb, b)
                    act(Y.t[:, 2 + hp, tb * TB:(tb + 1) * TB], PS.t[:, b, :], AF.Silu, psk(b), [Y.k(2 + hp, tb)])
                wdone(g)
            g_cw = gbase + 23
            g_v = [gbase + 24, gbase + 25]
            g_r = [gbase + 26, gbase + 28]
            g_k = [gbase + 27, gbase + 29]
            for hp in range(2):
                memset(H32[hp].t[:, :], 0.0, [H32[hp].k()])
                memset(HB[hp][0].t[:, :], 0.0, [HB[hp][0].k()])
            memset(CAR.t[:, :], 0.0, [CAR.k(c) for c in range(8)])

            def projC(g, sbi, bank, col0, rows=128):
                wap_, wk_ = wslot(g)
                tbx_ = (sbi * SB) // TB
                for kc in range(KC):
                    mm(PS.t[0:rows, bank, col0:col0 + SB], wap_[:, kc, 0:rows], XBF.t[:, kc, sbi * SB:(sbi + 1) * SB],
                       kc == 0, kc == KC - 1, [wk_, XBF.k(kc, tbx_)], psk(bank))

            def tshift(bank, col0, rows, mucol, ommcol, carcol, out, TMP):
                rs = slice(0, rows)
                src = PS.t[rs, bank, col0:col0 + SB]
                act(TMP.t[rs, :], src, AF.Identity, psk(bank) + KDV, [TMP.k()], scale=dv(ommcol, 1, rs))
                stt(out.t[rs, 1:SB], PS.t[rs, bank, col0:col0 + SB - 1], pv(mucol, 1, rs), TMP.t[rs, 1:SB],
                    ALU.mult, ALU.add, psk(bank) + KPV + [TMP.k()], [out.k()])
                stt(out.t[rs, 0:1], CAR.t[rs, carcol:carcol + 1], pv(mucol, 1, rs), TMP.t[rs, 0:1],
                    ALU.mult, ALU.add, [CAR.k(carcol)] + KPV + [TMP.k()], [out.k()])
                cp(CAR.t[rs, carcol:carcol + 1], PS.t[rs, bank, col0 + SB - 1:col0 + SB], psk(bank), [CAR.k(carcol)])

            rmaskb = CBT.t[:, CB_RM:CB_RM + 128].unsqueeze(1).to_broadcast([128, NCH, 128])
            masklb = CBT.t[:, CB_ML:CB_ML + 64].unsqueeze(1).to_broadcast([128, NCH, 64])
            ident2 = CBT.t[:, CB_ID2:CB_ID2 + 64]
            ident2b = ident2.unsqueeze(1).to_broadcast([128, NCH, 64])
            HS = [slice(0, 64), slice(64, 128)]

            def jh():
                for j in range(NCH):
                    for h in range(2):
                        yield j, h, HS[h], slice(j * CH, (j + 1) * CH)

            def prologue(sbi):
                tbx = (sbi * SB) // TB
                ssl = slice(sbi * SB, (sbi + 1) * SB)
                VB = VBp[sbi % 2]
                TMP, T1 = PTMP, PT1
                projC(g_cw, sbi, 6, 0, rows=64)
                tshift(6, 0, 64, PV_MUCW, DV_OMM + 6, 6, XSW, TMP)
                yield
                act(XSW.t[0:32, :], XSW.t[0:32, :], AF.Exp, [XSW.k()], [XSW.k()], scale=-2.0)
                act(XSW.t[0:32, :], XSW.t[0:32, :], AF.Ln, [XSW.k()] + KCF, [XSW.k()], bias=one_ap[0:32, :])
                act(XSW.t[0:32, :], XSW.t[0:32, :], AF.Exp, [XSW.k()], [XSW.k()], scale=-1.0)
                ts(TWAL.t[0:32, :], XSW.t[0:32, :], 2.0, -1.0, ALU.mult, ALU.add, [XSW.k()], [TWAL.k()])
                cp(TWAL.t[32:64, :], XSW.t[32:64, :], [XSW.k()], [TWAL.k()], eng="pool")
                yield
                for hp in range(2):
                    projC(g_v[hp], sbi, 7, hp * SB)
                    tshift(7, hp * SB, 128, PV_MU + 4 + hp, DV_OMM + 4 + hp, 4 + hp, VS[hp], TMP)
                    yield
                if lay == 0:
                    for hp in range(2):
                        cp(VB.t[:, hp, :], VS[hp].t[:, :], [VS[hp].k()], [VB.k(hp)], eng="pool")
                        cp(VF.t[:, hp, ssl], VS[hp].t[:, :], [VS[hp].k()], [VF.k(hp, tbx)], eng="pool")
                else:
                    for hp in range(2):
                        cp(VSB.t[:, hp, :], VS[hp].t[:, :], [VS[hp].k()], [VSB.k(hp)], eng="pool")
                    for hp in range(2):
                        mm(PS.t[0:16, 6, 256:512], SMW.t[:, SW_V1 + hp * 16:SW_V1 + hp * 16 + 16], VSB.t[:, hp, :],
                           hp == 0, hp == 1, KSW + [VSB.k(hp)], psk(6))
                    cp(P1B.t[:, :], PS.t[0:16, 6, 256:512], psk(6), [P1B.k()], eng="act")
                    for hp in range(2):
                        mm(PS.t[:, 6, 0:SB], SMW.t[0:16, SW_V2 + hp * 128:SW_V2 + hp * 128 + 128], P1B.t[0:16, :],
                           True, True, KSW + [P1B.k()], psk(6))
                        sigm(T1.t[:, :], PS.t[:, 6, 0:SB], dv(DV_NV0 + hp), psk(6) + KDV, [T1.k()])
                        tt(TMP.t[:, :], VF.t[:, hp, ssl], VS[hp].t[:, :], ALU.subtract,
                           [VF.k(hp, tbx), VS[hp].k()], [TMP.k()])
                        tt(TMP.t[:, :], TMP.t[:, :], T1.t[:, :], ALU.mult, [TMP.k(), T1.k()], [TMP.k()])
                        tt(VB.t[:, hp, :], VS[hp].t[:, :], TMP.t[:, :], ALU.add, [VS[hp].k(), TMP.k()], [VB.k(hp)])
                        yield

            def stage_E(sbi, hp):
                par = sbi % 2
                VB, AR, GC, BON = VBp[par], ARp[par], GCp[par], BONp[par]
                SQB, RKB = SQBh[hp], RKBh[hp]
                e_ = ET[hp]
                TMP, RS, KS, SG, GG, AA, EG, ENG, EGM = (e_[n] for n in
                                                         ("TMP", "RS", "KS", "SG", "GG", "AA", "EG", "ENG", "EGM"))
                RN, T1, BB, KK = TMP, TMP, SG, GG
                if True:
                    B0 = B1 = B2 = 6 + hp
                    projC(g_r[hp], sbi, B0, 0)
                    projC(g_k[hp], sbi, B0, SB)
                    tshift(B0, 0, 128, PV_MU + 0 + hp, DV_OMM + 0 + hp, 0 + hp, RS, TMP)
                    tshift(B0, SB, 128, PV_MU + 2 + hp, DV_OMM + 2 + hp, 2 + hp, KS, TMP)
                    yield
                    cz = SW_WA2 + hp * 128
                    mm(PS.t[:, B1, 0:SB], SMW.t[0:32, cz:cz + 128], TWAL.t[0:32, :], True, True, KSW + [TWAL.k()], psk(B1))
                    sigm(SG.t[:, :], PS.t[:, B1, 0:SB], dv(DV_NW0 + hp), psk(B1) + KDV, [SG.k()])
                    mm(PS.t[:, B2, 0:SB], SMW.t[32:64, cz:cz + 128], TWAL.t[32:64, :], True, True,
                       KSW + [TWAL.k()], psk(B2))
                    sigm(AA.t[:, :], PS.t[:, B2, 0:SB], dv(DV_NA0 + hp), psk(B2) + KDV, [AA.k()])
                    yield
                    scan(GG.t[:, :], CBT.t[:, CB_CM:CB_CM + SB], SG.t[:, :], 0.0, KCB + [SG.k()], [GG.k()])
                    act(EG.t[:, :], GG.t[:, :], AF.Exp, [GG.k()], [EG.k()], scale=-CW)
                    act(ENG.t[:, :], GG.t[:, :], AF.Exp, [GG.k()], [ENG.k()], scale=CW)
                    yield
                    tt(EGM.t[:, :], GG.t[:, :], SG.t[:, :], ALU.subtract, [GG.k(), SG.k()], [EGM.k()])
                    act(EGM.t[:, :], EGM.t[:, :], AF.Exp, [EGM.k()], [EGM.k()], scale=-CW)
                    cp(GC[hp].t[:, :].unsqueeze(2), r3(EG.t[:, :], CH)[:, :, CH - 1:CH], [EG.k()], [GC[hp].k()])
                    yield
                    ts(KK.t[:, :], KS.t[:, :], pv(PV_KK + hp), None, ALU.mult, ALU.bypass, [KS.k()] + KPV, [KK.k()])
                    tt(SQB.t[:, :], KK.t[:, :], KK.t[:, :], ALU.mult, [KK.k()], [SQB.k()], eng="pool")
                    mm(PS.t[:, B1, SB:2 * SB], bones_bf, SQB.t[:, :], True, True, KCB + [SQB.k()], psk(B1))
                    rpow(RN.t[:, :], PS.t[:, B1, SB:2 * SB], -0.5, psk(B1) + KDV, [RN.k()], bias=dv(DV_EPSKK))
                    yield
                    tt(KK.t[:, :], KK.t[:, :], RN.t[:, :], ALU.mult, [KK.k(), RN.k()], [KK.k()])
                    ts(T1.t[:, :], AA.t[:, :], pv(PV_KA + hp), dv(DV_OKA + hp), ALU.mult, ALU.add,
                       [AA.k()] + KPV + KDV, [T1.k()])
                    tt(KS.t[:, :], KS.t[:, :], T1.t[:, :], ALU.mult, [KS.k(), T1.k()], [KS.k()])
                    tt(BB.t[:, :], KK.t[:, :], AA.t[:, :], ALU.mult, [KK.k(), AA.k()], [BB.k()])
                    yield
                    tt(T1.t[:, :], RS.t[:, :], KS.t[:, :], ALU.mult, [RS.k(), KS.k()], [T1.k()])
                    ts(RKB.t[:, :], T1.t[:, :], pv(PV_RK + hp), None, ALU.mult, ALU.bypass, [T1.k()] + KPV, [RKB.k()])
                    mm(PS.t[:, B2, SB:2 * SB], bones_bf, RKB.t[:, :], True, True, KCB + [RKB.k()], psk(B2))
                    tt(BON[hp].t[:, :], PS.t[:, B2, SB:2 * SB], VB.t[:, hp, :], ALU.mult, psk(B2) + [VB.k(hp)],
                       [BON[hp].k()])
                    yield
                    stt(AR[hp].t[:, :, 0:64], r3(KK.t[:, :], CH), -1.0, r3(EGM.t[:, :], CH), ALU.mult, ALU.mult,
                        [KK.k(), EGM.k()], [AR[hp].k()])
                    tt(AR[hp].t[:, :, 64:128], r3(RS.t[:, :], CH), r3(EG.t[:, :], CH), ALU.mult, [RS.k(), EG.k()],
                       [AR[hp].k()])
                    yield
                    tt(BT[hp].t[:, :], BB.t[:, :], ENG.t[:, :], ALU.mult, [BB.k(), ENG.k()], [BT[hp].k()])
                    tt(KT[hp].t[:, :], KS.t[:, :], ENG.t[:, :], ALU.mult, [KS.k(), ENG.k()], [KT[hp].k()])
                    yield
                    gcb = GC[hp].t[:, :].unsqueeze(2).to_broadcast([128, NCH, CH])
                    tt(r3(BHt[hp].t[:, :], CH), r3(BT[hp].t[:, :], CH), gcb, ALU.mult, [BT[hp].k(), GC[hp].k()],
                       [BHt[hp].k()])
                    tt(r3(KHt[hp].t[:, :], CH), r3(KT[hp].t[:, :], CH), gcb, ALU.mult, [KT[hp].k(), GC[hp].k()],
                       [KHt[hp].k()])

            def stage_P2(sbi, hp):
                par = sbi % 2
                VB, AR, GC, BON = VBp[par], ARp[par], GCp[par], BONp[par]
                for j in range(NCH):
                    cs = slice(j * CH, (j + 1) * CH)
                    for q in range(4):
                        for h in range(2):
                            hs = HS[h]
                            src, sk = ((AR[hp].t[hs, j, 0:64], AR[hp].k()), (BHt[hp].t[hs, cs], BHt[hp].k()),
                                       (KHt[hp].t[hs, cs], KHt[hp].k()), (VB.t[hs, hp, cs], VB.k(hp)))[q]
                            bank = hp * 3 + j // 2
                            col = ((j % 2) * 4 + q) * 64
                            mm(PS.t[hs, bank, col:col + 64], src, ident2[hs, :], True, True, [sk] + KCB, psk(bank))
                for a_ in range(2):
                    bank = hp * 3 + a_
                    cp(TOK[hp].t[:, 2 * a_:2 * a_ + 2, :, :],
                       PS.t[:, bank, :].rearrange("p (a b c) -> p a b c", b=4, c=64), psk(bank), [TOK[hp].k()],
                       eng="act")

            def stage_P1(sbi, hp):
                par = sbi % 2
                VB, AR, GC, BON = VBp[par], ARp[par], GCp[par], BONp[par]
                B0, B1, B2 = hp * 3, hp * 3 + 1, hp * 3 + 2
                for j in range(NCH):
                    cs = slice(j * CH, (j + 1) * CH)
                    for h in range(2):
                        hs = HS[h]
                        mm(PS.t[hs, B0, j * 128:(j + 1) * 128], BT[hp].t[hs, cs], AR[hp].t[hs, j, :], True, True,
                           [BT[hp].k(), AR[hp].k()], psk(B0))
                    for h in range(2):
                        hs = HS[h]
                        mm(PS.t[hs, B1, j * 128:(j + 1) * 128], KT[hp].t[hs, cs], AR[hp].t[hs, j, :], True, True,
                           [KT[hp].k(), AR[hp].k()], psk(B1))
                    for h in range(2):
                        hs = HS[h]
                        mm(PS.t[hs, B2, j * 64:(j + 1) * 64], AR[hp].t[hs, j, 0:64], BT[hp].t[hs, cs], True, True,
                           [BT[hp].k(), AR[hp].k()], psk(B2))
                tt(NTMR[hp].t[:, :, :], r3(PS.t[:, B0, :], 128), rmaskb, ALU.mult, psk(B0) + KCB, [NTMR[hp].k()])
                tt(MKT[hp].t[:, :, :], r3(PS.t[:, B1, :], 128), rmaskb, ALU.mult, psk(B1) + KCB, [MKT[hp].k()])
                tt(NK[hp][0].t[:, :, :], r3(PS.t[:, B2, 0:SB], 64), masklb, ALU.mult, psk(B2) + KCB,
                   [NK[hp][0].k()])

            def stage_P3(sbi, hp):
                par = sbi % 2
                VB, AR, GC, BON = VBp[par], ARp[par], GCp[par], BONp[par]
                B2 = hp * 3 + 2
                for j, h, hs, cs in jh():
                    mm(PS.t[hs, B2, SB + j * 64:SB + (j + 1) * 64], MKT[hp].t[hs, j, 0:64], TOK[hp].t[hs, j, 3, :],
                       True, True, [MKT[hp].k(), TOK[hp].k()], psk(B2))
                cp(MVB[hp].t[:, :, :], r3(PS.t[:, B2, SB:2 * SB], 64), psk(B2), [MVB[hp].k()], eng="act")

            def stage_P4(sbi, hp, r):
                nk, ntq = NK[hp], NTQ[hp]
                B0, B1, B3 = hp * 3, hp * 3 + 1, hp * 3 + 2
                if r == 0:
                    for j in range(NCH):
                        for h in range(2):
                            hs = HS[h]
                            mm(PS.t[hs, B3, j * 64:(j + 1) * 64], NTMR[hp].t[hs, j, 0:64], nk[0].t[hs, j, :], True,
                               True, [NTMR[hp].k(), nk[0].k()], psk(B3))
                        for h in range(2):
                            hs = HS[h]
                            mm(PS.t[hs, B0, j * 128:j * 128 + 64], nk[0].t[hs, j, :], NTMR[hp].t[hs, j, 0:64], True,
                               True, [NTMR[hp].k(), nk[0].k()], psk(B0))
                    cp(nk[1].t[:, :, :], r3(PS.t[:, B3, 0:SB], 64), psk(B3), [nk[1].k()], eng="act")
                    cp(ntq[1].t[:, :, 0:64], r3(PS.t[:, B0, :], 128)[:, :, 0:64], psk(B0), [ntq[1].k()], eng="act")
                    tt(ntq[1].t[:, :, 64:128], NTMR[hp].t[:, :, 0:64], ident2b, ALU.add, [NTMR[hp].k()] + KCB,
                       [ntq[1].k()])
                elif r < 5:
                    cur, nxt = r % 2, 1 - (r % 2)
                    ca = 0 if r % 2 == 0 else SB
                    bb = B0 if r % 2 == 0 else B1
                    for j in range(NCH):
                        for h in range(2):
                            hs = HS[h]
                            mm(PS.t[hs, B3, ca + j * 64:ca + (j + 1) * 64], ntq[cur].t[hs, j, 0:64],
                               nk[cur].t[hs, j, :], True, True, [ntq[cur].k(), nk[cur].k()], psk(B3))
                        for h in range(2):
                            hs = HS[h]
                            if r < 4:
                                mm(PS.t[hs, bb, j * 128:(j + 1) * 128], nk[cur].t[hs, j, :], ntq[cur].t[hs, j, :],
                                   True, True, [ntq[cur].k(), nk[cur].k()], psk(bb))
                            else:
                                mm(PS.t[hs, bb, j * 128 + 64:(j + 1) * 128], nk[cur].t[hs, j, :],
                                   ntq[cur].t[hs, j, 64:128], True, True, [ntq[cur].k(), nk[cur].k()], psk(bb))
                    cp(nk[nxt].t[:, :, :], r3(PS.t[:, B3, ca:ca + SB], 64), psk(B3), [nk[nxt].k()], eng="act")
                    if r < 4:
                        cp(ntq[nxt].t[:, :, 0:64], r3(PS.t[:, bb, :], 128)[:, :, 0:64], psk(bb), [ntq[nxt].k()],
                           eng="act")
                    tt(ntq[nxt].t[:, :, 64:128], ntq[cur].t[:, :, 64:128], r3(PS.t[:, bb, :], 128)[:, :, 64:128],
                       ALU.add, [ntq[cur].k()] + psk(bb), [ntq[nxt].k()])
                else:
                    for j, h, hs, cs in jh():
                        mm(PS.t[hs, B3, j * 64:(j + 1) * 64], nk[1].t[hs, j, :], ntq[1].t[hs, j, 64:128], True, True,
                           [ntq[1].k(), nk[1].k()], psk(B3))
                    tt(QF[hp].t[:, :, :], ntq[1].t[:, :, 64:128], r3(PS.t[:, B3, 0:SB], 64), ALU.add,
                       [ntq[1].k()] + psk(B3), [QF[hp].k()])

            def stage_P5(sbi, hp):
                par = sbi % 2
                VB, AR, GC, BON = VBp[par], ARp[par], GCp[par], BONp[par]
                B0 = hp * 3
                for j in range(NCH):
                    for h in range(2):
                        hs = HS[h]
                        mm(PS.t[hs, B0, j * 64:(j + 1) * 64], TOK[hp].t[hs, j, 0, :], QF[hp].t[hs, j, :], True, True,
                           [TOK[hp].k(), QF[hp].k()], psk(B0))
                    for h in range(2):
                        hs = HS[h]
                        mm(PS.t[hs, B0, SB + j * 64:SB + (j + 1) * 64], QF[hp].t[hs, j, :], MVB[hp].t[hs, j, :],
                           True, True, [MVB[hp].k(), QF[hp].k()], psk(B0))
                cp(WT[hp].t[:, :, :], r3(PS.t[:, B0, 0:SB], 64), psk(B0), [WT[hp].k()], eng="dve")
                cp(UT[hp].t[:, :, :], r3(PS.t[:, B0, SB:2 * SB], 64), psk(B0), [UT[hp].k()], eng="dve")

            def chain_step(sbi, hp, j):
                par = sbi % 2
                VB, AR, GC, BON = VBp[par], ARp[par], GCp[par], BONp[par]
                B2 = hp * 3 + 2
                hcur = HB[hp][hpar[hp]]
                hnxt = HB[hp][1 - hpar[hp]]
                hpar[hp] = 1 - hpar[hp]
                tok, ub = TOK[hp], UB[hp]
                for h in range(2):
                    hs = HS[h]
                    mm(PS.t[hs, B2, 0:64], WT[hp].t[hs, j, :], hcur.t[hs, :], True, True, [WT[hp].k(), hcur.k()],
                       psk(B2))
                tt(ub.t[:, :], PS.t[:, B2, 0:64], UT[hp].t[:, j, :], ALU.add, psk(B2) + [UT[hp].k()], [ub.k()])
                for h in range(2):
                    hs = HS[h]
                    mm(PS.t[hs, B2, 64:128], tok.t[hs, j, 2, :], tok.t[hs, j, 3, :], True, False, [tok.k()], psk(B2))
                for h in range(2):
                    hs = HS[h]
                    mm(PS.t[hs, B2, 64:128], tok.t[hs, j, 1, :], ub.t[hs, :], False, True, [tok.k(), ub.k()], psk(B2))
                for h in range(2):
                    hs = HS[h]
                    yo = PS.t[hs, B2, 128 + j * 64:128 + (j + 1) * 64]
                    mm(yo, hcur.t[hs, :], AR[hp].t[hs, j, 64:128], True, False, [hcur.k(), AR[hp].k()], psk(B2))
                for h in range(2):
                    hs = HS[h]
                    yo = PS.t[hs, B2, 128 + j * 64:128 + (j + 1) * 64]
                    mm(yo, ub.t[hs, :], NTMR[hp].t[hs, j, 64:128], False, False, [ub.k(), NTMR[hp].k()], psk(B2))
                for h in range(2):
                    hs = HS[h]
                    yo = PS.t[hs, B2, 128 + j * 64:128 + (j + 1) * 64]
                    mm(yo, tok.t[hs, j, 3, :], MKT[hp].t[hs, j, 64:128], False, True, [tok.k(), MKT[hp].k()], psk(B2))
                stt(H32[hp].t[:, :], H32[hp].t[:, :], GC[hp].t[:, j:j + 1], PS.t[:, B2, 64:128], ALU.mult, ALU.add,
                    [H32[hp].k(), GC[hp].k()] + psk(B2), [H32[hp].k()])
                cp(hnxt.t[:, :], H32[hp].t[:, :], [H32[hp].k()], [hnxt.k()], eng="act")

            def stage_GN(sbi, hp):
                par = sbi % 2
                VB, AR, GC, BON = VBp[par], ARp[par], GCp[par], BONp[par]
                B1, B2 = hp * 3 + 1, hp * 3 + 2
                tbx = (sbi * SB) // TB
                ssl = slice(sbi * SB, (sbi + 1) * SB)
                YS, YQ, MEANc, VARc = (GNT[hp][n] for n in ("YS", "YQ", "MEAN", "VAR"))
                cp(YS.t[:, :], PS.t[:, B2, 128:128 + SB], psk(B2), [YS.k()], eng="act")
                act(YQ.t[:, :], PS.t[:, B2, 128:128 + SB], AF.Square, psk(B2), [YQ.k()])
                yield
                mm(PS.t[:, B1, 0:SB], bones32, YS.t[:, :], True, True, KCF + [YS.k()], psk(B1))
                mm(PS.t[:, B1, SB:2 * SB], bones32, YQ.t[:, :], True, True, KCF + [YQ.k()], psk(B1))
                act(MEANc.t[:, :], PS.t[:, B1, 0:SB], AF.Identity, psk(B1), [MEANc.k()], scale=1.0 / 64)
                act(VARc.t[:, :], PS.t[:, B1, 0:SB], AF.Square, psk(B1), [VARc.k()], scale=1.0 / 64)
                stt(VARc.t[:, :], PS.t[:, B1, SB:2 * SB], 1.0 / 64, VARc.t[:, :], ALU.mult, ALU.subtract,
                    psk(B1) + [VARc.k()], [VARc.k()])
                yield
                rpow(VARc.t[:, :], VARc.t[:, :], -0.5, [VARc.k()] + KDV, [VARc.k()], bias=dv(DV_EPSGN))
                yield
                tt(YS.t[:, :], YS.t[:, :], MEANc.t[:, :], ALU.subtract, [YS.k(), MEANc.k()], [YS.k()])
                tt(YS.t[:, :], YS.t[:, :], VARc.t[:, :], ALU.mult, [YS.k(), VARc.k()], [YS.k()])
                yield
                act(YS.t[:, :], YS.t[:, :], AF.Identity, [YS.k()] + KPV, [YS.k()],
                    scale=pv(PV_GNW + hp), bias=pv(PV_GNB + hp))
                tt(YS.t[:, :], YS.t[:, :], BON[hp].t[:, :], ALU.add, [YS.k(), BON[hp].k()], [YS.k()])
                yv = Y.t[:, 2 + hp, ssl]
                tt(yv, yv, YS.t[:, :], ALU.mult, [YS.k(), Y.k(2 + hp, tbx)], [Y.k(2 + hp, tbx)])

            def warm(n, bank=0):
                if n <= 0:
                    return

                def fn(e):
                    ins = None
                    for _ in range(n):
                        ins = e.matmul(PS.t[:, bank, :], lhsT=CBT.t[:, CB_ID:CB_ID + 128],
                                       rhs=CBT.t[:, CB_MP:CB_MP + 512], start=True, stop=True)
                    return ins
                P.add("pe", fn, KCB, psk(bank), cost=0.3 * n)

            def gen_Eall(sbi):
                for _ in prologue(sbi):
                    yield
                alive = [stage_E(sbi, 0), stage_E(sbi, 1)]
                while alive:
                    for g_ in list(alive):
                        try:
                            next(g_)
                        except StopIteration:
                            alive.remove(g_)
                        yield

            def gen_PC(sbi):
                warm(WARM_BURST)
                for hp in range(2):
                    stage_P2(sbi, hp)
                    yield
                for hp in range(2):
                    stage_P1(sbi, hp)
                    yield
                for hp in range(2):
                    stage_P3(sbi, hp)
                    yield
                for r in range(6):
                    for hp in range(2):
                        stage_P4(sbi, hp, r)
                        warm(WARM_P4)
                        yield
                for hp in range(2):
                    stage_P5(sbi, hp)
                    yield
                for j in range(NCH):
                    for hp in range(2):
                        chain_step(sbi, hp, j)
                        warm(WARM_CH)
                        yield

            def run_interleaved(*gens):
                alive = list(gens)
                while alive:
                    for g_ in list(alive):
                        try:
                            next(g_)
                        except StopIteration:
                            alive.remove(g_)

            def gen_GN(sbi):
                alive = [stage_GN(sbi, 0), stage_GN(sbi, 1)]
                while alive:
                    for g_ in list(alive):
                        try:
                            next(g_)
                        except StopIteration:
                            alive.remove(g_)
                        yield

            run_interleaved(gen_Eall(0))
            gg = iter(())
            for sbi in range(NSB):
                ge = gen_Eall(sbi + 1) if sbi + 1 < NSB else iter(())
                for i_, _ in enumerate(gen_PC(sbi)):
                    if i_ < E_START:
                        for _k in range(3):
                            next(gg, None)
                    else:
                        if i_ == E_START:
                            run_interleaved(gg)
                        for _k in range(E_CHAIN if i_ >= 20 else 1):
                            next(ge, None)
                run_interleaved(ge)
                gg = gen_GN(sbi)
            run_interleaved(gg)
            for g in range(gbase + 23, gbase + 30):
                wdone(g)

        P.set_fence()
        with ExitStack() as ph:
          if stage >= 5:
            SQ = [sb("LSQ0", [128, TB], F32, ph), sb("LSQ1", [128, TB], F32, ph)]
            MEAN = sb("LMEAN", [128, TB], F32, ph)
            RSTD = sb("LRSTD", [128, TB], F32, ph)
            L1 = sb("L1", [128, TB], F32, ph)

            def outproj_tb(tb):
                for j in range(2):
                    wap, wk = wslot(gbase + 30 + j)
                    for mm_ in range(4):
                        m = 4 * j + mm_
                        b = next_bank()
                        for kc in range(2):
                            mm(PS.t[:, b, :], wap[:, kc * 4 + mm_, :], Y.t[:, 2 + kc, tb * TB:(tb + 1) * TB],
                               kc == 0, kc == 1, [wk, Y.k(2 + kc, tb)], psk(b))
                        xs = X32.t[:, m, tb * TB:(tb + 1) * TB]
                        tt(xs, xs, PS.t[:, b, :], ALU.add, psk(b) + [X32.k(m, tb)], [X32.k(m, tb)])

            def ln_tb(tb):
                sl = slice(tb * TB, (tb + 1) * TB)
                for m in range(KC):
                    mm(PS.t[:, 2, :], ones32, X32.t[:, m, sl], m == 0, m == KC - 1, KCF + [X32.k(m, tb)], psk(2))
                for m in range(KC):
                    sq = SQ[m % 2]
                    act(sq.t[:, :], X32.t[:, m, sl], AF.Square, [X32.k(m, tb)], [sq.k()])
                    mm(PS.t[:, 3, :], ones32, sq.t[:, :], m == 0, m == KC - 1, KCF + [sq.k()], psk(3))
                act(MEAN.t[:, :], PS.t[:, 2, :], AF.Identity, psk(2), [MEAN.k()], scale=1.0 / D)
                act(L1.t[:, :], PS.t[:, 2, :], AF.Square, psk(2), [L1.k()], scale=1.0 / D)
                stt(L1.t[:, :], PS.t[:, 3, :], 1.0 / D, L1.t[:, :], ALU.mult, ALU.subtract, psk(3) + [L1.k()], [L1.k()])
                rpow(RSTD.t[:, :], L1.t[:, :], -0.5, [L1.k()] + KDV, [RSTD.k()], bias=dv(DV_EPSLN))
                tt(MEAN.t[:, :], MEAN.t[:, :], RSTD.t[:, :], ALU.mult, [MEAN.k(), RSTD.k()], [MEAN.k()])
                for m in range(KC):
                    xs = X32.t[:, m, sl]
                    tt(xs, xs, RSTD.t[:, :], ALU.mult, [X32.k(m, tb), RSTD.k()], [X32.k(m, tb)])
                    tt(xs, xs, MEAN.t[:, :], ALU.subtract, [X32.k(m, tb), MEAN.k()], [X32.k(m, tb)])
                    act(xs, xs, AF.Identity, [X32.k(m, tb)] + KPV, [X32.k(m, tb)],
                        scale=pv(PV_LNG + m), bias=pv(PV_LNB + m))
                    if li < NLY - 1:
                        cp(XBF.t[:, m, sl], xs, [X32.k(m, tb)], [XBF.k(m, tb)], eng="act")

            outproj_tb(0)
            for tb in range(NTB):
                if tb + 1 < NTB:
                    outproj_tb(tb + 1)
                ln_tb(tb)
            for j in range(2):
                wdone(gbase + 30 + j)

    outk = []
    for kc in range(KC):
        for tb in range(NTB):
            k = ("out", kc, tb)
            dma(y_out[kc * 128:(kc + 1) * 128, tb * TB:(tb + 1) * TB], X32.t[:, kc, tb * TB:(tb + 1) * TB],
                [X32.k(kc, tb)], [k])
            outk.append(k)
    P.add("sp", None, outk, (), cost=0.01)

    esems = {e: es.enter_context(nc.semaphore("sem_" + e)) for e in Prog.ENG}
    dsems = [es.enter_context(nc.semaphore("dsem%d" % i)) for i in range(NDSEM)]
    block = es.enter_context(nc.Block())
    P.emit(nc, block, esems, dsems)
    es.close()
    return nc, P


_CACHE = {}


def _get_program(layers):
    key = tuple(layers)
    if key not in _CACHE:
        _CACHE[key] = build(list(layers))
    return _CACHE[key]


def run_layers(x, inp, layers):
    ws = _prep_weights(inp)
    pvs, sws = _prep_small(inp)
    cb, cf = _consts()
    ls = list(layers)
    ws = np.ascontiguousarray(ws[ls])
    pvs = np.ascontiguousarray(pvs[ls])
    sws = np.ascontiguousarray(sws[ls])
    nc, _ = _get_program(ls)
    in_maps = []
    for b in range(8):
        in_maps.append({"xT": np.ascontiguousarray(x[b].T), "wst": ws, "pv": pvs, "sw": sws, "cb": cb, "cf": cf})
    res = run_bass_kernel_spmd(nc, in_maps, core_ids=list(range(8)))
    out = np.stack([np.asarray(res.results[b]["yT"]).T for b in range(8)])
    return np.ascontiguousarray(out.astype(np.float32))


def kernel(**inputs):
    x = np.asarray(inputs["x"], np.float32)
    return run_layers(x, inputs, range(DEPTH))
```

```python
import math
from contextlib import ExitStack

import numpy as np
import concourse.bass as bass
import concourse.mybir as mybir
from concourse.bass_utils import run_bass_kernel_spmd

F32 = mybir.dt.float32
BF16 = mybir.dt.bfloat16
AF = mybir.ActivationFunctionType
ALU = mybir.AluOpType

T = 2048
TB = 512
NTB = T // TB
D = 1024
KC = 8
DEPTH = 4
CH = 64
SB = 256
NCH = SB // CH
NSB = T // SB
NSLOT = 8
NWCH = 32
ALPHA = (2 * DEPTH) ** 0.25
LN_EPS = 1e-5
GN_EPS = 64e-5
CW = math.exp(-0.5)
NEG = -30000.0
NDSEM = 40

PV_LNG, PV_LNB, PV_SINK, PV_CONVW, PV_CONVB, PV_BA, PV_BX, PV_LAM = 0, 8, 16, 20, 28, 30, 32, 34
PV_MU, PV_MUCW, PV_W0, PV_A0, PV_KK, PV_KA, PV_RK, PV_GNW, PV_GNB, PV_V0 = 36, 42, 43, 45, 47, 49, 51, 53, 55, 57
NPV = 64
DV_OMM, DV_SE, DV_LC, DV_LC2, DV_OKA = 0, 7, 11, 13, 15
DV_NBA, DV_NBX, DV_NW0, DV_NA0, DV_NV0, DV_EPSLN, DV_EPSGN, DV_EPSKK = 17, 19, 21, 23, 25, 27, 28, 29
NDV = 32
SW_LRU, SW_WA2, SW_V1, SW_V2, NSW = 0, 512, 768, 800, 1056
CB_ID, CB_ID2, CB_MP, CB_MC, CB_RM, CB_ML, CB_BO, CB_CM, CB_ON, NCB = 0, 128, 192, 704, 1216, 1344, 1408, 1536, 2048, 2112


class Op:
    __slots__ = ("idx", "eng", "fn", "dma", "deps", "odeps", "signal", "semval", "dsem", "dval", "dguard",
                 "cost", "start", "finish", "pos", "barrier")


SCHED = True
PRIO_BLEV = True
BLEV_PE_W = 0.55
CSCALE = {"pe": 1.0, "act": 1.0, "dve": 1.0, "pool": 1.0, "sp": 1.0}
XLAT = 0.5


class Prog:
    ENG = ("pe", "act", "dve", "pool", "sp")

    def __init__(self):
        self.ops = []
        self.lastw = {}
        self.readers = {}
        self.fence = {}
        self.since_fence = {e: [] for e in self.ENG}

    def add(self, eng, fn, r=(), w=(), dma=False, cost=0.3):
        op = Op()
        op.idx = len(self.ops)
        op.eng = eng
        op.fn = fn
        op.dma = dma
        op.signal = False
        op.semval = 0
        op.cost = cost
        op.barrier = False
        deps = {}
        ps_r = [k for k in r if k[0] == "PS"]
        if ps_r:
            w = list(w) + [k for k in ps_r if k not in w]

        def dep(d):
            if d is not None:
                deps[d.idx] = d

        for k in r:
            dep(self.lastw.get(k))
        for k in w:
            if k not in self.lastw and k not in self.readers:
                for d in self.fence.values():
                    dep(d)
            dep(self.lastw.get(k))
            for d in self.readers.get(k, ()):
                dep(d)
        deps.pop(op.idx, None)
        op.odeps = list(deps.values())
        op.deps = [d for d in op.odeps if d.dma or dma or not (d.eng == eng and eng == "pe")]
        for k in w:
            self.lastw[k] = op
            self.readers[k] = []
        for k in r:
            self.readers.setdefault(k, []).append(op)
        self.ops.append(op)
        if not dma and fn is not None:
            for k in list(r) + list(w):
                if isinstance(k[0], str) and "_L" in k[0]:
                    self.since_fence[eng].append(op)
                    break
        return op

    def set_fence(self):
        for e in self.ENG:
            prev = self.since_fence[e]
            if not prev:
                continue
            b = Op()
            b.idx = len(self.ops)
            b.eng = e
            b.fn = None
            b.dma = False
            b.signal = False
            b.semval = 0
            b.cost = 0.0
            b.barrier = True
            b.odeps = list(prev)
            b.deps = []
            self.ops.append(b)
            self.fence[e] = b
            self.since_fence[e] = [b]

    def schedule(self):
        ops = self.ops
        per_eng = {e: [] for e in self.ENG}
        if not SCHED:
            for op in ops:
                op.start = float(op.idx)
                op.pos = len(per_eng[op.eng])
                per_eng[op.eng].append(op)
            return per_eng, list(ops)
        succ = [[] for _ in ops]
        indeg = [0] * len(ops)
        cst = [op.cost * (1.0 if op.dma else CSCALE[op.eng]) for op in ops]
        for op in ops:
            indeg[op.idx] = len(op.odeps)
            for d in op.odeps:
                succ[d.idx].append(op)
        blev = [0.0] * len(ops)
        for op in reversed(ops):
            b_ = 0.0
            for s_ in succ[op.idx]:
                v = blev[s_.idx] + (0.0 if (s_.eng == op.eng and not op.dma) else XLAT)
                if v > b_:
                    b_ = v
            blev[op.idx] = b_ + cst[op.idx] * (BLEV_PE_W if op.eng == "pe" else 1.0)
        ready = {e: [] for e in self.ENG}
        eng_t = {e: 0.0 for e in self.ENG}

        def push(op):
            rt = 0.0
            for d in op.odeps:
                t = d.finish + (0.0 if (d.eng == op.eng and not d.dma) else XLAT)
                if t > rt:
                    rt = t
            ready[op.eng].append((rt, (-blev[op.idx] if PRIO_BLEV else op.idx), op))

        for op in ops:
            if indeg[op.idx] == 0:
                push(op)
        order = []
        for _ in range(len(ops)):
            best = None
            for e in self.ENG:
                rl = ready[e]
                if not rl:
                    continue
                te = eng_t[e]
                c = min(rl, key=lambda x: (x[0] if x[0] > te else te, x[1]))
                st = c[0] if c[0] > te else te
                if best is None or (st, c[1]) < (best[0], best[1][1]):
                    best = (st, c, e)
            st, c, e = best
            ready[e].remove(c)
            op = c[2]
            op.start = st
            if op.dma:
                eng_t[e] = st + 0.06
                op.finish = st + op.cost
            else:
                op.finish = st + cst[op.idx]
                eng_t[e] = op.finish
            op.pos = len(per_eng[e])
            per_eng[e].append(op)
            order.append(op)
            for s_ in succ[op.idx]:
                indeg[s_.idx] -= 1
                if indeg[s_.idx] == 0:
                    push(s_)
        self.est_us = max(eng_t.values())
        return per_eng, order

    def emit(self, nc, block, esems, dsems):
        per_eng, order = self.schedule()
        def real_before(d):
            lst = per_eng[d.eng]
            i = d.pos - 1
            while i >= 0:
                c = lst[i]
                if c.fn is not None and not c.dma:
                    return c
                i -= 1
            return None

        for op in order:
            if op.barrier:
                op.deps = []
                continue
            lastdep = {}
            keep = []
            for d in op.deps:
                if d.barrier:
                    d = real_before(d)
                    if d is None:
                        continue
                if d.dma:
                    keep.append(d)
                else:
                    cur = lastdep.get(d.eng)
                    if cur is None or d.pos > cur.pos:
                        lastdep[d.eng] = d
            for d in lastdep.values():
                if d.eng == op.eng and d.pos > op.pos:
                    raise RuntimeError("scheduler order violation")
                d.signal = True
                keep.append(d)
            op.deps = keep
        cnt = {e: 0 for e in self.ENG}
        dval = [0] * len(dsems)
        rrs = {"pool": 0, "sp": 0}
        half = len(dsems) // 2
        for op in order:
            if op.dma:
                if op.eng == "pool":
                    s = rrs["pool"] % half
                    rrs["pool"] += 1
                else:
                    s = half + rrs["sp"] % (len(dsems) - half)
                    rrs["sp"] += 1
                op.dsem = s
                op.dguard = dval[s]
                dval[s] += 16
                op.dval = dval[s]
        for e in self.ENG:
            for op in per_eng[e]:
                if (not op.dma) and op.signal:
                    cnt[e] += 1
                    op.semval = cnt[e]
        self.stats = dict(cnt)

        def run(eng_name, e):
            seen = {}

            def wait(key, sem, val):
                if val <= 0 or seen.get(key, 0) >= val:
                    return
                seen[key] = val
                e.wait_ge(sem, val)

            for op in per_eng[eng_name]:
                for d in op.deps:
                    if d.dma:
                        wait(("d", d.dsem), dsems[d.dsem], d.dval)
                    else:
                        wait(("e", d.eng), esems[d.eng], d.semval)
                if op.dma:
                    wait(("d", op.dsem), dsems[op.dsem], op.dguard)
                if op.fn is None:
                    continue
                ins = op.fn(e)
                if op.dma:
                    ins.then_inc(dsems[op.dsem], 16)
                elif op.signal:
                    ins.then_inc(esems[op.eng], 1)

        @block.sync
        def _(e):
            run("sp", e)

        @block.gpsimd
        def _(e):
            run("pool", e)

        @block.tensor
        def _(e):
            run("pe", e)

        @block.scalar
        def _(e):
            run("act", e)

        @block.vector
        def _(e):
            run("dve", e)


class Tile:
    def __init__(self, t, name):
        self.t = t
        self.name = name

    def k(self, *idx):
        return (self.name,) + tuple(idx)


def _chunk_cols(w, cols):
    out = np.zeros((KC, 128, 128), np.float32)
    out[:, :, : len(cols)] = w[:, cols].reshape(KC, 128, len(cols))
    return np.ascontiguousarray(out.transpose(1, 0, 2))


def _prep_weights(inp):
    w_in = np.asarray(inp["w_in"], np.float32)
    w_out = np.asarray(inp["w_out"], np.float32)
    ws = np.zeros((DEPTH, NWCH, 128, KC, 128), np.float32)
    ar = np.arange
    for l in range(DEPTH):
        ch = []
        for cb in range(2):
            ch.append(_chunk_cols(w_in[l], 1536 + 128 * cb + ar(128)))
        for cb in range(2):
            ch.append(_chunk_cols(w_in[l], 1280 + 128 * cb + ar(128)))
        for c in range(4):
            ch.append(_chunk_cols(w_in[l], 128 * c + ar(128)))
        for g in range(2):
            kc_ = 512 + 64 * g + ar(64)
            ch.append(_chunk_cols(w_in[l], np.concatenate([kc_, kc_])))
        ch.append(_chunk_cols(w_in[l], 640 + ar(128)))

        def outp2(row0):
            res = []
            for j in range(2):
                o = np.zeros((128, KC, 128), np.float32)
                for kc in range(2):
                    for mm in range(4):
                        m = 4 * j + mm
                        o[:, kc * 4 + mm, :] = w_out[l][row0 + kc * 128:row0 + (kc + 1) * 128, m * 128:(m + 1) * 128]
                res.append(o)
            return res

        ch += outp2(512)
        for c in range(4):
            ch.append(_chunk_cols(w_in[l], 768 + 128 * c + ar(128)))
        for j in range(4):
            o = np.zeros((128, KC, 128), np.float32)
            for kc in range(4):
                for mm in range(2):
                    m = 2 * j + mm
                    o[:, kc * 2 + mm, :] = w_out[l][kc * 128:(kc + 1) * 128, m * 128:(m + 1) * 128]
            ch.append(o)
        for hp in range(2):
            ch.append(_chunk_cols(w_in[l], 2624 + 128 * hp + ar(128)))
        ch.append(_chunk_cols(w_in[l], 2560 + ar(64)))
        for hp in range(2):
            ch.append(_chunk_cols(w_in[l], 2304 + 128 * hp + ar(128)))
        for hp in range(2):
            ch.append(_chunk_cols(w_in[l], 1792 + 128 * hp + ar(128)))
            ch.append(_chunk_cols(w_in[l], 2048 + 128 * hp + ar(128)))
        ch += outp2(768)
        assert len(ch) == NWCH
        ws[l] = np.stack(ch)
    return ws


def _prep_small(inp):
    f = lambda k: np.asarray(inp[k], np.float32)
    pv = np.zeros((DEPTH, 128, NPV), np.float32)
    sw = np.zeros((DEPTH, 128, NSW), np.float32)
    p = np.arange(128)
    for l in range(DEPTH):
        for kc in range(KC):
            pv[l, :, PV_LNG + kc] = f("ln_g")[l, kc * 128 + p]
            pv[l, :, PV_LNB + kc] = f("ln_b")[l, kc * 128 + p]
        for c in range(4):
            pv[l, :, PV_SINK + c] = f("attn_sinks")[l, 2 * c + p // 64]
        for j in range(4):
            for cb in range(2):
                pv[l, :, PV_CONVW + j * 2 + cb] = f("conv_w")[l, j, cb * 128 + p]
        for cb in range(2):
            pv[l, :, PV_CONVB + cb] = f("conv_b")[l, cb * 128 + p]
            pv[l, :, PV_BA + cb] = f("lru_ba")[l, cb * 128 + p]
            pv[l, :, PV_BX + cb] = f("lru_bx")[l, cb * 128 + p]
            pv[l, :, PV_LAM + cb] = f("lru_lambda")[l, cb * 128 + p]
        mu = f("rwkv_mu")[l]
        for q in range(3):
            for hp in range(2):
                pv[l, :, PV_MU + q * 2 + hp] = mu[q * 256 + hp * 128 + p]
        pv[l, :64, PV_MUCW] = mu[768:832]
        for hp in range(2):
            s = hp * 128 + p
            pv[l, :, PV_W0 + hp] = f("rwkv_w0")[l, s]
            pv[l, :, PV_A0 + hp] = f("rwkv_a0")[l, s]
            pv[l, :, PV_KK + hp] = f("rwkv_kk")[l, s]
            pv[l, :, PV_KA + hp] = f("rwkv_ka")[l, s]
            pv[l, :, PV_RK + hp] = f("rwkv_rk")[l].reshape(256)[s]
            pv[l, :, PV_GNW + hp] = f("rwkv_gn_w")[l, s]
            pv[l, :, PV_GNB + hp] = f("rwkv_gn_b")[l, s]
            if l > 0:
                pv[l, :, PV_V0 + hp] = f("rwkv_v0")[l - 1, s]
        wa, wx = f("lru_wa")[l], f("lru_wx")[l]
        for ax, wmat in enumerate((wa, wx)):
            for cb in range(2):
                for i in range(2):
                    c0 = SW_LRU + (ax * 2 + cb) * 128 + i * 64
                    sw[l, i * 64:(i + 1) * 64, c0:c0 + 64] = wmat[2 * cb + i]
        sw[l, 0:32, SW_WA2:SW_WA2 + 256] = f("rwkv_w2")[l]
        sw[l, 32:64, SW_WA2:SW_WA2 + 256] = f("rwkv_a2")[l]
        if l > 0:
            v1 = f("rwkv_v1")[l - 1]
            for hp in range(2):
                sw[l, :, SW_V1 + hp * 16:SW_V1 + hp * 16 + 16] = v1[hp * 128:(hp + 1) * 128, :]
            sw[l, 0:16, SW_V2:SW_V2 + 256] = f("rwkv_v2")[l - 1]
    return pv, sw


def _consts():
    cb = np.zeros((128, NCB), np.float32)
    p = np.arange(128)[:, None]
    q = np.arange(128)[None, :]
    cb[:, CB_ID:CB_ID + 128] = (p == q)
    cb[:, CB_ID2:CB_ID2 + 64] = ((p % 64) == np.arange(64)[None, :])
    mp = np.where(q < p, 0.0, NEG)
    mc = np.where(q >= p, 0.0, NEG)
    cb[:, CB_MP:CB_MP + 512] = np.tile(mp, (1, 4))
    cb[:, CB_MC:CB_MC + 512] = np.tile(mc, (1, 4))
    s = (np.arange(128) % 64)[:, None]
    t = np.arange(64)[None, :]
    cb[:, CB_RM:CB_RM + 64] = (t > s)
    cb[:, CB_RM + 64:CB_RM + 128] = (t >= s)
    cb[:, CB_ML:CB_ML + 64] = (t < s)
    cb[:, CB_BO:CB_BO + 128] = ((p // 64) == (q // 64))
    cb[:, CB_CM:CB_CM + 512] = ((np.arange(512) % 64) != 0)[None, :]
    cb[:, CB_ON:CB_ON + 64] = 1.0
    cf = np.zeros((128, 256), np.float32)
    cf[:, 0:128] = ((p // 64) == (q // 64))
    cf[:, 128:256] = 1.0
    return cb, cf


E_START, E_CHAIN = 6, 2
WARM_P4, WARM_CH = 0, 0
WARM_BURST = 0


def build(layers, stage=9):
    nc = bass.Bass("TRN2", target_bir_lowering=False)
    NLY = len(layers)
    x_in = nc.dram_tensor("xT", [D, T], F32, kind="ExternalInput").ap()
    wst = nc.dram_tensor("wst", [NLY, NWCH, 128, KC, 128], F32, kind="ExternalInput").ap()
    pv_d = nc.dram_tensor("pv", [NLY, 128, NPV], F32, kind="ExternalInput").ap()
    sw_d = nc.dram_tensor("sw", [NLY, 128, NSW], F32, kind="ExternalInput").ap()
    cb_d = nc.dram_tensor("cb", [128, NCB], F32, kind="ExternalInput").ap()
    cf_d = nc.dram_tensor("cf", [128, 256], F32, kind="ExternalInput").ap()
    y_out = nc.dram_tensor("yT", [D, T], F32, kind="ExternalOutput").ap()

    P = Prog()
    es = ExitStack()

    lyr = {"i": 0}

    def sb(name, shape, dt, stack=None):
        if stack is None:
            stack = es
        else:
            name = "%s_L%d" % (name, lyr["i"])
        return Tile(stack.enter_context(nc.sbuf_tensor(name, shape, dt)), name)

    X32 = sb("X32", [128, KC, T], F32)
    XBF = sb("XBF", [128, KC, T], BF16)
    Y = sb("Y", [128, 4, T], BF16)
    VF = sb("VF", [128, 2, T], BF16)
    WR = sb("WR", [128, NSLOT, KC, 128], BF16)
    PVT = sb("PVT", [128, NLY, NPV], F32)
    DVT = sb("DVT", [128, NDV], F32)
    SMW = sb("SMW", [128, NSW], BF16)
    CBT = sb("CBT", [128, NCB], BF16)
    CFT = sb("CFT", [128, 256], F32)
    PS = Tile(es.enter_context(nc.psum_tensor("PS", [128, 8, 512], F32)), "PS")

    ident = CBT.t[:, CB_ID:CB_ID + 128]
    bones_bf = CBT.t[:, CB_BO:CB_BO + 128]
    ones_bf64 = CBT.t[:, CB_ON:CB_ON + 64]
    bones32 = CFT.t[:, 0:128]
    ones32 = CFT.t[:, 128:256]
    KCB = [CBT.k()]
    KCF = [CFT.k()]
    KPV = [PVT.k()]
    KDV = [DVT.k()]
    KSW = [SMW.k()]

    def fs(ap):
        n = 1
        for d_ in ap.shape[1:]:
            n *= int(d_)
        return n

    def mm(out, lhsT, rhs, start, stop, r, w, tp=None):
        if tp is None:
            fn = lambda e: e.matmul(out, lhsT=lhsT, rhs=rhs, start=start, stop=stop)
        else:
            fn = lambda e: e.matmul(out, lhsT=lhsT, rhs=rhs, start=start, stop=stop, tile_position=tp)
        m_, n_ = fs(lhsT), fs(rhs)
        if lhsT.dtype == F32:
            c = 0.06 + n_ / 700.0
        elif m_ <= 64 and n_ <= 128:
            c = 0.03 + n_ / 6000.0
        elif n_ >= 512:
            c = 0.225
        else:
            c = 0.04 + n_ / 1200.0
        P.add("pe", fn, r, w, cost=c)

    def act(out, in_, func, r, w, scale=None, bias=None):
        kw = {}
        if scale is not None:
            kw["scale"] = scale
        if bias is not None:
            kw["bias"] = bias
        P.add("act", lambda e: e.activation(out=out, in_=in_, func=func, **kw), r, w, cost=0.22 + fs(out) / 1100.0)

    def ecost(eng, out):
        if eng == "pool":
            return 0.3 + fs(out) / 450.0
        return 0.12 + fs(out) / 900.0

    def tt(out, in0, in1, op, r, w, eng="dve"):
        P.add(eng, lambda e: e.tensor_tensor(out=out, in0=in0, in1=in1, op=op), r, w, cost=ecost(eng, out))

    def ts(out, in0, s1, s2, op0, op1, r, w, eng="dve"):
        P.add(eng, lambda e: e.tensor_scalar(out=out, in0=in0, scalar1=s1, scalar2=s2, op0=op0, op1=op1), r, w,
              cost=ecost(eng, out))

    def stt(out, in0, sc, in1, op0, op1, r, w):
        P.add("dve", lambda e: e.scalar_tensor_tensor(out=out, in0=in0, scalar=sc, in1=in1, op0=op0, op1=op1), r, w,
              cost=ecost("dve", out))

    def cp(out, in_, r, w, eng="dve"):
        if eng == "act":
            P.add("act", lambda e: e.activation(out=out, in_=in_, func=AF.Copy), r, w, cost=0.22 + fs(out) / 1100.0)
        else:
            P.add(eng, lambda e: e.tensor_copy(out=out, in_=in_), r, w, cost=ecost(eng, out))

    def scan(out, d0, d1, init, r, w):
        P.add("dve", lambda e: e.tensor_tensor_scan(out=out, data0=d0, data1=d1, initial=init,
                                                    op0=ALU.mult, op1=ALU.add), r, w, cost=0.1 + fs(out) / 450.0)

    def recip(out, in_, r, w):
        P.add("dve", lambda e: e.reciprocal(out=out, in_=in_), r, w, cost=0.1 + fs(out) / 120.0)

    def memset(ap, val, w, eng="dve"):
        P.add(eng, lambda e: e.memset(ap, val), (), w, cost=ecost(eng, ap))

    def dma(out, in_, r, w, q="sp"):
        P.add(q, lambda e: e.dma_start(out=out, in_=in_), r, w, dma=True, cost=2.5)

    one_ap = CFT.t[:, 128:129]

    def sigm(out, in_, negbias, r, w):
        act(out, in_, AF.Exp, r, w, scale=-1.0, bias=negbias)
        act(out, out, AF.Ln, list(w) + KCF, w, bias=one_ap)
        act(out, out, AF.Exp, w, w, scale=-1.0)

    def rpow(out, in_, p, r, w, bias=None):
        act(out, in_, AF.Ln, r, w, bias=bias)
        act(out, out, AF.Exp, w, w, scale=p)

    def psk(*banks):
        return [PS.k(b) for b in banks]

    def r3(ap, inner):
        return ap.rearrange("p (a b) -> p a b", b=inner)

    wstate = {"loaded": 0}
    total_chunks = NLY * NWCH

    def wload_upto(n):
        while wstate["loaded"] < min(n, total_chunks):
            g = wstate["loaded"]
            li_, i = divmod(g, NWCH)
            s = g % NSLOT
            dma(WR.t[:, s, :, :], wst[li_, i, :, :, :], (), [WR.k(s)], q="pool")
            wstate["loaded"] += 1

    def wdone(g):
        wload_upto(g + 1 + NSLOT)

    def wslot(g):
        s = g % NSLOT
        return WR.t[:, s, :, :], WR.k(s)

    dma(CBT.t[:, :], cb_d[:, :], (), KCB, q="pool")
    dma(CFT.t[:, :], cf_d[:, :], (), KCF)
    dma(PVT.t[:, :, :], pv_d.rearrange("l p c -> p l c"), (), KPV)
    for tb in range(NTB):
        for kc in range(KC):
            dma(X32.t[:, kc, tb * TB:(tb + 1) * TB], x_in[kc * 128:(kc + 1) * 128, tb * TB:(tb + 1) * TB],
                (), [X32.k(kc, tb)])
    wload_upto(NSLOT)
    for tb in range(NTB):
        for kc in range(KC):
            sl = slice(tb * TB, (tb + 1) * TB)
            cp(XBF.t[:, kc, sl], X32.t[:, kc, sl], [X32.k(kc, tb)], [XBF.k(kc, tb)],
               eng="act" if (kc + tb) % 2 == 0 else "dve")

    pbank = {"i": 0}

    def next_bank(cands=(0, 1, 6, 7)):
        b = cands[pbank["i"] % len(cands)]
        pbank["i"] += 1
        return b

    def proj(g, tb, bank):
        wap, wk = wslot(g)
        for kc in range(KC):
            mm(PS.t[:, bank, :], wap[:, kc, :], XBF.t[:, kc, tb * TB:(tb + 1) * TB],
               kc == 0, kc == KC - 1, [wk, XBF.k(kc, tb)], psk(bank))

    for li, lay in enumerate(layers):
        gbase = li * NWCH
        lyr["i"] = li

        def pv(col, n=1, rows=slice(0, 128)):
            return PVT.t[rows, li, col:col + n]

        def dv(col, n=1, rows=slice(0, 128)):
            return DVT.t[rows, col:col + n]

        dma(SMW.t[:, :], sw_d[li, :, :], (), KSW, q="pool")
        ts(dv(DV_OMM, 7), pv(PV_MU, 7), -1.0, 1.0, ALU.mult, ALU.add, KPV, KDV)
        act(dv(DV_SE, 4), pv(PV_SINK, 4), AF.Exp, KPV, KDV)
        act(dv(DV_LC, 2), pv(PV_LAM, 2), AF.Exp, KPV, KDV, scale=-1.0)
        ts(dv(DV_LC, 2), dv(DV_LC, 2), 1.0, None, ALU.add, ALU.bypass, KDV, KDV)
        act(dv(DV_LC, 2), dv(DV_LC, 2), AF.Ln, KDV, KDV)
        ts(dv(DV_LC2, 2), dv(DV_LC, 2), -16.0, None, ALU.mult, ALU.bypass, KDV, KDV)
        ts(dv(DV_LC, 2), dv(DV_LC, 2), -8.0, None, ALU.mult, ALU.bypass, KDV, KDV)
        ts(dv(DV_OKA, 2), pv(PV_KA, 2), -1.0, 1.0, ALU.mult, ALU.add, KPV, KDV)
        for dcol, pcol in ((DV_NBA, PV_BA), (DV_NBX, PV_BX), (DV_NW0, PV_W0), (DV_NA0, PV_A0), (DV_NV0, PV_V0)):
            ts(dv(dcol, 2), pv(pcol, 2), -1.0, None, ALU.mult, ALU.bypass, KPV, KDV)
        memset(dv(DV_EPSLN), LN_EPS, KDV)
        memset(dv(DV_EPSGN), GN_EPS, KDV)
        memset(dv(DV_EPSKK), 1e-12, KDV)

        P.set_fence()
        with ExitStack() as pha:
          if stage >= 1:
            QT = sb("QT", [128, 4, T], BF16, pha)
            KD = sb("KD", [128, 2, T], BF16, pha)
            VT = sb("VT", [128, 16, 128], BF16, pha)
            with ExitStack() as ph:
                XBs = [sb("XB", [128, T + 4], F32, ph)] * 2
                XCs = [sb("XC", [128, T], F32, ph)] * 2
                XCBs = [sb("XCB", [128, T], BF16, ph)] * 2
                BRs = [sb("BR", [128, TB], F32, ph)] * 2
                BIs = [sb("BI", [128, TB], F32, ph)] * 2
                BMs = [sb("BM", [128, TB], F32, ph)] * 2
                BHs = [[sb("BH%d" % i, [128, TB], F32, ph) for i in range(2)]] * 2
                memset(XBs[0].t[:, 0:4], 0.0, [XBs[0].k("pad")])
                for cb in range(2):
                    g = gbase + 0 + cb
                    for tb in range(NTB):
                        b = next_bank()
                        proj(g, tb, b)
                        act(Y.t[:, cb, tb * TB:(tb + 1) * TB], PS.t[:, b, :], AF.Silu, psk(b), [Y.k(cb, tb)])
                    wdone(g)
                def gen_B(cb):
                    XB, XC, XCB, BR, BI, BM, BH = XBs[cb], XCs[cb], XCBs[cb], BRs[cb], BIs[cb], BMs[cb], BHs[cb]
                    gb = 2 + 2 * cb
                    for tb in range(NTB):
                        rk = [XB.k(tb)] + ([XB.k(tb - 1)] if tb > 0 else [XB.k("pad")])
                        o = XC.t[:, tb * TB:(tb + 1) * TB]
                        ts(o, XB.t[:, 4 + tb * TB:4 + (tb + 1) * TB], pv(PV_CONVW + 3 * 2 + cb), pv(PV_CONVB + cb),
                           ALU.mult, ALU.add, rk + KPV, [XC.k(tb)])
                        for j in range(3):
                            s0 = 4 + tb * TB - 3 + j
                            stt(o, XB.t[:, s0:s0 + TB], pv(PV_CONVW + j * 2 + cb), o, ALU.mult, ALU.add,
                                rk + KPV + [XC.k(tb)], [XC.k(tb)])
                        cp(XCB.t[:, tb * TB:(tb + 1) * TB], o, [XC.k(tb)], [XCB.k(tb)], eng="pool")
                        yield
                    for tb in range(NTB):
                        sl = slice(tb * TB, (tb + 1) * TB)
                        ca = SW_LRU + (0 * 2 + cb) * 128
                        cx = SW_LRU + (1 * 2 + cb) * 128
                        mm(PS.t[:, gb, :], SMW.t[:, ca:ca + 128], XCB.t[:, sl], True, True, KSW + [XCB.k(tb)], psk(gb))
                        mm(PS.t[:, gb + 1, :], SMW.t[:, cx:cx + 128], XCB.t[:, sl], True, True, KSW + [XCB.k(tb)],
                           psk(gb + 1))
                        sigm(BR.t[:, :], PS.t[:, gb, :], dv(DV_NBA + cb), psk(gb) + KDV, [BR.k()])
                        yield
                        sigm(BI.t[:, :], PS.t[:, gb + 1, :], dv(DV_NBX + cb), psk(gb + 1) + KDV, [BI.k()])
                        yield
                        act(BM.t[:, :], BR.t[:, :], AF.Exp, [BR.k()] + KDV, [BM.k()], scale=dv(DV_LC2 + cb))
                        act(BR.t[:, :], BR.t[:, :], AF.Exp, [BR.k()] + KDV, [BR.k()], scale=dv(DV_LC + cb))
                        yield
                        act(BM.t[:, :], BM.t[:, :], AF.Ln, [BM.k()] + KCF, [BM.k()], scale=-1.0, bias=one_ap)
                        act(BM.t[:, :], BM.t[:, :], AF.Exp, [BM.k()], [BM.k()], scale=0.5)
                        tt(BI.t[:, :], BI.t[:, :], XC.t[:, sl], ALU.mult, [BI.k(), XC.k(tb)], [BI.k()])
                        yield
                        tt(BI.t[:, :], BI.t[:, :], BM.t[:, :], ALU.mult, [BI.k(), BM.k()], [BI.k()])
                        hcur, hprev = BH[tb % 2], BH[(tb + 1) % 2]
                        init = 0.0 if tb == 0 else hprev.t[:, TB - 1:TB]
                        scan(hcur.t[:, :], BR.t[:, :], BI.t[:, :], init,
                             [BR.k(), BI.k()] + ([hprev.k()] if tb > 0 else []), [hcur.k()])
                        yv = Y.t[:, cb, sl]
                        tt(yv, yv, hcur.t[:, :], ALU.mult, [hcur.k(), Y.k(cb, tb)], [Y.k(cb, tb)])
                        yield

                for cb in range(2):
                    g = gbase + 2 + cb
                    XB = XBs[cb]
                    for tb in range(NTB):
                        b = next_bank()
                        proj(g, tb, b)
                        cp(XB.t[:, 4 + tb * TB:4 + (tb + 1) * TB], PS.t[:, b, :], psk(b), [XB.k(tb)],
                           eng="dve" if tb % 2 == 0 else "act")
                    wdone(g)
                    for _ in gen_B(cb):
                        pass

                for c in range(4):
                    g = gbase + 4 + c
                    for tb in range(NTB):
                        b = next_bank()
                        proj(g, tb, b)
                        cp(QT.t[:, c, tb * TB:(tb + 1) * TB], PS.t[:, b, :], psk(b), [QT.k(c, tb)],
                           eng="act" if tb % 2 == 0 else "dve")
                    wdone(g)
                for gi in range(2):
                    g = gbase + 8 + gi
                    for tb in range(NTB):
                        b = next_bank()
                        proj(g, tb, b)
                        cp(KD.t[:, gi, tb * TB:(tb + 1) * TB], PS.t[:, b, :], psk(b), [KD.k(gi, tb)],
                           eng="act" if tb % 2 == 0 else "dve")
                    wdone(g)
                g = gbase + 10
                wap, wk = wslot(g)
                for tb in range(NTB):
                    b = next_bank()
                    for nn in range(4):
                        n = tb * 4 + nn
                        for kc in range(KC):
                            mm(PS.t[:, b, nn * 128:(nn + 1) * 128], XBF.t[:, kc, n * 128:(n + 1) * 128], wap[:, kc, :],
                               kc == 0, kc == KC - 1, [wk, XBF.k(kc, tb)], psk(b))
                    cp(VT.t[:, tb * 4:(tb + 1) * 4, :], r3(PS.t[:, b, :], 128), psk(b), [VT.k(tb)], eng="act")
                wdone(g)
            P.set_fence()
            for j in (range(2) if stage >= 2 else ()):
                g = gbase + 11 + j
                wap, wk = wslot(g)
                for mm_ in range(4):
                    m = 4 * j + mm_
                    for tb in range(NTB):
                        b = next_bank()
                        for kc in range(2):
                            mm(PS.t[:, b, :], wap[:, kc * 4 + mm_, :], Y.t[:, kc, tb * TB:(tb + 1) * TB],
                               kc == 0, kc == 1, [wk, Y.k(kc, tb)], psk(b))
                        xs = X32.t[:, m, tb * TB:(tb + 1) * TB]
                        stt(xs, xs, ALPHA, PS.t[:, b, :], ALU.mult, ALU.add, psk(b) + [X32.k(m, tb)], [X32.k(m, tb)])
                wdone(g)
            with ExitStack() as ph:
                PT = sb("PT", [128, 2, 2, 1024], BF16, ph)
                A1 = sb("A1", [128, 512], F32, ph)
                A2 = sb("A2", [128, 512], F32, ph)
                A3 = sb("A3", [128, 512], F32, ph)
                for c in range(4):
                    g = gbase + 13 + c
                    for tb in range(NTB):
                        b = next_bank()
                        proj(g, tb, b)
                        act(Y.t[:, c, tb * TB:(tb + 1) * TB], PS.t[:, b, :], AF.Silu, psk(b), [Y.k(c, tb)])
                    wdone(g)
                def attn_scores(n):
                    tbq = n // 4
                    jlist = [(0, n - 1), (1, n)] if n > 0 else [(1, n)]
                    for jj, kb in jlist:
                        tbk = kb // 4
                        mcol = CB_MP if jj == 0 else CB_MC
                        for par in range(2):
                            bank = 2 + jj * 2 + par
                            mm(PS.t[:, bank, :], ident, CBT.t[:, mcol:mcol + 512], True, False, KCB, psk(bank))
                        for gq in range(2):
                            for cg in range(2):
                                for par in range(2):
                                    bank = 2 + jj * 2 + par
                                    c = gq * 2 + cg
                                    pos = gq * 2 + cg
                                    mm(PS.t[:, bank, pos * 128:(pos + 1) * 128],
                                       KD.t[par * 64:(par + 1) * 64, gq, kb * 128:(kb + 1) * 128],
                                       QT.t[par * 64:(par + 1) * 64, c, n * 128:(n + 1) * 128],
                                       False, (gq == 1 and cg == 1), [KD.k(gq, tbk), QT.k(c, tbq)], psk(bank))
                        b0 = 2 + jj * 2
                        act(PT.t[:, n % 2, jj, :], PS.t[:, b0:b0 + 2, :].rearrange("p a b -> p (a b)"), AF.Exp,
                            psk(b0, b0 + 1), [PT.k(n % 2, jj)], scale=0.125)

                def attn_pv(n):
                    tbq = n // 4
                    jlist = [(0, n - 1), (1, n)] if n > 0 else [(1, n)]
                    for (obank, use_ones) in ((6, False), (7, True)):
                        for gq in range(2):
                            for par in range(2):
                                for ji, (jj, kb) in enumerate(jlist):
                                    if use_ones:
                                        lhsT, rk = ones_bf64, KCB
                                    else:
                                        lhsT, rk = VT.t[:, kb, gq * 64:(gq + 1) * 64], [VT.k(kb // 4)]
                                    c0 = par * 512 + gq * 256
                                    mm(PS.t[par * 64:(par + 1) * 64, obank, gq * 256:(gq + 1) * 256],
                                       lhsT, PT.t[:, n % 2, jj, c0:c0 + 256],
                                       ji == 0, ji == len(jlist) - 1, rk + [PT.k(n % 2, jj)], psk(obank),
                                       tp=(0, 64 * par))
                    tt(r3(A1.t[:, :], 128), r3(PS.t[:, 7, :], 128),
                       dv(DV_SE, 4).unsqueeze(2).to_broadcast([128, 4, 128]), ALU.add, psk(7) + KDV, [A1.k()])
                    rpow(A2.t[:, :], A1.t[:, :], -1.0, [A1.k()], [A2.k()])
                    tt(A3.t[:, :], PS.t[:, 6, :], A2.t[:, :], ALU.mult, psk(6) + [A2.k()], [A3.k()])
                    yv = Y.t[:, 0:4, n * 128:(n + 1) * 128]
                    yk = [Y.k(c, tbq) for c in range(4)]
                    tt(yv, yv, r3(A3.t[:, :], 128), ALU.mult, [A3.k()] + yk, yk)

                attn_scores(0)
                for n in range(16):
                    if n + 1 < 16:
                        attn_scores(n + 1)
                    attn_pv(n)
        for j in (range(4) if stage >= 2 else ()):
            g = gbase + 17 + j
            wap, wk = wslot(g)
            for mm_ in range(2):
                m = 2 * j + mm_
                for tb in range(NTB):
                    b = next_bank()
                    for kc in range(4):
                        mm(PS.t[:, b, :], wap[:, kc * 2 + mm_, :], Y.t[:, kc, tb * TB:(tb + 1) * TB],
                           kc == 0, kc == 3, [wk, Y.k(kc, tb)], psk(b))
                    xs = X32.t[:, m, tb * TB:(tb + 1) * TB]
                    tt(xs, xs, PS.t[:, b, :], ALU.add, psk(b) + [X32.k(m, tb)], [X32.k(m, tb)])
            wdone(g)


        P.set_fence()
        with ExitStack() as ph:
          if stage >= 4:
            def f32t(name):
                return sb(name, [128, SB], F32, ph)

            def bft(name, shape):
                return sb(name, shape, BF16, ph)

            two = range(2)
            ET = [{n: f32t("c%s%d" % (n, h)) for n in ("TMP", "RS", "KS", "SG", "GG", "AA", "EG", "ENG", "EGM")}
                  for h in two]
            PTMP, PT1 = f32t("cPTMP"), f32t("cPT1")
            GNT = [{"YS": ET[h]["SG"], "YQ": ET[h]["GG"], "MEAN": ET[h]["AA"], "VAR": ET[h]["TMP"]} for h in two]
            VS = [f32t("cVS0"), f32t("cVS1")]
            XSW = sb("cXSW", [64, SB], F32, ph)
            TWAL = bft("cTWAL", [64, SB])
            VBp = [bft("cVB%d" % i, [128, 2, SB]) for i in two]
            VSB = bft("cVSB", [128, 2, SB])
            P1B = bft("cP1B", [16, SB])
            SQBh = [bft("cSQB%d" % h, [128, SB]) for h in two]
            RKBh = [bft("cRKB%d" % h, [128, SB]) for h in two]
            CAR = sb("cCAR", [128, 8], F32, ph)
            BONp = [[f32t("cBON%d%d" % (i, h)) for h in two] for i in two]
            ARp = [[bft("cAR%d%d" % (i, h), [128, NCH, 128]) for h in two] for i in two]
            BT = [bft("cBT%d" % h, [128, SB]) for h in two]
            KT = [bft("cKT%d" % h, [128, SB]) for h in two]
            BHt = [bft("cBHt%d" % h, [128, SB]) for h in two]
            KHt = [bft("cKHt%d" % h, [128, SB]) for h in two]
            NTMR = [bft("cNTMR%d" % h, [128, NCH, 128]) for h in two]
            MKT = [bft("cMKT%d" % h, [128, NCH, 128]) for h in two]
            NK = [[bft("cNK%d%d" % (h, i), [128, NCH, 64]) for i in two] for h in two]
            NTQ = [[bft("cNTQ%d%d" % (h, i), [128, NCH, 128]) for i in two] for h in two]
            QF = [bft("cQF%d" % h, [128, NCH, 64]) for h in two]
            TOK = [bft("cTOK%d" % h, [128, NCH, 4, 64]) for h in two]
            MVB = [bft("cMVB%d" % h, [128, NCH, 64]) for h in two]
            WT = [bft("cWT%d" % h, [128, NCH, 64]) for h in two]
            UT = [sb("cUT%d" % h, [128, NCH, 64], F32, ph) for h in two]
            UB = [bft("cUB%d" % h, [128, 64]) for h in two]
            HB = [[bft("cHB%d%d" % (h, i), [128, 64]) for i in two] for h in two]
            H32 = [sb("cH32%d" % h, [128, 64], F32, ph) for h in two]
            GCp = [[sb("cGC%d%d" % (i, h), [128, NCH], F32, ph) for h in two] for i in two]
            hpar = [0, 0]

            for hp in range(2):
                g = gbase + 21 + hp
                for tb in range(NTB):
                    b = next_bank()
                    proj(g, tb, b)
                    act(Y.t[:, 2 + hp, tb * TB:(tb + 1) * TB], PS.t[:, b, :], AF.Silu, psk(b), [Y.k(2 + hp, tb)])
                wdone(g)
            g_cw = gbase + 23
            g_v = [gbase + 24, gbase + 25]
            g_r = [gbase + 26, gbase + 28]
            g_k = [gbase + 27, gbase + 29]
            for hp in range(2):
                memset(H32[hp].t[:, :], 0.0, [H32[hp].k()])
                memset(HB[hp][0].t[:, :], 0.0, [HB[hp][0].k()])
            memset(CAR.t[:, :], 0.0, [CAR.k(c) for c in range(8)])

            def projC(g, sbi, bank, col0, rows=128):
                wap_, wk_ = wslot(g)
                tbx_ = (sbi * SB) // TB
                for kc in range(KC):
                    mm(PS.t[0:rows, bank, col0:col0 + SB], wap_[:, kc, 0:rows], XBF.t[:, kc, sbi * SB:(sbi + 1) * SB],
                       kc == 0, kc == KC - 1, [wk_, XBF.k(kc, tbx_)], psk(bank))

            def tshift(bank, col0, rows, mucol, ommcol, carcol, out, TMP):
                rs = slice(0, rows)
                src = PS.t[rs, bank, col0:col0 + SB]
                act(TMP.t[rs, :], src, AF.Identity, psk(bank) + KDV, [TMP.k()], scale=dv(ommcol, 1, rs))
                stt(out.t[rs, 1:SB], PS.t[rs, bank, col0:col0 + SB - 1], pv(mucol, 1, rs), TMP.t[rs, 1:SB],
                    ALU.mult, ALU.add, psk(bank) + KPV + [TMP.k()], [out.k()])
                stt(out.t[rs, 0:1], CAR.t[rs, carcol:carcol + 1], pv(mucol, 1, rs), TMP.t[rs, 0:1],
                    ALU.mult, ALU.add, [CAR.k(carcol)] + KPV + [TMP.k()], [out.k()])
                cp(CAR.t[rs, carcol:carcol + 1], PS.t[rs, bank, col0 + SB - 1:col0 + SB], psk(bank), [CAR.k(carcol)])

            rmaskb = CBT.t[:, CB_RM:CB_RM + 128].unsqueeze(1).to_broadcast([128, NCH, 128])
            masklb = CBT.t[:, CB_ML:CB_ML + 64].unsqueeze(1).to_broadcast([128, NCH, 64])
            ident2 = CBT.t[:, CB_ID2:CB_ID2 + 64]
            ident2b = ident2.unsqueeze(1).to_broadcast([128, NCH, 64])
            HS = [slice(0, 64), slice(64, 128)]

            def jh():
                for j in range(NCH):
                    for h in range(2):
                        yield j, h, HS[h], slice(j * CH, (j + 1) * CH)

            def prologue(sbi):
                tbx = (sbi * SB) // TB
                ssl = slice(sbi * SB, (sbi + 1) * SB)
                VB = VBp[sbi % 2]
                TMP, T1 = PTMP, PT1
                projC(g_cw, sbi, 6, 0, rows=64)
                tshift(6, 0, 64, PV_MUCW, DV_OMM + 6, 6, XSW, TMP)
                yield
                act(XSW.t[0:32, :], XSW.t[0:32, :], AF.Exp, [XSW.k()], [XSW.k()], scale=-2.0)
                act(XSW.t[0:32, :], XSW.t[0:32, :], AF.Ln, [XSW.k()] + KCF, [XSW.k()], bias=one_ap[0:32, :])
                act(XSW.t[0:32, :], XSW.t[0:32, :], AF.Exp, [XSW.k()], [XSW.k()], scale=-1.0)
                ts(TWAL.t[0:32, :], XSW.t[0:32, :], 2.0, -1.0, ALU.mult, ALU.add, [XSW.k()], [TWAL.k()])
                cp(TWAL.t[32:64, :], XSW.t[32:64, :], [XSW.k()], [TWAL.k()], eng="pool")
                yield
                for hp in range(2):
                    projC(g_v[hp], sbi, 7, hp * SB)
                    tshift(7, hp * SB, 128, PV_MU + 4 + hp, DV_OMM + 4 + hp, 4 + hp, VS[hp], TMP)
                    yield
                if lay == 0:
                    for hp in range(2):
                        cp(VB.t[:, hp, :], VS[hp].t[:, :], [VS[hp].k()], [VB.k(hp)], eng="pool")
                        cp(VF.t[:, hp, ssl], VS[hp].t[:, :], [VS[hp].k()], [VF.k(hp, tbx)], eng="pool")
                else:
                    for hp in range(2):
                        cp(VSB.t[:, hp, :], VS[hp].t[:, :], [VS[hp].k()], [VSB.k(hp)], eng="pool")
                    for hp in range(2):
                        mm(PS.t[0:16, 6, 256:512], SMW.t[:, SW_V1 + hp * 16:SW_V1 + hp * 16 + 16], VSB.t[:, hp, :],
                           hp == 0, hp == 1, KSW + [VSB.k(hp)], psk(6))
                    cp(P1B.t[:, :], PS.t[0:16, 6, 256:512], psk(6), [P1B.k()], eng="act")
                    for hp in range(2):
                        mm(PS.t[:, 6, 0:SB], SMW.t[0:16, SW_V2 + hp * 128:SW_V2 + hp * 128 + 128], P1B.t[0:16, :],
                           True, True, KSW + [P1B.k()], psk(6))
                        sigm(T1.t[:, :], PS.t[:, 6, 0:SB], dv(DV_NV0 + hp), psk(6) + KDV, [T1.k()])
                        tt(TMP.t[:, :], VF.t[:, hp, ssl], VS[hp].t[:, :], ALU.subtract,
                           [VF.k(hp, tbx), VS[hp].k()], [TMP.k()])
                        tt(TMP.t[:, :], TMP.t[:, :], T1.t[:, :], ALU.mult, [TMP.k(), T1.k()], [TMP.k()])
                        tt(VB.t[:, hp, :], VS[hp].t[:, :], TMP.t[:, :], ALU.add, [VS[hp].k(), TMP.k()], [VB.k(hp)])
                        yield

            def stage_E(sbi, hp):
                par = sbi % 2
                VB, AR, GC, BON = VBp[par], ARp[par], GCp[par], BONp[par]
                SQB, RKB = SQBh[hp], RKBh[hp]
                e_ = ET[hp]
                TMP, RS, KS, SG, GG, AA, EG, ENG, EGM = (e_[n] for n in
                                                         ("TMP", "RS", "KS", "SG", "GG", "AA", "EG", "ENG", "EGM"))
                RN, T1, BB, KK = TMP, TMP, SG, GG
                if True:
                    B0 = B1 = B2 = 6 + hp
                    projC(g_r[hp], sbi, B0, 0)
                    projC(g_k[hp], sbi, B0, SB)
                    tshift(B0, 0, 128, PV_MU + 0 + hp, DV_OMM + 0 + hp, 0 + hp, RS, TMP)
                    tshift(B0, SB, 128, PV_MU + 2 + hp, DV_OMM + 2 + hp, 2 + hp, KS, TMP)
                    yield
                    cz = SW_WA2 + hp * 128
                    mm(PS.t[:, B1, 0:SB], SMW.t[0:32, cz:cz + 128], TWAL.t[0:32, :], True, True, KSW + [TWAL.k()], psk(B1))
                    sigm(SG.t[:, :], PS.t[:, B1, 0:SB], dv(DV_NW0 + hp), psk(B1) + KDV, [SG.k()])
                    mm(PS.t[:, B2, 0:SB], SMW.t[32:64, cz:cz + 128], TWAL.t[32:64, :], True, True,
                       KSW + [TWAL.k()], psk(B2))
                    sigm(AA.t[:, :], PS.t[:, B2, 0:SB], dv(DV_NA0 + hp), psk(B2) + KDV, [AA.k()])
                    yield
                    scan(GG.t[:, :], CBT.t[:, CB_CM:CB_CM + SB], SG.t[:, :], 0.0, KCB + [SG.k()], [GG.k()])
                    act(EG.t[:, :], GG.t[:, :], AF.Exp, [GG.k()], [EG.k()], scale=-CW)
                    act(ENG.t[:, :], GG.t[:, :], AF.Exp, [GG.k()], [ENG.k()], scale=CW)
                    yield
                    tt(EGM.t[:, :], GG.t[:, :], SG.t[:, :], ALU.subtract, [GG.k(), SG.k()], [EGM.k()])
                    act(EGM.t[:, :], EGM.t[:, :], AF.Exp, [EGM.k()], [EGM.k()], scale=-CW)
                    cp(GC[hp].t[:, :].unsqueeze(2), r3(EG.t[:, :], CH)[:, :, CH - 1:CH], [EG.k()], [GC[hp].k()])
                    yield
                    ts(KK.t[:, :], KS.t[:, :], pv(PV_KK + hp), None, ALU.mult, ALU.bypass, [KS.k()] + KPV, [KK.k()])
                    tt(SQB.t[:, :], KK.t[:, :], KK.t[:, :], ALU.mult, [KK.k()], [SQB.k()], eng="pool")
                    mm(PS.t[:, B1, SB:2 * SB], bones_bf, SQB.t[:, :], True, True, KCB + [SQB.k()], psk(B1))
                    rpow(RN.t[:, :], PS.t[:, B1, SB:2 * SB], -0.5, psk(B1) + KDV, [RN.k()], bias=dv(DV_EPSKK))
                    yield
                    tt(KK.t[:, :], KK.t[:, :], RN.t[:, :], ALU.mult, [KK.k(), RN.k()], [KK.k()])
                    ts(T1.t[:, :], AA.t[:, :], pv(PV_KA + hp), dv(DV_OKA + hp), ALU.mult, ALU.add,
                       [AA.k()] + KPV + KDV, [T1.k()])
                    tt(KS.t[:, :], KS.t[:, :], T1.t[:, :], ALU.mult, [KS.k(), T1.k()], [KS.k()])
                    tt(BB.t[:, :], KK.t[:, :], AA.t[:, :], ALU.mult, [KK.k(), AA.k()], [BB.k()])
                    yield
                    tt(T1.t[:, :], RS.t[:, :], KS.t[:, :], ALU.mult, [RS.k(), KS.k()], [T1.k()])
                    ts(RKB.t[:, :], T1.t[:, :], pv(PV_RK + hp), None, ALU.mult, ALU.bypass, [T1.k()] + KPV, [RKB.k()])
                    mm(PS.t[:, B2, SB:2 * SB], bones_bf, RKB.t[:, :], True, True, KCB + [RKB.k()], psk(B2))
                    tt(BON[hp].t[:, :], PS.t[:, B2, SB:2 * SB], VB.t[:, hp, :], ALU.mult, psk(B2) + [VB.k(hp)],
                       [BON[hp].k()])
                    yield
                    stt(AR[hp].t[:, :, 0:64], r3(KK.t[:, :], CH), -1.0, r3(EGM.t[:, :], CH), ALU.mult, ALU.mult,
                        [KK.k(), EGM.k()], [AR[hp].k()])
                    tt(AR[hp].t[:, :, 64:128], r3(RS.t[:, :], CH), r3(EG.t[:, :], CH), ALU.mult, [RS.k(), EG.k()],
                       [AR[hp].k()])
                    yield
                    tt(BT[hp].t[:, :], BB.t[:, :], ENG.t[:, :], ALU.mult, [BB.k(), ENG.k()], [BT[hp].k()])
                    tt(KT[hp].t[:, :], KS.t[:, :], ENG.t[:, :], ALU.mult, [KS.k(), ENG.k()], [KT[hp].k()])
                    yield
                    gcb = GC[hp].t[:, :].unsqueeze(2).to_broadcast([128, NCH, CH])
                    tt(r3(BHt[hp].t[:, :], CH), r3(BT[hp].t[:, :], CH), gcb, ALU.mult, [BT[hp].k(), GC[hp].k()],
                       [BHt[hp].k()])
                    tt(r3(KHt[hp].t[:, :], CH), r3(KT[hp].t[:, :], CH), gcb, ALU.mult, [KT[hp].k(), GC[hp].k()],
                       [KHt[hp].k()])

            def stage_P2(sbi, hp):
                par = sbi % 2
                VB, AR, GC, BON = VBp[par], ARp[par], GCp[par], BONp[par]
                for j in range(NCH):
                    cs = slice(j * CH, (j + 1) * CH)
                    for q in range(4):
                        for h in range(2):
                            hs = HS[h]
                            src, sk = ((AR[hp].t[hs, j, 0:64], AR[hp].k()), (BHt[hp].t[hs, cs], BHt[hp].k()),
                                       (KHt[hp].t[hs, cs], KHt[hp].k()), (VB.t[hs, hp, cs], VB.k(hp)))[q]
                            bank = hp * 3 + j // 2
                            col = ((j % 2) * 4 + q) * 64
                            mm(PS.t[hs, bank, col:col + 64], src, ident2[hs, :], True, True, [sk] + KCB, psk(bank))
                for a_ in range(2):
                    bank = hp * 3 + a_
                    cp(TOK[hp].t[:, 2 * a_:2 * a_ + 2, :, :],
                       PS.t[:, bank, :].rearrange("p (a b c) -> p a b c", b=4, c=64), psk(bank), [TOK[hp].k()],
                       eng="act")

            def stage_P1(sbi, hp):
                par = sbi % 2
                VB, AR, GC, BON = VBp[par], ARp[par], GCp[par], BONp[par]
                B0, B1, B2 = hp * 3, hp * 3 + 1, hp * 3 + 2
                for j in range(NCH):
                    cs = slice(j * CH, (j + 1) * CH)
                    for h in range(2):
                        hs = HS[h]
                        mm(PS.t[hs, B0, j * 128:(j + 1) * 128], BT[hp].t[hs, cs], AR[hp].t[hs, j, :], True, True,
                           [BT[hp].k(), AR[hp].k()], psk(B0))
                    for h in range(2):
                        hs = HS[h]
                        mm(PS.t[hs, B1, j * 128:(j + 1) * 128], KT[hp].t[hs, cs], AR[hp].t[hs, j, :], True, True,
                           [KT[hp].k(), AR[hp].k()], psk(B1))
                    for h in range(2):
                        hs = HS[h]
                        mm(PS.t[hs, B2, j * 64:(j + 1) * 64], AR[hp].t[hs, j, 0:64], BT[hp].t[hs, cs], True, True,
                           [BT[hp].k(), AR[hp].k()], psk(B2))
                tt(NTMR[hp].t[:, :, :], r3(PS.t[:, B0, :], 128), rmaskb, ALU.mult, psk(B0) + KCB, [NTMR[hp].k()])
                tt(MKT[hp].t[:, :, :], r3(PS.t[:, B1, :], 128), rmaskb, ALU.mult, psk(B1) + KCB, [MKT[hp].k()])
                tt(NK[hp][0].t[:, :, :], r3(PS.t[:, B2, 0:SB], 64), masklb, ALU.mult, psk(B2) + KCB,
                   [NK[hp][0].k()])

            def stage_P3(sbi, hp):
                par = sbi % 2
                VB, AR, GC, BON = VBp[par], ARp[par], GCp[par], BONp[par]
                B2 = hp * 3 + 2
                for j, h, hs, cs in jh():
                    mm(PS.t[hs, B2, SB + j * 64:SB + (j + 1) * 64], MKT[hp].t[hs, j, 0:64], TOK[hp].t[hs, j, 3, :],
                       True, True, [MKT[hp].k(), TOK[hp].k()], psk(B2))
                cp(MVB[hp].t[:, :, :], r3(PS.t[:, B2, SB:2 * SB], 64), psk(B2), [MVB[hp].k()], eng="act")

            def stage_P4(sbi, hp, r):
                nk, ntq = NK[hp], NTQ[hp]
                B0, B1, B3 = hp * 3, hp * 3 + 1, hp * 3 + 2
                if r == 0:
                    for j in range(NCH):
                        for h in range(2):
                            hs = HS[h]
                            mm(PS.t[hs, B3, j * 64:(j + 1) * 64], NTMR[hp].t[hs, j, 0:64], nk[0].t[hs, j, :], True,
                               True, [NTMR[hp].k(), nk[0].k()], psk(B3))
                        for h in range(2):
                            hs = HS[h]
                            mm(PS.t[hs, B0, j * 128:j * 128 + 64], nk[0].t[hs, j, :], NTMR[hp].t[hs, j, 0:64], True,
                               True, [NTMR[hp].k(), nk[0].k()], psk(B0))
                    cp(nk[1].t[:, :, :], r3(PS.t[:, B3, 0:SB], 64), psk(B3), [nk[1].k()], eng="act")
                    cp(ntq[1].t[:, :, 0:64], r3(PS.t[:, B0, :], 128)[:, :, 0:64], psk(B0), [ntq[1].k()], eng="act")
                    tt(ntq[1].t[:, :, 64:128], NTMR[hp].t[:, :, 0:64], ident2b, ALU.add, [NTMR[hp].k()] + KCB,
                       [ntq[1].k()])
                elif r < 5:
                    cur, nxt = r % 2, 1 - (r % 2)
                    ca = 0 if r % 2 == 0 else SB
                    bb = B0 if r % 2 == 0 else B1
                    for j in range(NCH):
                        for h in range(2):
                            hs = HS[h]
                            mm(PS.t[hs, B3, ca + j * 64:ca + (j + 1) * 64], ntq[cur].t[hs, j, 0:64],
                               nk[cur].t[hs, j, :], True, True, [ntq[cur].k(), nk[cur].k()], psk(B3))
                        for h in range(2):
                            hs = HS[h]
                            if r < 4:
                                mm(PS.t[hs, bb, j * 128:(j + 1) * 128], nk[cur].t[hs, j, :], ntq[cur].t[hs, j, :],
                                   True, True, [ntq[cur].k(), nk[cur].k()], psk(bb))
                            else:
                                mm(PS.t[hs, bb, j * 128 + 64:(j + 1) * 128], nk[cur].t[hs, j, :],
                                   ntq[cur].t[hs, j, 64:128], True, True, [ntq[cur].k(), nk[cur].k()], psk(bb))
                    cp(nk[nxt].t[:, :, :], r3(PS.t[:, B3, ca:ca + SB], 64), psk(B3), [nk[nxt].k()], eng="act")
                    if r < 4:
                        cp(ntq[nxt].t[:, :, 0:64], r3(PS.t[:, bb, :], 128)[:, :, 0:64], psk(bb), [ntq[nxt].k()],
                           eng="act")
                    tt(ntq[nxt].t[:, :, 64:128], ntq[cur].t[:, :, 64:128], r3(PS.t[:, bb, :], 128)[:, :, 64:128],
                       ALU.add, [ntq[cur].k()] + psk(bb), [ntq[nxt].k()])
                else:
                    for j, h, hs, cs in jh():
                        mm(PS.t[hs, B3, j * 64:(j + 1) * 64], nk[1].t[hs, j, :], ntq[1].t[hs, j, 64:128], True, True,
                           [ntq[1].k(), nk[1].k()], psk(B3))
                    tt(QF[hp].t[:, :, :], ntq[1].t[:, :, 64:128], r3(PS.t[:, B3, 0:SB], 64), ALU.add,
                       [ntq[1].k()] + psk(B3), [QF[hp].k()])

            def stage_P5(sbi, hp):
                par = sbi % 2
                VB, AR, GC, BON = VBp[par], ARp[par], GCp[par], BONp[par]
                B0 = hp * 3
                for j in range(NCH):
                    for h in range(2):
                        hs = HS[h]
                        mm(PS.t[hs, B0, j * 64:(j + 1) * 64], TOK[hp].t[hs, j, 0, :], QF[hp].t[hs, j, :], True, True,
                           [TOK[hp].k(), QF[hp].k()], psk(B0))
                    for h in range(2):
                        hs = HS[h]
                        mm(PS.t[hs, B0, SB + j * 64:SB + (j + 1) * 64], QF[hp].t[hs, j, :], MVB[hp].t[hs, j, :],
                           True, True, [MVB[hp].k(), QF[hp].k()], psk(B0))
                cp(WT[hp].t[:, :, :], r3(PS.t[:, B0, 0:SB], 64), psk(B0), [WT[hp].k()], eng="dve")
                cp(UT[hp].t[:, :, :], r3(PS.t[:, B0, SB:2 * SB], 64), psk(B0), [UT[hp].k()], eng="dve")

            def chain_step(sbi, hp, j):
                par = sbi % 2
                VB, AR, GC, BON = VBp[par], ARp[par], GCp[par], BONp[par]
                B2 = hp * 3 + 2
                hcur = HB[hp][hpar[hp]]
                hnxt = HB[hp][1 - hpar[hp]]
                hpar[hp] = 1 - hpar[hp]
                tok, ub = TOK[hp], UB[hp]
                for h in range(2):
                    hs = HS[h]
                    mm(PS.t[hs, B2, 0:64], WT[hp].t[hs, j, :], hcur.t[hs, :], True, True, [WT[hp].k(), hcur.k()],
                       psk(B2))
                tt(ub.t[:, :], PS.t[:, B2, 0:64], UT[hp].t[:, j, :], ALU.add, psk(B2) + [UT[hp].k()], [ub.k()])
                for h in range(2):
                    hs = HS[h]
                    mm(PS.t[hs, B2, 64:128], tok.t[hs, j, 2, :], tok.t[hs, j, 3, :], True, False, [tok.k()], psk(B2))
                for h in range(2):
                    hs = HS[h]
                    mm(PS.t[hs, B2, 64:128], tok.t[hs, j, 1, :], ub.t[hs, :], False, True, [tok.k(), ub.k()], psk(B2))
                for h in range(2):
                    hs = HS[h]
                    yo = PS.t[hs, B2, 128 + j * 64:128 + (j + 1) * 64]
                    mm(yo, hcur.t[hs, :], AR[hp].t[hs, j, 64:128], True, False, [hcur.k(), AR[hp].k()], psk(B2))
                for h in range(2):
                    hs = HS[h]
                    yo = PS.t[hs, B2, 128 + j * 64:128 + (j + 1) * 64]
                    mm(yo, ub.t[hs, :], NTMR[hp].t[hs, j, 64:128], False, False, [ub.k(), NTMR[hp].k()], psk(B2))
                for h in range(2):
                    hs = HS[h]
                    yo = PS.t[hs, B2, 128 + j * 64:128 + (j + 1) * 64]
                    mm(yo, tok.t[hs, j, 3, :], MKT[hp].t[hs, j, 64:128], False, True, [tok.k(), MKT[hp].k()], psk(B2))
                stt(H32[hp].t[:, :], H32[hp].t[:, :], GC[hp].t[:, j:j + 1], PS.t[:, B2, 64:128], ALU.mult, ALU.add,
                    [H32[hp].k(), GC[hp].k()] + psk(B2), [H32[hp].k()])
                cp(hnxt.t[:, :], H32[hp].t[:, :], [H32[hp].k()], [hnxt.k()], eng="act")

            def stage_GN(sbi, hp):
                par = sbi % 2
                VB, AR, GC, BON = VBp[par], ARp[par], GCp[par], BONp[par]
                B1, B2 = hp * 3 + 1, hp * 3 + 2
                tbx = (sbi * SB) // TB
                ssl = slice(sbi * SB, (sbi + 1) * SB)
                YS, YQ, MEANc, VARc = (GNT[hp][n] for n in ("YS", "YQ", "MEAN", "VAR"))
                cp(YS.t[:, :], PS.t[:, B2, 128:128 + SB], psk(B2), [YS.k()], eng="act")
                act(YQ.t[:, :], PS.t[:, B2, 128:128 + SB], AF.Square, psk(B2), [YQ.k()])
                yield
                mm(PS.t[:, B1, 0:SB], bones32, YS.t[:, :], True, True, KCF + [YS.k()], psk(B1))
                mm(PS.t[:, B1, SB:2 * SB], bones32, YQ.t[:, :], True, True, KCF + [YQ.k()], psk(B1))
                act(MEANc.t[:, :], PS.t[:, B1, 0:SB], AF.Identity, psk(B1), [MEANc.k()], scale=1.0 / 64)
                act(VARc.t[:, :], PS.t[:, B1, 0:SB], AF.Square, psk(B1), [VARc.k()], scale=1.0 / 64)
                stt(VARc.t[:, :], PS.t[:, B1, SB:2 * SB], 1.0 / 64, VARc.t[:, :], ALU.mult, ALU.subtract,
                    psk(B1) + [VARc.k()], [VARc.k()])
                yield
                rpow(VARc.t[:, :], VARc.t[:, :], -0.5, [VARc.k()] + KDV, [VARc.k()], bias=dv(DV_EPSGN))
                yield
                tt(YS.t[:, :], YS.t[:, :], MEANc.t[:, :], ALU.subtract, [YS.k(), MEANc.k()], [YS.k()])
                tt(YS.t[:, :], YS.t[:, :], VARc.t[:, :], ALU.mult, [YS.k(), VARc.k()], [YS.k()])
                yield
                act(YS.t[:, :], YS.t[:, :], AF.Identity, [YS.k()] + KPV, [YS.k()],
                    scale=pv(PV_GNW + hp), bias=pv(PV_GNB + hp))
                tt(YS.t[:, :], YS.t[:, :], BON[hp].t[:, :], ALU.add, [YS.k(), BON[hp].k()], [YS.k()])
                yv = Y.t[:, 2 + hp, ssl]
                tt(yv, yv, YS.t[:, :], ALU.mult, [YS.k(), Y.k(2 + hp, tbx)], [Y.k(2 + hp, tbx)])

            def warm(n, bank=0):
                if n <= 0:
                    return

                def fn(e):
                    ins = None
                    for _ in range(n):
                        ins = e.matmul(PS.t[:, bank, :], lhsT=CBT.t[:, CB_ID:CB_ID + 128],
                                       rhs=CBT.t[:, CB_MP:CB_MP + 512], start=True, stop=True)
                    return ins
                P.add("pe", fn, KCB, psk(bank), cost=0.3 * n)

            def gen_Eall(sbi):
                for _ in prologue(sbi):
                    yield
                alive = [stage_E(sbi, 0), stage_E(sbi, 1)]
                while alive:
                    for g_ in list(alive):
                        try:
                            next(g_)
                        except StopIteration:
                            alive.remove(g_)
                        yield

            def gen_PC(sbi):
                warm(WARM_BURST)
                for hp in range(2):
                    stage_P2(sbi, hp)
                    yield
                for hp in range(2):
                    stage_P1(sbi, hp)
                    yield
                for hp in range(2):
                    stage_P3(sbi, hp)
                    yield
                for r in range(6):
                    for hp in range(2):
                        stage_P4(sbi, hp, r)
                        warm(WARM_P4)
                        yield
                for hp in range(2):
                    stage_P5(sbi, hp)
                    yield
                for j in range(NCH):
                    for hp in range(2):
                        chain_step(sbi, hp, j)
                        warm(WARM_CH)
                        yield

            def run_interleaved(*gens):
                alive = list(gens)
                while alive:
                    for g_ in list(alive):
                        try:
                            next(g_)
                        except StopIteration:
                            alive.remove(g_)

            def gen_GN(sbi):
                alive = [stage_GN(sbi, 0), stage_GN(sbi, 1)]
                while alive:
                    for g_ in list(alive):
                        try:
                            next(g_)
                        except StopIteration:
                            alive.remove(g_)
                        yield

            run_interleaved(gen_Eall(0))
            gg = iter(())
            for sbi in range(NSB):
                ge = gen_Eall(sbi + 1) if sbi + 1 < NSB else iter(())
                for i_, _ in enumerate(gen_PC(sbi)):
                    if i_ < E_START:
                        for _k in range(3):
                            next(gg, None)
                    else:
                        if i_ == E_START:
                            run_interleaved(gg)
                        for _k in range(E_CHAIN if i_ >= 20 else 1):
                            next(ge, None)
                run_interleaved(ge)
                gg = gen_GN(sbi)
            run_interleaved(gg)
            for g in range(gbase + 23, gbase + 30):
                wdone(g)

        P.set_fence()
        with ExitStack() as ph:
          if stage >= 5:
            SQ = [sb("LSQ0", [128, TB], F32, ph), sb("LSQ1", [128, TB], F32, ph)]
            MEAN = sb("LMEAN", [128, TB], F32, ph)
            RSTD = sb("LRSTD", [128, TB], F32, ph)
            L1 = sb("L1", [128, TB], F32, ph)

            def outproj_tb(tb):
                for j in range(2):
                    wap, wk = wslot(gbase + 30 + j)
                    for mm_ in range(4):
                        m = 4 * j + mm_
                        b = next_bank()
                        for kc in range(2):
                            mm(PS.t[:, b, :], wap[:, kc * 4 + mm_, :], Y.t[:, 2 + kc, tb * TB:(tb + 1) * TB],
                               kc == 0, kc == 1, [wk, Y.k(2 + kc, tb)], psk(b))
                        xs = X32.t[:, m, tb * TB:(tb + 1) * TB]
                        tt(xs, xs, PS.t[:, b, :], ALU.add, psk(b) + [X32.k(m, tb)], [X32.k(m, tb)])

            def ln_tb(tb):
                sl = slice(tb * TB, (tb + 1) * TB)
                s1b, s2b = 2 + 2 * (tb % 2), 3 + 2 * (tb % 2)
                for m in range(KC):
                    mm(PS.t[:, s1b, :], ones32, X32.t[:, m, sl], m == 0, m == KC - 1, KCF + [X32.k(m, tb)], psk(s1b))
                for m in range(KC):
                    sq = SQ[m % 2]
                    act(sq.t[:, :], X32.t[:, m, sl], AF.Square, [X32.k(m, tb)], [sq.k()])
                    mm(PS.t[:, s2b, :], ones32, sq.t[:, :], m == 0, m == KC - 1, KCF + [sq.k()], psk(s2b))
                act(MEAN.t[:, :], PS.t[:, s1b, :], AF.Identity, psk(s1b), [MEAN.k()], scale=1.0 / D)
                act(L1.t[:, :], PS.t[:, s1b, :], AF.Square, psk(s1b), [L1.k()], scale=1.0 / D)
                stt(L1.t[:, :], PS.t[:, s2b, :], 1.0 / D, L1.t[:, :], ALU.mult, ALU.subtract, psk(s2b) + [L1.k()], [L1.k()])
                rpow(RSTD.t[:, :], L1.t[:, :], -0.5, [L1.k()] + KDV, [RSTD.k()], bias=dv(DV_EPSLN))
                tt(MEAN.t[:, :], MEAN.t[:, :], RSTD.t[:, :], ALU.mult, [MEAN.k(), RSTD.k()], [MEAN.k()])
                for m in range(KC):
                    xs = X32.t[:, m, sl]
                    tt(xs, xs, RSTD.t[:, :], ALU.mult, [X32.k(m, tb), RSTD.k()], [X32.k(m, tb)])
                    tt(xs, xs, MEAN.t[:, :], ALU.subtract, [X32.k(m, tb), MEAN.k()], [X32.k(m, tb)])
                    act(xs, xs, AF.Identity, [X32.k(m, tb)] + KPV, [X32.k(m, tb)],
                        scale=pv(PV_LNG + m), bias=pv(PV_LNB + m))
                    if li < NLY - 1:
                        cp(XBF.t[:, m, sl], xs, [X32.k(m, tb)], [XBF.k(m, tb)], eng="act")

            outproj_tb(0)
            for tb in range(NTB):
                if tb + 1 < NTB:
                    outproj_tb(tb + 1)
                ln_tb(tb)
            for j in range(2):
                wdone(gbase + 30 + j)

    outk = []
    for kc in range(KC):
        for tb in range(NTB):
            k = ("out", kc, tb)
            dma(y_out[kc * 128:(kc + 1) * 128, tb * TB:(tb + 1) * TB], X32.t[:, kc, tb * TB:(tb + 1) * TB],
                [X32.k(kc, tb)], [k])
            outk.append(k)
    P.add("sp", None, outk, (), cost=0.01)

    esems = {e: es.enter_context(nc.semaphore("sem_" + e)) for e in Prog.ENG}
    dsems = [es.enter_context(nc.semaphore("dsem%d" % i)) for i in range(NDSEM)]
    block = es.enter_context(nc.Block())
    P.emit(nc, block, esems, dsems)
    es.close()
    return nc, P


_CACHE = {}


def _get_program(layers):
    key = tuple(layers)
    if key not in _CACHE:
        _CACHE[key] = build(list(layers))
    return _CACHE[key]


def run_layers(x, inp, layers):
    ws = _prep_weights(inp)
    pvs, sws = _prep_small(inp)
    cb, cf = _consts()
    ls = list(layers)
    ws = np.ascontiguousarray(ws[ls])
    pvs = np.ascontiguousarray(pvs[ls])
    sws = np.ascontiguousarray(sws[ls])
    nc, _ = _get_program(ls)
    in_maps = []
    for b in range(8):
        in_maps.append({"xT": np.ascontiguousarray(x[b].T), "wst": ws, "pv": pvs, "sw": sws, "cb": cb, "cf": cf})
    res = run_bass_kernel_spmd(nc, in_maps, core_ids=list(range(8)))
    out = np.stack([np.asarray(res.results[b]["yT"]).T for b in range(8)])
    return np.ascontiguousarray(out.astype(np.float32))


def kernel(**inputs):
    x = np.asarray(inputs["x"], np.float32)
    return run_layers(x, inputs, range(DEPTH))
```

```python
import math
from contextlib import ExitStack

import numpy as np
import concourse.bass as bass
import concourse.mybir as mybir
from concourse.bass_utils import run_bass_kernel_spmd

F32 = mybir.dt.float32
BF16 = mybir.dt.bfloat16
AF = mybir.ActivationFunctionType
ALU = mybir.AluOpType

T = 2048
TB = 512
NTB = T // TB
D = 1024
KC = 8
DEPTH = 4
CH = 64
SB = 256
NCH = SB // CH
NSB = T // SB
NSLOT = 8
NWCH = 32
ALPHA = (2 * DEPTH) ** 0.25
LN_EPS = 1e-5
GN_EPS = 64e-5
CW = math.exp(-0.5)
NEG = -30000.0
NDSEM = 40

PV_LNG, PV_LNB, PV_SINK, PV_CONVW, PV_CONVB, PV_BA, PV_BX, PV_LAM = 0, 8, 16, 20, 28, 30, 32, 34
PV_MU, PV_MUCW, PV_W0, PV_A0, PV_KK, PV_KA, PV_RK, PV_GNW, PV_GNB, PV_V0 = 36, 42, 43, 45, 47, 49, 51, 53, 55, 57
NPV = 64
DV_OMM, DV_SE, DV_LC, DV_LC2, DV_OKA = 0, 7, 11, 13, 15
DV_NBA, DV_NBX, DV_NW0, DV_NA0, DV_NV0, DV_EPSLN, DV_EPSGN, DV_EPSKK = 17, 19, 21, 23, 25, 27, 28, 29
NDV = 32
SW_LRU, SW_WA2, SW_V1, SW_V2, NSW = 0, 512, 768, 800, 1056
CB_ID, CB_ID2, CB_MP, CB_MC, CB_RM, CB_ML, CB_BO, CB_CM, CB_ON, NCB = 0, 128, 192, 704, 1216, 1344, 1408, 1536, 2048, 2112


class Op:
    __slots__ = ("idx", "eng", "fn", "dma", "deps", "odeps", "signal", "semval", "dsem", "dval", "dguard",
                 "cost", "start", "finish", "pos", "barrier")


SCHED = True
PRIO_BLEV = True
BLEV_PE_W = 0.55
CSCALE = {"pe": 1.0, "act": 1.0, "dve": 1.0, "pool": 1.0, "sp": 1.0}
XLAT = 0.5


class Prog:
    ENG = ("pe", "act", "dve", "pool", "sp")

    def __init__(self):
        self.ops = []
        self.lastw = {}
        self.readers = {}
        self.fence = {}
        self.since_fence = {e: [] for e in self.ENG}

    def add(self, eng, fn, r=(), w=(), dma=False, cost=0.3):
        op = Op()
        op.idx = len(self.ops)
        op.eng = eng
        op.fn = fn
        op.dma = dma
        op.signal = False
        op.semval = 0
        op.cost = cost
        op.barrier = False
        deps = {}
        ps_r = [k for k in r if k[0] == "PS"]
        if ps_r:
            w = list(w) + [k for k in ps_r if k not in w]

        def dep(d):
            if d is not None:
                deps[d.idx] = d

        for k in r:
            dep(self.lastw.get(k))
        for k in w:
            if k not in self.lastw and k not in self.readers:
                for d in self.fence.values():
                    dep(d)
            dep(self.lastw.get(k))
            for d in self.readers.get(k, ()):
                dep(d)
        deps.pop(op.idx, None)
        op.odeps = list(deps.values())
        op.deps = [d for d in op.odeps if d.dma or dma or not (d.eng == eng and eng == "pe")]
        for k in w:
            self.lastw[k] = op
            self.readers[k] = []
        for k in r:
            self.readers.setdefault(k, []).append(op)
        self.ops.append(op)
        if not dma and fn is not None:
            for k in list(r) + list(w):
                if isinstance(k[0], str) and "_L" in k[0]:
                    self.since_fence[eng].append(op)
                    break
        return op

    def set_fence(self):
        for e in self.ENG:
            prev = self.since_fence[e]
            if not prev:
                continue
            b = Op()
            b.idx = len(self.ops)
            b.eng = e
            b.fn = None
            b.dma = False
            b.signal = False
            b.semval = 0
            b.cost = 0.0
            b.barrier = True
            b.odeps = list(prev)
            b.deps = []
            self.ops.append(b)
            self.fence[e] = b
            self.since_fence[e] = [b]

    def schedule(self):
        ops = self.ops
        per_eng = {e: [] for e in self.ENG}
        if not SCHED:
            for op in ops:
                op.start = float(op.idx)
                op.pos = len(per_eng[op.eng])
                per_eng[op.eng].append(op)
            return per_eng, list(ops)
        succ = [[] for _ in ops]
        indeg = [0] * len(ops)
        cst = [op.cost * (1.0 if op.dma else CSCALE[op.eng]) for op in ops]
        for op in ops:
            indeg[op.idx] = len(op.odeps)
            for d in op.odeps:
                succ[d.idx].append(op)
        blev = [0.0] * len(ops)
        for op in reversed(ops):
            b_ = 0.0
            for s_ in succ[op.idx]:
                v = blev[s_.idx] + (0.0 if (s_.eng == op.eng and not op.dma) else XLAT)
                if v > b_:
                    b_ = v
            blev[op.idx] = b_ + cst[op.idx] * (BLEV_PE_W if op.eng == "pe" else 1.0)
        ready = {e: [] for e in self.ENG}
        eng_t = {e: 0.0 for e in self.ENG}

        def push(op):
            rt = 0.0
            for d in op.odeps:
                t = d.finish + (0.0 if (d.eng == op.eng and not d.dma) else XLAT)
                if t > rt:
                    rt = t
            ready[op.eng].append((rt, (-blev[op.idx] if PRIO_BLEV else op.idx), op))

        for op in ops:
            if indeg[op.idx] == 0:
                push(op)
        order = []
        for _ in range(len(ops)):
            best = None
            for e in self.ENG:
                rl = ready[e]
                if not rl:
                    continue
                te = eng_t[e]
                c = min(rl, key=lambda x: (x[0] if x[0] > te else te, x[1]))
                st = c[0] if c[0] > te else te
                if best is None or (st, c[1]) < (best[0], best[1][1]):
                    best = (st, c, e)
            st, c, e = best
            ready[e].remove(c)
            op = c[2]
            op.start = st
            if op.dma:
                eng_t[e] = st + 0.06
                op.finish = st + op.cost
            else:
                op.finish = st + cst[op.idx]
                eng_t[e] = op.finish
            op.pos = len(per_eng[e])
            per_eng[e].append(op)
            order.append(op)
            for s_ in succ[op.idx]:
                indeg[s_.idx] -= 1
                if indeg[s_.idx] == 0:
                    push(s_)
        self.est_us = max(eng_t.values())
        return per_eng, order

    def emit(self, nc, block, esems, dsems):
        per_eng, order = self.schedule()
        def real_before(d):
            lst = per_eng[d.eng]
            i = d.pos - 1
            while i >= 0:
                c = lst[i]
                if c.fn is not None and not c.dma:
                    return c
                i -= 1
            return None

        for op in order:
            if op.barrier:
                op.deps = []
                continue
            lastdep = {}
            keep = []
            for d in op.deps:
                if d.barrier:
                    d = real_before(d)
                    if d is None:
                        continue
                if d.dma:
                    keep.append(d)
                else:
                    cur = lastdep.get(d.eng)
                    if cur is None or d.pos > cur.pos:
                        lastdep[d.eng] = d
            for d in lastdep.values():
                if d.eng == op.eng and d.pos > op.pos:
                    raise RuntimeError("scheduler order violation")
                d.signal = True
                keep.append(d)
            op.deps = keep
        cnt = {e: 0 for e in self.ENG}
        dval = [0] * len(dsems)
        rrs = {"pool": 0, "sp": 0}
        half = len(dsems) // 2
        for op in order:
            if op.dma:
                if op.eng == "pool":
                    s = rrs["pool"] % half
                    rrs["pool"] += 1
                else:
                    s = half + rrs["sp"] % (len(dsems) - half)
                    rrs["sp"] += 1
                op.dsem = s
                op.dguard = dval[s]
                dval[s] += 16
                op.dval = dval[s]
        for e in self.ENG:
            for op in per_eng[e]:
                if (not op.dma) and op.signal:
                    cnt[e] += 1
                    op.semval = cnt[e]
        self.stats = dict(cnt)

        def run(eng_name, e):
            seen = {}

            def wait(key, sem, val):
                if val <= 0 or seen.get(key, 0) >= val:
                    return
                seen[key] = val
                e.wait_ge(sem, val)

            for op in per_eng[eng_name]:
                for d in op.deps:
                    if d.dma:
                        wait(("d", d.dsem), dsems[d.dsem], d.dval)
                    else:
                        wait(("e", d.eng), esems[d.eng], d.semval)
                if op.dma:
                    wait(("d", op.dsem), dsems[op.dsem], op.dguard)
                if op.fn is None:
                    continue
                ins = op.fn(e)
                if op.dma:
                    ins.then_inc(dsems[op.dsem], 16)
                elif op.signal:
                    ins.then_inc(esems[op.eng], 1)

        @block.sync
        def _(e):
            run("sp", e)

        @block.gpsimd
        def _(e):
            run("pool", e)

        @block.tensor
        def _(e):
            run("pe", e)

        @block.scalar
        def _(e):
            run("act", e)

        @block.vector
        def _(e):
            run("dve", e)


class Tile:
    def __init__(self, t, name):
        self.t = t
        self.name = name

    def k(self, *idx):
        return (self.name,) + tuple(idx)


def _chunk_cols(w, cols):
    out = np.zeros((KC, 128, 128), np.float32)
    out[:, :, : len(cols)] = w[:, cols].reshape(KC, 128, len(cols))
    return np.ascontiguousarray(out.transpose(1, 0, 2))


def _prep_weights(inp):
    w_in = np.asarray(inp["w_in"], np.float32)
    w_out = np.asarray(inp["w_out"], np.float32)
    ws = np.zeros((DEPTH, NWCH, 128, KC, 128), np.float32)
    ar = np.arange
    for l in range(DEPTH):
        ch = []
        for cb in range(2):
            ch.append(_chunk_cols(w_in[l], 1536 + 128 * cb + ar(128)))
        for cb in range(2):
            ch.append(_chunk_cols(w_in[l], 1280 + 128 * cb + ar(128)))
        for c in range(4):
            ch.append(_chunk_cols(w_in[l], 128 * c + ar(128)))
        for g in range(2):
            kc_ = 512 + 64 * g + ar(64)
            ch.append(_chunk_cols(w_in[l], np.concatenate([kc_, kc_])))
        ch.append(_chunk_cols(w_in[l], 640 + ar(128)))

        def outp2(row0):
            res = []
            for j in range(2):
                o = np.zeros((128, KC, 128), np.float32)
                for kc in range(2):
                    for mm in range(4):
                        m = 4 * j + mm
                        o[:, kc * 4 + mm, :] = w_out[l][row0 + kc * 128:row0 + (kc + 1) * 128, m * 128:(m + 1) * 128]
                res.append(o)
            return res

        ch += outp2(512)
        for c in range(4):
            ch.append(_chunk_cols(w_in[l], 768 + 128 * c + ar(128)))
        for j in range(4):
            o = np.zeros((128, KC, 128), np.float32)
            for kc in range(4):
                for mm in range(2):
                    m = 2 * j + mm
                    o[:, kc * 2 + mm, :] = w_out[l][kc * 128:(kc + 1) * 128, m * 128:(m + 1) * 128]
            ch.append(o)
        for hp in range(2):
            ch.append(_chunk_cols(w_in[l], 2624 + 128 * hp + ar(128)))
        ch.append(_chunk_cols(w_in[l], 2560 + ar(64)))
        for hp in range(2):
            ch.append(_chunk_cols(w_in[l], 2304 + 128 * hp + ar(128)))
        for hp in range(2):
            ch.append(_chunk_cols(w_in[l], 1792 + 128 * hp + ar(128)))
            ch.append(_chunk_cols(w_in[l], 2048 + 128 * hp + ar(128)))
        ch += outp2(768)
        assert len(ch) == NWCH
        ws[l] = np.stack(ch)
    return ws


def _prep_small(inp):
    f = lambda k: np.asarray(inp[k], np.float32)
    pv = np.zeros((DEPTH, 128, NPV), np.float32)
    sw = np.zeros((DEPTH, 128, NSW), np.float32)
    p = np.arange(128)
    for l in range(DEPTH):
        for kc in range(KC):
            pv[l, :, PV_LNG + kc] = f("ln_g")[l, kc * 128 + p]
            pv[l, :, PV_LNB + kc] = f("ln_b")[l, kc * 128 + p]
        for c in range(4):
            pv[l, :, PV_SINK + c] = f("attn_sinks")[l, 2 * c + p // 64]
        for j in range(4):
            for cb in range(2):
                pv[l, :, PV_CONVW + j * 2 + cb] = f("conv_w")[l, j, cb * 128 + p]
        for cb in range(2):
            pv[l, :, PV_CONVB + cb] = f("conv_b")[l, cb * 128 + p]
            pv[l, :, PV_BA + cb] = f("lru_ba")[l, cb * 128 + p]
            pv[l, :, PV_BX + cb] = f("lru_bx")[l, cb * 128 + p]
            pv[l, :, PV_LAM + cb] = f("lru_lambda")[l, cb * 128 + p]
        mu = f("rwkv_mu")[l]
        for q in range(3):
            for hp in range(2):
                pv[l, :, PV_MU + q * 2 + hp] = mu[q * 256 + hp * 128 + p]
        pv[l, :64, PV_MUCW] = mu[768:832]
        for hp in range(2):
            s = hp * 128 + p
            pv[l, :, PV_W0 + hp] = f("rwkv_w0")[l, s]
            pv[l, :, PV_A0 + hp] = f("rwkv_a0")[l, s]
            pv[l, :, PV_KK + hp] = f("rwkv_kk")[l, s]
            pv[l, :, PV_KA + hp] = f("rwkv_ka")[l, s]
            pv[l, :, PV_RK + hp] = f("rwkv_rk")[l].reshape(256)[s]
            pv[l, :, PV_GNW + hp] = f("rwkv_gn_w")[l, s]
            pv[l, :, PV_GNB + hp] = f("rwkv_gn_b")[l, s]
            if l > 0:
                pv[l, :, PV_V0 + hp] = f("rwkv_v0")[l - 1, s]
        wa, wx = f("lru_wa")[l], f("lru_wx")[l]
        for ax, wmat in enumerate((wa, wx)):
            for cb in range(2):
                for i in range(2):
                    c0 = SW_LRU + (ax * 2 + cb) * 128 + i * 64
                    sw[l, i * 64:(i + 1) * 64, c0:c0 + 64] = wmat[2 * cb + i]
        sw[l, 0:32, SW_WA2:SW_WA2 + 256] = f("rwkv_w2")[l]
        sw[l, 32:64, SW_WA2:SW_WA2 + 256] = f("rwkv_a2")[l]
        if l > 0:
            v1 = f("rwkv_v1")[l - 1]
            for hp in range(2):
                sw[l, :, SW_V1 + hp * 16:SW_V1 + hp * 16 + 16] = v1[hp * 128:(hp + 1) * 128, :]
            sw[l, 0:16, SW_V2:SW_V2 + 256] = f("rwkv_v2")[l - 1]
    return pv, sw


def _consts():
    cb = np.zeros((128, NCB), np.float32)
    p = np.arange(128)[:, None]
    q = np.arange(128)[None, :]
    cb[:, CB_ID:CB_ID + 128] = (p == q)
    cb[:, CB_ID2:CB_ID2 + 64] = ((p % 64) == np.arange(64)[None, :])
    mp = np.where(q < p, 0.0, NEG)
    mc = np.where(q >= p, 0.0, NEG)
    cb[:, CB_MP:CB_MP + 512] = np.tile(mp, (1, 4))
    cb[:, CB_MC:CB_MC + 512] = np.tile(mc, (1, 4))
    s = (np.arange(128) % 64)[:, None]
    t = np.arange(64)[None, :]
    cb[:, CB_RM:CB_RM + 64] = (t > s)
    cb[:, CB_RM + 64:CB_RM + 128] = (t >= s)
    cb[:, CB_ML:CB_ML + 64] = (t < s)
    cb[:, CB_BO:CB_BO + 128] = ((p // 64) == (q // 64))
    cb[:, CB_CM:CB_CM + 512] = ((np.arange(512) % 64) != 0)[None, :]
    cb[:, CB_ON:CB_ON + 64] = 1.0
    cf = np.zeros((128, 256), np.float32)
    cf[:, 0:128] = ((p // 64) == (q // 64))
    cf[:, 128:256] = 1.0
    return cb, cf


E_START, E_CHAIN = 6, 2
WARM_P4, WARM_CH = 0, 0
WARM_BURST = 0


def build(layers, stage=9):
    nc = bass.Bass("TRN2", target_bir_lowering=False)
    NLY = len(layers)
    x_in = nc.dram_tensor("xT", [D, T], F32, kind="ExternalInput").ap()
    wst = nc.dram_tensor("wst", [NLY, NWCH, 128, KC, 128], F32, kind="ExternalInput").ap()
    pv_d = nc.dram_tensor("pv", [NLY, 128, NPV], F32, kind="ExternalInput").ap()
    sw_d = nc.dram_tensor("sw", [NLY, 128, NSW], F32, kind="ExternalInput").ap()
    cb_d = nc.dram_tensor("cb", [128, NCB], F32, kind="ExternalInput").ap()
    cf_d = nc.dram_tensor("cf", [128, 256], F32, kind="ExternalInput").ap()
    y_out = nc.dram_tensor("yT", [D, T], F32, kind="ExternalOutput").ap()

    P = Prog()
    es = ExitStack()

    lyr = {"i": 0}

    def sb(name, shape, dt, stack=None):
        if stack is None:
            stack = es
        else:
            name = "%s_L%d" % (name, lyr["i"])
        return Tile(stack.enter_context(nc.sbuf_tensor(name, shape, dt)), name)

    X32 = sb("X32", [128, KC, T], F32)
    XBF = sb("XBF", [128, KC, T], BF16)
    Y = sb("Y", [128, 4, T], BF16)
    VF = sb("VF", [128, 2, T], BF16)
    WR = sb("WR", [128, NSLOT, KC, 128], BF16)
    PVT = sb("PVT", [128, NLY, NPV], F32)
    DVT = sb("DVT", [128, NDV], F32)
    SMW = sb("SMW", [128, NSW], BF16)
    CBT = sb("CBT", [128, NCB], BF16)
    CFT = sb("CFT", [128, 256], F32)
    PS = Tile(es.enter_context(nc.psum_tensor("PS", [128, 8, 512], F32)), "PS")

    ident = CBT.t[:, CB_ID:CB_ID + 128]
    bones_bf = CBT.t[:, CB_BO:CB_BO + 128]
    ones_bf64 = CBT.t[:, CB_ON:CB_ON + 64]
    bones32 = CFT.t[:, 0:128]
    ones32 = CFT.t[:, 128:256]
    KCB = [CBT.k()]
    KCF = [CFT.k()]
    KPV = [PVT.k()]
    KDV = [DVT.k()]
    KSW = [SMW.k()]

    def fs(ap):
        n = 1
        for d_ in ap.shape[1:]:
            n *= int(d_)
        return n

    def mm(out, lhsT, rhs, start, stop, r, w, tp=None):
        if tp is None:
            fn = lambda e: e.matmul(out, lhsT=lhsT, rhs=rhs, start=start, stop=stop)
        else:
            fn = lambda e: e.matmul(out, lhsT=lhsT, rhs=rhs, start=start, stop=stop, tile_position=tp)
        m_, n_ = fs(lhsT), fs(rhs)
        if lhsT.dtype == F32:
            c = 0.06 + n_ / 700.0
        elif m_ <= 64 and n_ <= 128:
            c = 0.03 + n_ / 6000.0
        elif n_ >= 512:
            c = 0.225
        else:
            c = 0.04 + n_ / 1200.0
        P.add("pe", fn, r, w, cost=c)

    def act(out, in_, func, r, w, scale=None, bias=None):
        kw = {}
        if scale is not None:
            kw["scale"] = scale
        if bias is not None:
            kw["bias"] = bias
        P.add("act", lambda e: e.activation(out=out, in_=in_, func=func, **kw), r, w, cost=0.22 + fs(out) / 1100.0)

    def ecost(eng, out):
        if eng == "pool":
            return 0.3 + fs(out) / 450.0
        return 0.12 + fs(out) / 900.0

    def tt(out, in0, in1, op, r, w, eng="dve"):
        P.add(eng, lambda e: e.tensor_tensor(out=out, in0=in0, in1=in1, op=op), r, w, cost=ecost(eng, out))

    def ts(out, in0, s1, s2, op0, op1, r, w, eng="dve"):
        P.add(eng, lambda e: e.tensor_scalar(out=out, in0=in0, scalar1=s1, scalar2=s2, op0=op0, op1=op1), r, w,
              cost=ecost(eng, out))

    def stt(out, in0, sc, in1, op0, op1, r, w):
        P.add("dve", lambda e: e.scalar_tensor_tensor(out=out, in0=in0, scalar=sc, in1=in1, op0=op0, op1=op1), r, w,
              cost=ecost("dve", out))

    def cp(out, in_, r, w, eng="dve"):
        if eng == "act":
            P.add("act", lambda e: e.activation(out=out, in_=in_, func=AF.Copy), r, w, cost=0.22 + fs(out) / 1100.0)
        else:
            P.add(eng, lambda e: e.tensor_copy(out=out, in_=in_), r, w, cost=ecost(eng, out))

    def scan(out, d0, d1, init, r, w):
        P.add("dve", lambda e: e.tensor_tensor_scan(out=out, data0=d0, data1=d1, initial=init,
                                                    op0=ALU.mult, op1=ALU.add), r, w, cost=0.1 + fs(out) / 450.0)

    def recip(out, in_, r, w):
        P.add("dve", lambda e: e.reciprocal(out=out, in_=in_), r, w, cost=0.1 + fs(out) / 120.0)

    def memset(ap, val, w, eng="dve"):
        P.add(eng, lambda e: e.memset(ap, val), (), w, cost=ecost(eng, ap))

    def dma(out, in_, r, w, q="sp"):
        P.add(q, lambda e: e.dma_start(out=out, in_=in_), r, w, dma=True, cost=2.5)

    one_ap = CFT.t[:, 128:129]

    def sigm(out, in_, negbias, r, w):
        act(out, in_, AF.Exp, r, w, scale=-1.0, bias=negbias)
        act(out, out, AF.Ln, list(w) + KCF, w, bias=one_ap)
        act(out, out, AF.Exp, w, w, scale=-1.0)

    def rpow(out, in_, p, r, w, bias=None):
        act(out, in_, AF.Ln, r, w, bias=bias)
        act(out, out, AF.Exp, w, w, scale=p)

    def psk(*banks):
        return [PS.k(b) for b in banks]

    def r3(ap, inner):
        return ap.rearrange("p (a b) -> p a b", b=inner)

    wstate = {"loaded": 0}
    total_chunks = NLY * NWCH

    def wload_upto(n):
        while wstate["loaded"] < min(n, total_chunks):
            g = wstate["loaded"]
            li_, i = divmod(g, NWCH)
            s = g % NSLOT
            dma(WR.t[:, s, :, :], wst[li_, i, :, :, :], (), [WR.k(s)], q="pool")
            wstate["loaded"] += 1

    def wdone(g):
        wload_upto(g + 1 + NSLOT)

    def wslot(g):
        s = g % NSLOT
        return WR.t[:, s, :, :], WR.k(s)

    dma(CBT.t[:, :], cb_d[:, :], (), KCB, q="pool")
    dma(CFT.t[:, :], cf_d[:, :], (), KCF)
    dma(PVT.t[:, :, :], pv_d.rearrange("l p c -> p l c"), (), KPV)
    for tb in range(NTB):
        for kc in range(KC):
            dma(X32.t[:, kc, tb * TB:(tb + 1) * TB], x_in[kc * 128:(kc + 1) * 128, tb * TB:(tb + 1) * TB],
                (), [X32.k(kc, tb)])
    wload_upto(NSLOT)
    for tb in range(NTB):
        for kc in range(KC):
            sl = slice(tb * TB, (tb + 1) * TB)
            cp(XBF.t[:, kc, sl], X32.t[:, kc, sl], [X32.k(kc, tb)], [XBF.k(kc, tb)],
               eng="act" if (kc + tb) % 2 == 0 else "dve")

    pbank = {"i": 0}

    def next_bank(cands=(0, 1, 6, 7)):
        b = cands[pbank["i"] % len(cands)]
        pbank["i"] += 1
        return b

    def proj(g, tb, bank):
        wap, wk = wslot(g)
        for kc in range(KC):
            mm(PS.t[:, bank, :], wap[:, kc, :], XBF.t[:, kc, tb * TB:(tb + 1) * TB],
               kc == 0, kc == KC - 1, [wk, XBF.k(kc, tb)], psk(bank))

    for li, lay in enumerate(layers):
        gbase = li * NWCH
        lyr["i"] = li

        def pv(col, n=1, rows=slice(0, 128)):
            return PVT.t[rows, li, col:col + n]

        def dv(col, n=1, rows=slice(0, 128)):
            return DVT.t[rows, col:col + n]

        dma(SMW.t[:, :], sw_d[li, :, :], (), KSW, q="pool")
        ts(dv(DV_OMM, 7), pv(PV_MU, 7), -1.0, 1.0, ALU.mult, ALU.add, KPV, KDV)
        act(dv(DV_SE, 4), pv(PV_SINK, 4), AF.Exp, KPV, KDV)
        act(dv(DV_LC, 2), pv(PV_LAM, 2), AF.Exp, KPV, KDV, scale=-1.0)
        ts(dv(DV_LC, 2), dv(DV_LC, 2), 1.0, None, ALU.add, ALU.bypass, KDV, KDV)
        act(dv(DV_LC, 2), dv(DV_LC, 2), AF.Ln, KDV, KDV)
        ts(dv(DV_LC2, 2), dv(DV_LC, 2), -16.0, None, ALU.mult, ALU.bypass, KDV, KDV)
        ts(dv(DV_LC, 2), dv(DV_LC, 2), -8.0, None, ALU.mult, ALU.bypass, KDV, KDV)
        ts(dv(DV_OKA, 2), pv(PV_KA, 2), -1.0, 1.0, ALU.mult, ALU.add, KPV, KDV)
        for dcol, pcol in ((DV_NBA, PV_BA), (DV_NBX, PV_BX), (DV_NW0, PV_W0), (DV_NA0, PV_A0), (DV_NV0, PV_V0)):
            ts(dv(dcol, 2), pv(pcol, 2), -1.0, None, ALU.mult, ALU.bypass, KPV, KDV)
        memset(dv(DV_EPSLN), LN_EPS, KDV)
        memset(dv(DV_EPSGN), GN_EPS, KDV)
        memset(dv(DV_EPSKK), 1e-12, KDV)

        P.set_fence()
        with ExitStack() as pha:
          if stage >= 1:
            QT = sb("QT", [128, 4, T], BF16, pha)
            KD = sb("KD", [128, 2, T], BF16, pha)
            VT = sb("VT", [128, 16, 128], BF16, pha)
            with ExitStack() as ph:
                XBs = [sb("XB", [128, T + 4], F32, ph)] * 2
                XCs = [sb("XC", [128, T], F32, ph)] * 2
                XCBs = [sb("XCB", [128, T], BF16, ph)] * 2
                BRs = [sb("BR", [128, TB], F32, ph)] * 2
                BIs = [sb("BI", [128, TB], F32, ph)] * 2
                BMs = [sb("BM", [128, TB], F32, ph)] * 2
                BHs = [[sb("BH%d" % i, [128, TB], F32, ph) for i in range(2)]] * 2
                memset(XBs[0].t[:, 0:4], 0.0, [XBs[0].k("pad")])
                for cb in range(2):
                    g = gbase + 0 + cb
                    for tb in range(NTB):
                        b = next_bank()
                        proj(g, tb, b)
                        act(Y.t[:, cb, tb * TB:(tb + 1) * TB], PS.t[:, b, :], AF.Silu, psk(b), [Y.k(cb, tb)])
                    wdone(g)
                def gen_B(cb):
                    XB, XC, XCB, BR, BI, BM, BH = XBs[cb], XCs[cb], XCBs[cb], BRs[cb], BIs[cb], BMs[cb], BHs[cb]
                    gb = 2 + 2 * cb
                    for tb in range(NTB):
                        rk = [XB.k(tb)] + ([XB.k(tb - 1)] if tb > 0 else [XB.k("pad")])
                        o = XC.t[:, tb * TB:(tb + 1) * TB]
                        ts(o, XB.t[:, 4 + tb * TB:4 + (tb + 1) * TB], pv(PV_CONVW + 3 * 2 + cb), pv(PV_CONVB + cb),
                           ALU.mult, ALU.add, rk + KPV, [XC.k(tb)])
                        for j in range(3):
                            s0 = 4 + tb * TB - 3 + j
                            stt(o, XB.t[:, s0:s0 + TB], pv(PV_CONVW + j * 2 + cb), o, ALU.mult, ALU.add,
                                rk + KPV + [XC.k(tb)], [XC.k(tb)])
                        cp(XCB.t[:, tb * TB:(tb + 1) * TB], o, [XC.k(tb)], [XCB.k(tb)], eng="pool")
                        yield
                    for tb in range(NTB):
                        sl = slice(tb * TB, (tb + 1) * TB)
                        ca = SW_LRU + (0 * 2 + cb) * 128
                        cx = SW_LRU + (1 * 2 + cb) * 128
                        mm(PS.t[:, gb, :], SMW.t[:, ca:ca + 128], XCB.t[:, sl], True, True, KSW + [XCB.k(tb)], psk(gb))
                        mm(PS.t[:, gb + 1, :], SMW.t[:, cx:cx + 128], XCB.t[:, sl], True, True, KSW + [XCB.k(tb)],
                           psk(gb + 1))
                        sigm(BR.t[:, :], PS.t[:, gb, :], dv(DV_NBA + cb), psk(gb) + KDV, [BR.k()])
                        yield
                        sigm(BI.t[:, :], PS.t[:, gb + 1, :], dv(DV_NBX + cb), psk(gb + 1) + KDV, [BI.k()])
                        yield
                        act(BM.t[:, :], BR.t[:, :], AF.Exp, [BR.k()] + KDV, [BM.k()], scale=dv(DV_LC2 + cb))
                        act(BR.t[:, :], BR.t[:, :], AF.Exp, [BR.k()] + KDV, [BR.k()], scale=dv(DV_LC + cb))
                        yield
                        act(BM.t[:, :], BM.t[:, :], AF.Ln, [BM.k()] + KCF, [BM.k()], scale=-1.0, bias=one_ap)
                        act(BM.t[:, :], BM.t[:, :], AF.Exp, [BM.k()], [BM.k()], scale=0.5)
                        tt(BI.t[:, :], BI.t[:, :], XC.t[:, sl], ALU.mult, [BI.k(), XC.k(tb)], [BI.k()])
                        yield
                        tt(BI.t[:, :], BI.t[:, :], BM.t[:, :], ALU.mult, [BI.k(), BM.k()], [BI.k()])
                        hcur, hprev = BH[tb % 2], BH[(tb + 1) % 2]
                        init = 0.0 if tb == 0 else hprev.t[:, TB - 1:TB]
                        scan(hcur.t[:, :], BR.t[:, :], BI.t[:, :], init,
                             [BR.k(), BI.k()] + ([hprev.k()] if tb > 0 else []), [hcur.k()])
                        yv = Y.t[:, cb, sl]
                        tt(yv, yv, hcur.t[:, :], ALU.mult, [hcur.k(), Y.k(cb, tb)], [Y.k(cb, tb)])
                        yield

                for cb in range(2):
                    g = gbase + 2 + cb
                    XB = XBs[cb]
                    for tb in range(NTB):
                        b = next_bank()
                        proj(g, tb, b)
                        cp(XB.t[:, 4 + tb * TB:4 + (tb + 1) * TB], PS.t[:, b, :], psk(b), [XB.k(tb)],
                           eng="dve" if tb % 2 == 0 else "act")
                    wdone(g)
                    for _ in gen_B(cb):
                        pass

                for c in range(4):
                    g = gbase + 4 + c
                    for tb in range(NTB):
                        b = next_bank()
                        proj(g, tb, b)
                        cp(QT.t[:, c, tb * TB:(tb + 1) * TB], PS.t[:, b, :], psk(b), [QT.k(c, tb)],
                           eng="act" if tb % 2 == 0 else "dve")
                    wdone(g)
                for gi in range(2):
                    g = gbase + 8 + gi
                    for tb in range(NTB):
                        b = next_bank()
                        proj(g, tb, b)
                        cp(KD.t[:, gi, tb * TB:(tb + 1) * TB], PS.t[:, b, :], psk(b), [KD.k(gi, tb)],
                           eng="act" if tb % 2 == 0 else "dve")
                    wdone(g)
                g = gbase + 10
                wap, wk = wslot(g)
                for tb in range(NTB):
                    b = next_bank()
                    for nn in range(4):
                        n = tb * 4 + nn
                        for kc in range(KC):
                            mm(PS.t[:, b, nn * 128:(nn + 1) * 128], XBF.t[:, kc, n * 128:(n + 1) * 128], wap[:, kc, :],
                               kc == 0, kc == KC - 1, [wk, XBF.k(kc, tb)], psk(b))
                    cp(VT.t[:, tb * 4:(tb + 1) * 4, :], r3(PS.t[:, b, :], 128), psk(b), [VT.k(tb)], eng="act")
                wdone(g)
            P.set_fence()
            for j in (range(2) if stage >= 2 else ()):
                g = gbase + 11 + j
                wap, wk = wslot(g)
                for mm_ in range(4):
                    m = 4 * j + mm_
                    for tb in range(NTB):
                        b = next_bank()
                        for kc in range(2):
                            mm(PS.t[:, b, :], wap[:, kc * 4 + mm_, :], Y.t[:, kc, tb * TB:(tb + 1) * TB],
                               kc == 0, kc == 1, [wk, Y.k(kc, tb)], psk(b))
                        xs = X32.t[:, m, tb * TB:(tb + 1) * TB]
                        stt(xs, xs, ALPHA, PS.t[:, b, :], ALU.mult, ALU.add, psk(b) + [X32.k(m, tb)], [X32.k(m, tb)])
                wdone(g)
            with ExitStack() as ph:
                PT = sb("PT", [128, 2, 2, 1024], BF16, ph)
                A1 = sb("A1", [128, 512], F32, ph)
                A2 = sb("A2", [128, 512], F32, ph)
                A3 = sb("A3", [128, 512], F32, ph)
                for c in range(4):
                    g = gbase + 13 + c
                    for tb in range(NTB):
                        b = next_bank()
                        proj(g, tb, b)
                        act(Y.t[:, c, tb * TB:(tb + 1) * TB], PS.t[:, b, :], AF.Silu, psk(b), [Y.k(c, tb)])
                    wdone(g)
                def attn_scores(n):
                    tbq = n // 4
                    jlist = [(0, n - 1), (1, n)] if n > 0 else [(1, n)]
                    for jj, kb in jlist:
                        tbk = kb // 4
                        mcol = CB_MP if jj == 0 else CB_MC
                        for par in range(2):
                            bank = 2 + jj * 2 + par
                            mm(PS.t[:, bank, :], ident, CBT.t[:, mcol:mcol + 512], True, False, KCB, psk(bank))
                        for gq in range(2):
                            for cg in range(2):
                                for par in range(2):
                                    bank = 2 + jj * 2 + par
                                    c = gq * 2 + cg
                                    pos = gq * 2 + cg
                                    mm(PS.t[:, bank, pos * 128:(pos + 1) * 128],
                                       KD.t[par * 64:(par + 1) * 64, gq, kb * 128:(kb + 1) * 128],
                                       QT.t[par * 64:(par + 1) * 64, c, n * 128:(n + 1) * 128],
                                       False, (gq == 1 and cg == 1), [KD.k(gq, tbk), QT.k(c, tbq)], psk(bank))
                        b0 = 2 + jj * 2
                        act(PT.t[:, n % 2, jj, :], PS.t[:, b0:b0 + 2, :].rearrange("p a b -> p (a b)"), AF.Exp,
                            psk(b0, b0 + 1), [PT.k(n % 2, jj)], scale=0.125)

                def attn_pv(n):
                    tbq = n // 4
                    jlist = [(0, n - 1), (1, n)] if n > 0 else [(1, n)]
                    for (obank, use_ones) in ((6, False), (7, True)):
                        for gq in range(2):
                            for par in range(2):
                                for ji, (jj, kb) in enumerate(jlist):
                                    if use_ones:
                                        lhsT, rk = ones_bf64, KCB
                                    else:
                                        lhsT, rk = VT.t[:, kb, gq * 64:(gq + 1) * 64], [VT.k(kb // 4)]
                                    c0 = par * 512 + gq * 256
                                    mm(PS.t[par * 64:(par + 1) * 64, obank, gq * 256:(gq + 1) * 256],
                                       lhsT, PT.t[:, n % 2, jj, c0:c0 + 256],
                                       ji == 0, ji == len(jlist) - 1, rk + [PT.k(n % 2, jj)], psk(obank),
                                       tp=(0, 64 * par))
                    tt(r3(A1.t[:, :], 128), r3(PS.t[:, 7, :], 128),
                       dv(DV_SE, 4).unsqueeze(2).to_broadcast([128, 4, 128]), ALU.add, psk(7) + KDV, [A1.k()])
                    rpow(A2.t[:, :], A1.t[:, :], -1.0, [A1.k()], [A2.k()])
                    tt(A3.t[:, :], PS.t[:, 6, :], A2.t[:, :], ALU.mult, psk(6) + [A2.k()], [A3.k()])
                    yv = Y.t[:, 0:4, n * 128:(n + 1) * 128]
                    yk = [Y.k(c, tbq) for c in range(4)]
                    tt(yv, yv, r3(A3.t[:, :], 128), ALU.mult, [A3.k()] + yk, yk)

                attn_scores(0)
                for n in range(16):
                    if n + 1 < 16:
                        attn_scores(n + 1)
                    attn_pv(n)
        for j in (range(4) if stage >= 2 else ()):
            g = gbase + 17 + j
            wap, wk = wslot(g)
            for mm_ in range(2):
                m = 2 * j + mm_
                for tb in range(NTB):
                    b = next_bank()
                    for kc in range(4):
                        mm(PS.t[:, b, :], wap[:, kc * 2 + mm_, :], Y.t[:, kc, tb * TB:(tb + 1) * TB],
                           kc == 0, kc == 3, [wk, Y.k(kc, tb)], psk(b))
                    xs = X32.t[:, m, tb * TB:(tb + 1) * TB]
                    tt(xs, xs, PS.t[:, b, :], ALU.add, psk(b) + [X32.k(m, tb)], [X32.k(m, tb)])
            wdone(g)


        P.set_fence()
        with ExitStack() as ph:
          if stage >= 4:
            def f32t(name):
                return sb(name, [128, SB], F32, ph)

            def bft(name, shape):
                return sb(name, shape, BF16, ph)

            two = range(2)
            ET = [{n: f32t("c%s%d" % (n, h)) for n in ("TMP", "RS", "KS", "SG", "GG", "AA", "EG", "ENG", "EGM")}
                  for h in two]
            PTMP, PT1 = f32t("cPTMP"), f32t("cPT1")
            GNT = [{"YS": ET[h]["SG"], "YQ": ET[h]["GG"], "MEAN": ET[h]["AA"], "VAR": ET[h]["TMP"]} for h in two]
            VS = [f32t("cVS0"), f32t("cVS1")]
            XSW = sb("cXSW", [64, SB], F32, ph)
            TWAL = bft("cTWAL", [64, SB])
            VBp = [bft("cVB%d" % i, [128, 2, SB]) for i in two]
            VSB = bft("cVSB", [128, 2, SB])
            P1B = bft("cP1B", [16, SB])
            SQBh = [bft("cSQB%d" % h, [128, SB]) for h in two]
            RKBh = [bft("cRKB%d" % h, [128, SB]) for h in two]
            CAR = sb("cCAR", [128, 8], F32, ph)
            BONp = [[f32t("cBON%d%d" % (i, h)) for h in two] for i in two]
            ARp = [[bft("cAR%d%d" % (i, h), [128, NCH, 128]) for h in two] for i in two]
            BT = [bft("cBT%d" % h, [128, SB]) for h in two]
            KT = [bft("cKT%d" % h, [128, SB]) for h in two]
            BHt = [bft("cBHt%d" % h, [128, SB]) for h in two]
            KHt = [bft("cKHt%d" % h, [128, SB]) for h in two]
            NTMR = [bft("cNTMR%d" % h, [128, NCH, 128]) for h in two]
            MKT = [bft("cMKT%d" % h, [128, NCH, 128]) for h in two]
            NK = [[bft("cNK%d%d" % (h, i), [128, NCH, 64]) for i in two] for h in two]
            NTQ = [[bft("cNTQ%d%d" % (h, i), [128, NCH, 128]) for i in two] for h in two]
            QF = [bft("cQF%d" % h, [128, NCH, 64]) for h in two]
            TOK = [bft("cTOK%d" % h, [128, NCH, 4, 64]) for h in two]
            MVB = [bft("cMVB%d" % h, [128, NCH, 64]) for h in two]
            WT = [bft("cWT%d" % h, [128, NCH, 64]) for h in two]
            UT = [sb("cUT%d" % h, [128, NCH, 64], F32, ph) for h in two]
            UB = [bft("cUB%d" % h, [128, 64]) for h in two]
            HB = [[bft("cHB%d%d" % (h, i), [128, 64]) for i in two] for h in two]
            H32 = [sb("cH32%d" % h, [128, 64], F32, ph) for h in two]
            GCp = [[sb("cGC%d%d" % (i, h), [128, NCH], F32, ph) for h in two] for i in two]
            hpar = [0, 0]

            for hp in range(2):
                g = gbase + 21 + hp
                for tb in range(NTB):
                    b = next_bank()
                    proj(g, tb, b)
                    act(Y.t[:, 2 + hp, tb * TB:(tb + 1) * TB], PS.t[:, b, :], AF.Silu, psk(b), [Y.k(2 + hp, tb)])
                wdone(g)
            g_cw = gbase + 23
            g_v = [gbase + 24, gbase + 25]
            g_r = [gbase + 26, gbase + 28]
            g_k = [gbase + 27, gbase + 29]
            for hp in range(2):
                memset(H32[hp].t[:, :], 0.0, [H32[hp].k()])
                memset(HB[hp][0].t[:, :], 0.0, [HB[hp][0].k()])
            memset(CAR.t[:, :], 0.0, [CAR.k(c) for c in range(8)])

            def projC(g, sbi, bank, col0, rows=128):
                wap_, wk_ = wslot(g)
                tbx_ = (sbi * SB) // TB
                for kc in range(KC):
                    mm(PS.t[0:rows, bank, col0:col0 + SB], wap_[:, kc, 0:rows], XBF.t[:, kc, sbi * SB:(sbi + 1) * SB],
                       kc == 0, kc == KC - 1, [wk_, XBF.k(kc, tbx_)], psk(bank))

            def tshift(bank, col0, rows, mucol, ommcol, carcol, out, TMP):
                rs = slice(0, rows)
                src = PS.t[rs, bank, col0:col0 + SB]
                act(TMP.t[rs, :], src, AF.Identity, psk(bank) + KDV, [TMP.k()], scale=dv(ommcol, 1, rs))
                stt(out.t[rs, 1:SB], PS.t[rs, bank, col0:col0 + SB - 1], pv(mucol, 1, rs), TMP.t[rs, 1:SB],
                    ALU.mult, ALU.add, psk(bank) + KPV + [TMP.k()], [out.k()])
                stt(out.t[rs, 0:1], CAR.t[rs, carcol:carcol + 1], pv(mucol, 1, rs), TMP.t[rs, 0:1],
                    ALU.mult, ALU.add, [CAR.k(carcol)] + KPV + [TMP.k()], [out.k()])
                cp(CAR.t[rs, carcol:carcol + 1], PS.t[rs, bank, col0 + SB - 1:col0 + SB], psk(bank), [CAR.k(carcol)])

            rmaskb = CBT.t[:, CB_RM:CB_RM + 128].unsqueeze(1).to_broadcast([128, NCH, 128])
            masklb = CBT.t[:, CB_ML:CB_ML + 64].unsqueeze(1).to_broadcast([128, NCH, 64])
            ident2 = CBT.t[:, CB_ID2:CB_ID2 + 64]
            ident2b = ident2.unsqueeze(1).to_broadcast([128, NCH, 64])
            HS = [slice(0, 64), slice(64, 128)]

            def jh():
                for j in range(NCH):
                    for h in range(2):
                        yield j, h, HS[h], slice(j * CH, (j + 1) * CH)

            def prologue(sbi):
                tbx = (sbi * SB) // TB
                ssl = slice(sbi * SB, (sbi + 1) * SB)
                VB = VBp[sbi % 2]
                TMP, T1 = PTMP, PT1
                projC(g_cw, sbi, 6, 0, rows=64)
                tshift(6, 0, 64, PV_MUCW, DV_OMM + 6, 6, XSW, TMP)
                yield
                act(XSW.t[0:32, :], XSW.t[0:32, :], AF.Exp, [XSW.k()], [XSW.k()], scale=-2.0)
                act(XSW.t[0:32, :], XSW.t[0:32, :], AF.Ln, [XSW.k()] + KCF, [XSW.k()], bias=one_ap[0:32, :])
                act(XSW.t[0:32, :], XSW.t[0:32, :], AF.Exp, [XSW.k()], [XSW.k()], scale=-1.0)
                ts(TWAL.t[0:32, :], XSW.t[0:32, :], 2.0, -1.0, ALU.mult, ALU.add, [XSW.k()], [TWAL.k()])
                cp(TWAL.t[32:64, :], XSW.t[32:64, :], [XSW.k()], [TWAL.k()], eng="pool")
                yield
                for hp in range(2):
                    projC(g_v[hp], sbi, 7, hp * SB)
                    tshift(7, hp * SB, 128, PV_MU + 4 + hp, DV_OMM + 4 + hp, 4 + hp, VS[hp], TMP)
                    yield
                if lay == 0:
                    for hp in range(2):
                        cp(VB.t[:, hp, :], VS[hp].t[:, :], [VS[hp].k()], [VB.k(hp)], eng="pool")
                        cp(VF.t[:, hp, ssl], VS[hp].t[:, :], [VS[hp].k()], [VF.k(hp, tbx)], eng="pool")
                else:
                    for hp in range(2):
                        cp(VSB.t[:, hp, :], VS[hp].t[:, :], [VS[hp].k()], [VSB.k(hp)], eng="pool")
                    for hp in range(2):
                        mm(PS.t[0:16, 6, 256:512], SMW.t[:, SW_V1 + hp * 16:SW_V1 + hp * 16 + 16], VSB.t[:, hp, :],
                           hp == 0, hp == 1, KSW + [VSB.k(hp)], psk(6))
                    cp(P1B.t[:, :], PS.t[0:16, 6, 256:512], psk(6), [P1B.k()], eng="act")
                    for hp in range(2):
                        mm(PS.t[:, 6, 0:SB], SMW.t[0:16, SW_V2 + hp * 128:SW_V2 + hp * 128 + 128], P1B.t[0:16, :],
                           True, True, KSW + [P1B.k()], psk(6))
                        sigm(T1.t[:, :], PS.t[:, 6, 0:SB], dv(DV_NV0 + hp), psk(6) + KDV, [T1.k()])
                        tt(TMP.t[:, :], VF.t[:, hp, ssl], VS[hp].t[:, :], ALU.subtract,
                           [VF.k(hp, tbx), VS[hp].k()], [TMP.k()])
                        tt(TMP.t[:, :], TMP.t[:, :], T1.t[:, :], ALU.mult, [TMP.k(), T1.k()], [TMP.k()])
                        tt(VB.t[:, hp, :], VS[hp].t[:, :], TMP.t[:, :], ALU.add, [VS[hp].k(), TMP.k()], [VB.k(hp)])
                        yield

            def stage_E(sbi, hp):
                par = sbi % 2
                VB, AR, GC, BON = VBp[par], ARp[par], GCp[par], BONp[par]
                SQB, RKB = SQBh[hp], RKBh[hp]
                e_ = ET[hp]
                TMP, RS, KS, SG, GG, AA, EG, ENG, EGM = (e_[n] for n in
                                                         ("TMP", "RS", "KS", "SG", "GG", "AA", "EG", "ENG", "EGM"))
                RN, T1, BB, KK = TMP, TMP, SG, GG
                if True:
                    B0 = B1 = B2 = 6 + hp
                    projC(g_r[hp], sbi, B0, 0)
                    projC(g_k[hp], sbi, B0, SB)
                    tshift(B0, 0, 128, PV_MU + 0 + hp, DV_OMM + 0 + hp, 0 + hp, RS, TMP)
                    tshift(B0, SB, 128, PV_MU + 2 + hp, DV_OMM + 2 + hp, 2 + hp, KS, TMP)
                    yield
                    cz = SW_WA2 + hp * 128
                    mm(PS.t[:, B1, 0:SB], SMW.t[0:32, cz:cz + 128], TWAL.t[0:32, :], True, True, KSW + [TWAL.k()], psk(B1))
                    sigm(SG.t[:, :], PS.t[:, B1, 0:SB], dv(DV_NW0 + hp), psk(B1) + KDV, [SG.k()])
                    mm(PS.t[:, B2, 0:SB], SMW.t[32:64, cz:cz + 128], TWAL.t[32:64, :], True, True,
                       KSW + [TWAL.k()], psk(B2))
                    sigm(AA.t[:, :], PS.t[:, B2, 0:SB], dv(DV_NA0 + hp), psk(B2) + KDV, [AA.k()])
                    yield
                    scan(GG.t[:, :], CBT.t[:, CB_CM:CB_CM + SB], SG.t[:, :], 0.0, KCB + [SG.k()], [GG.k()])
                    act(EG.t[:, :], GG.t[:, :], AF.Exp, [GG.k()], [EG.k()], scale=-CW)
                    act(ENG.t[:, :], GG.t[:, :], AF.Exp, [GG.k()], [ENG.k()], scale=CW)
                    yield
                    tt(EGM.t[:, :], GG.t[:, :], SG.t[:, :], ALU.subtract, [GG.k(), SG.k()], [EGM.k()])
                    act(EGM.t[:, :], EGM.t[:, :], AF.Exp, [EGM.k()], [EGM.k()], scale=-CW)
                    cp(GC[hp].t[:, :].unsqueeze(2), r3(EG.t[:, :], CH)[:, :, CH - 1:CH], [EG.k()], [GC[hp].k()])
                    yield
                    ts(KK.t[:, :], KS.t[:, :], pv(PV_KK + hp), None, ALU.mult, ALU.bypass, [KS.k()] + KPV, [KK.k()])
                    tt(SQB.t[:, :], KK.t[:, :], KK.t[:, :], ALU.mult, [KK.k()], [SQB.k()], eng="pool")
                    mm(PS.t[:, B1, SB:2 * SB], bones_bf, SQB.t[:, :], True, True, KCB + [SQB.k()], psk(B1))
                    rpow(RN.t[:, :], PS.t[:, B1, SB:2 * SB], -0.5, psk(B1) + KDV, [RN.k()], bias=dv(DV_EPSKK))
                    yield
                    tt(KK.t[:, :], KK.t[:, :], RN.t[:, :], ALU.mult, [KK.k(), RN.k()], [KK.k()])
                    ts(T1.t[:, :], AA.t[:, :], pv(PV_KA + hp), dv(DV_OKA + hp), ALU.mult, ALU.add,
                       [AA.k()] + KPV + KDV, [T1.k()])
                    tt(KS.t[:, :], KS.t[:, :], T1.t[:, :], ALU.mult, [KS.k(), T1.k()], [KS.k()])
                    tt(BB.t[:, :], KK.t[:, :], AA.t[:, :], ALU.mult, [KK.k(), AA.k()], [BB.k()])
                    yield
                    tt(T1.t[:, :], RS.t[:, :], KS.t[:, :], ALU.mult, [RS.k(), KS.k()], [T1.k()])
                    ts(RKB.t[:, :], T1.t[:, :], pv(PV_RK + hp), None, ALU.mult, ALU.bypass, [T1.k()] + KPV, [RKB.k()])
                    mm(PS.t[:, B2, SB:2 * SB], bones_bf, RKB.t[:, :], True, True, KCB + [RKB.k()], psk(B2))
                    tt(BON[hp].t[:, :], PS.t[:, B2, SB:2 * SB], VB.t[:, hp, :], ALU.mult, psk(B2) + [VB.k(hp)],
                       [BON[hp].k()])
                    yield
                    stt(AR[hp].t[:, :, 0:64], r3(KK.t[:, :], CH), -1.0, r3(EGM.t[:, :], CH), ALU.mult, ALU.mult,
                        [KK.k(), EGM.k()], [AR[hp].k()])
                    tt(AR[hp].t[:, :, 64:128], r3(RS.t[:, :], CH), r3(EG.t[:, :], CH), ALU.mult, [RS.k(), EG.k()],
                       [AR[hp].k()])
                    yield
                    tt(BT[hp].t[:, :], BB.t[:, :], ENG.t[:, :], ALU.mult, [BB.k(), ENG.k()], [BT[hp].k()])
                    tt(KT[hp].t[:, :], KS.t[:, :], ENG.t[:, :], ALU.mult, [KS.k(), ENG.k()], [KT[hp].k()])
                    yield
                    gcb = GC[hp].t[:, :].unsqueeze(2).to_broadcast([128, NCH, CH])
                    tt(r3(BHt[hp].t[:, :], CH), r3(BT[hp].t[:, :], CH), gcb, ALU.mult, [BT[hp].k(), GC[hp].k()],
                       [BHt[hp].k()])
                    tt(r3(KHt[hp].t[:, :], CH), r3(KT[hp].t[:, :], CH), gcb, ALU.mult, [KT[hp].k(), GC[hp].k()],
                       [KHt[hp].k()])

            def stage_P2(sbi, hp):
                par = sbi % 2
                VB, AR, GC, BON = VBp[par], ARp[par], GCp[par], BONp[par]
                for j in range(NCH):
                    cs = slice(j * CH, (j + 1) * CH)
                    for q in range(4):
                        for h in range(2):
                            hs = HS[h]
                            src, sk = ((AR[hp].t[hs, j, 0:64], AR[hp].k()), (BHt[hp].t[hs, cs], BHt[hp].k()),
                                       (KHt[hp].t[hs, cs], KHt[hp].k()), (VB.t[hs, hp, cs], VB.k(hp)))[q]
                            bank = hp * 3 + j // 2
                            col = ((j % 2) * 4 + q) * 64
                            mm(PS.t[hs, bank, col:col + 64], src, ident2[hs, :], True, True, [sk] + KCB, psk(bank))
                for a_ in range(2):
                    bank = hp * 3 + a_
                    cp(TOK[hp].t[:, 2 * a_:2 * a_ + 2, :, :],
                       PS.t[:, bank, :].rearrange("p (a b c) -> p a b c", b=4, c=64), psk(bank), [TOK[hp].k()],
                       eng="act")

            def stage_P1(sbi, hp):
                par = sbi % 2
                VB, AR, GC, BON = VBp[par], ARp[par], GCp[par], BONp[par]
                B0, B1, B2 = hp * 3, hp * 3 + 1, hp * 3 + 2
                for j in range(NCH):
                    cs = slice(j * CH, (j + 1) * CH)
                    for h in range(2):
                        hs = HS[h]
                        mm(PS.t[hs, B0, j * 128:(j + 1) * 128], BT[hp].t[hs, cs], AR[hp].t[hs, j, :], True, True,
                           [BT[hp].k(), AR[hp].k()], psk(B0))
                    for h in range(2):
                        hs = HS[h]
                        mm(PS.t[hs, B1, j * 128:(j + 1) * 128], KT[hp].t[hs, cs], AR[hp].t[hs, j, :], True, True,
                           [KT[hp].k(), AR[hp].k()], psk(B1))
                    for h in range(2):
                        hs = HS[h]
                        mm(PS.t[hs, B2, j * 64:(j + 1) * 64], AR[hp].t[hs, j, 0:64], BT[hp].t[hs, cs], True, True,
                           [BT[hp].k(), AR[hp].k()], psk(B2))
                tt(NTMR[hp].t[:, :, :], r3(PS.t[:, B0, :], 128), rmaskb, ALU.mult, psk(B0) + KCB, [NTMR[hp].k()])
                tt(MKT[hp].t[:, :, :], r3(PS.t[:, B1, :], 128), rmaskb, ALU.mult, psk(B1) + KCB, [MKT[hp].k()])
                tt(NK[hp][0].t[:, :, :], r3(PS.t[:, B2, 0:SB], 64), masklb, ALU.mult, psk(B2) + KCB,
                   [NK[hp][0].k()])

            def stage_P3(sbi, hp):
                par = sbi % 2
                VB, AR, GC, BON = VBp[par], ARp[par], GCp[par], BONp[par]
                B2 = hp * 3 + 2
                for j, h, hs, cs in jh():
                    mm(PS.t[hs, B2, SB + j * 64:SB + (j + 1) * 64], MKT[hp].t[hs, j, 0:64], TOK[hp].t[hs, j, 3, :],
                       True, True, [MKT[hp].k(), TOK[hp].k()], psk(B2))
                cp(MVB[hp].t[:, :, :], r3(PS.t[:, B2, SB:2 * SB], 64), psk(B2), [MVB[hp].k()], eng="act")

            def stage_P4(sbi, hp, r):
                nk, ntq = NK[hp], NTQ[hp]
                B0, B1, B3 = hp * 3, hp * 3 + 1, hp * 3 + 2
                if r == 0:
                    for j in range(NCH):
                        for h in range(2):
                            hs = HS[h]
                            mm(PS.t[hs, B3, j * 64:(j + 1) * 64], NTMR[hp].t[hs, j, 0:64], nk[0].t[hs, j, :], True,
                               True, [NTMR[hp].k(), nk[0].k()], psk(B3))
                        for h in range(2):
                            hs = HS[h]
                            mm(PS.t[hs, B0, j * 128:j * 128 + 64], nk[0].t[hs, j, :], NTMR[hp].t[hs, j, 0:64], True,
                               True, [NTMR[hp].k(), nk[0].k()], psk(B0))
                    cp(nk[1].t[:, :, :], r3(PS.t[:, B3, 0:SB], 64), psk(B3), [nk[1].k()], eng="act")
                    cp(ntq[1].t[:, :, 0:64], r3(PS.t[:, B0, :], 128)[:, :, 0:64], psk(B0), [ntq[1].k()], eng="act")
                    tt(ntq[1].t[:, :, 64:128], NTMR[hp].t[:, :, 0:64], ident2b, ALU.add, [NTMR[hp].k()] + KCB,
                       [ntq[1].k()])
                elif r < 5:
                    cur, nxt = r % 2, 1 - (r % 2)
                    ca = 0 if r % 2 == 0 else SB
                    bb = B0 if r % 2 == 0 else B1
                    for j in range(NCH):
                        for h in range(2):
                            hs = HS[h]
                            mm(PS.t[hs, B3, ca + j * 64:ca + (j + 1) * 64], ntq[cur].t[hs, j, 0:64],
                               nk[cur].t[hs, j, :], True, True, [ntq[cur].k(), nk[cur].k()], psk(B3))
                        for h in range(2):
                            hs = HS[h]
                            if r < 4:
                                mm(PS.t[hs, bb, j * 128:(j + 1) * 128], nk[cur].t[hs, j, :], ntq[cur].t[hs, j, :],
                                   True, True, [ntq[cur].k(), nk[cur].k()], psk(bb))
                            else:
                                mm(PS.t[hs, bb, j * 128 + 64:(j + 1) * 128], nk[cur].t[hs, j, :],
                                   ntq[cur].t[hs, j, 64:128], True, True, [ntq[cur].k(), nk[cur].k()], psk(bb))
                    cp(nk[nxt].t[:, :, :], r3(PS.t[:, B3, ca:ca + SB], 64), psk(B3), [nk[nxt].k()], eng="act")
                    if r < 4:
                        cp(ntq[nxt].t[:, :, 0:64], r3(PS.t[:, bb, :], 128)[:, :, 0:64], psk(bb), [ntq[nxt].k()],
                           eng="act")
                    tt(ntq[nxt].t[:, :, 64:128], ntq[cur].t[:, :, 64:128], r3(PS.t[:, bb, :], 128)[:, :, 64:128],
                       ALU.add, [ntq[cur].k()] + psk(bb), [ntq[nxt].k()])
                else:
                    for j, h, hs, cs in jh():
                        mm(PS.t[hs, B3, j * 64:(j + 1) * 64], nk[1].t[hs, j, :], ntq[1].t[hs, j, 64:128], True, True,
                           [ntq[1].k(), nk[1].k()], psk(B3))
                    tt(QF[hp].t[:, :, :], ntq[1].t[:, :, 64:128], r3(PS.t[:, B3, 0:SB], 64), ALU.add,
                       [ntq[1].k()] + psk(B3), [QF[hp].k()])

            def stage_P5(sbi, hp):
                par = sbi % 2
                VB, AR, GC, BON = VBp[par], ARp[par], GCp[par], BONp[par]
                B0 = hp * 3
                for j in range(NCH):
                    for h in range(2):
                        hs = HS[h]
                        mm(PS.t[hs, B0, j * 64:(j + 1) * 64], TOK[hp].t[hs, j, 0, :], QF[hp].t[hs, j, :], True, True,
                           [TOK[hp].k(), QF[hp].k()], psk(B0))
                    for h in range(2):
                        hs = HS[h]
                        mm(PS.t[hs, B0, SB + j * 64:SB + (j + 1) * 64], QF[hp].t[hs, j, :], MVB[hp].t[hs, j, :],
                           True, True, [MVB[hp].k(), QF[hp].k()], psk(B0))
                cp(WT[hp].t[:, :, :], r3(PS.t[:, B0, 0:SB], 64), psk(B0), [WT[hp].k()], eng="dve")
                cp(UT[hp].t[:, :, :], r3(PS.t[:, B0, SB:2 * SB], 64), psk(B0), [UT[hp].k()], eng="dve")

            def chain_step(sbi, hp, j):
                par = sbi % 2
                VB, AR, GC, BON = VBp[par], ARp[par], GCp[par], BONp[par]
                B2 = hp * 3 + 2
                hcur = HB[hp][hpar[hp]]
                hnxt = HB[hp][1 - hpar[hp]]
                hpar[hp] = 1 - hpar[hp]
                tok, ub = TOK[hp], UB[hp]
                for h in range(2):
                    hs = HS[h]
                    mm(PS.t[hs, B2, 0:64], WT[hp].t[hs, j, :], hcur.t[hs, :], True, True, [WT[hp].k(), hcur.k()],
                       psk(B2))
                tt(ub.t[:, :], PS.t[:, B2, 0:64], UT[hp].t[:, j, :], ALU.add, psk(B2) + [UT[hp].k()], [ub.k()])
                for h in range(2):
                    hs = HS[h]
                    mm(PS.t[hs, B2, 64:128], tok.t[hs, j, 2, :], tok.t[hs, j, 3, :], True, False, [tok.k()], psk(B2))
                for h in range(2):
                    hs = HS[h]
                    mm(PS.t[hs, B2, 64:128], tok.t[hs, j, 1, :], ub.t[hs, :], False, True, [tok.k(), ub.k()], psk(B2))
                for h in range(2):
                    hs = HS[h]
                    yo = PS.t[hs, B2, 128 + j * 64:128 + (j + 1) * 64]
                    mm(yo, hcur.t[hs, :], AR[hp].t[hs, j, 64:128], True, False, [hcur.k(), AR[hp].k()], psk(B2))
                for h in range(2):
                    hs = HS[h]
                    yo = PS.t[hs, B2, 128 + j * 64:128 + (j + 1) * 64]
                    mm(yo, ub.t[hs, :], NTMR[hp].t[hs, j, 64:128], False, False, [ub.k(), NTMR[hp].k()], psk(B2))
                for h in range(2):
                    hs = HS[h]
                    yo = PS.t[hs, B2, 128 + j * 64:128 + (j + 1) * 64]
                    mm(yo, tok.t[hs, j, 3, :], MKT[hp].t[hs, j, 64:128], False, True, [tok.k(), MKT[hp].k()], psk(B2))
                stt(H32[hp].t[:, :], H32[hp].t[:, :], GC[hp].t[:, j:j + 1], PS.t[:, B2, 64:128], ALU.mult, ALU.add,
                    [H32[hp].k(), GC[hp].k()] + psk(B2), [H32[hp].k()])
                cp(hnxt.t[:, :], H32[hp].t[:, :], [H32[hp].k()], [hnxt.k()], eng="act")

            def stage_GN(sbi, hp):
                par = sbi % 2
                VB, AR, GC, BON = VBp[par], ARp[par], GCp[par], BONp[par]
                B1, B2 = hp * 3 + 1, hp * 3 + 2
                tbx = (sbi * SB) // TB
                ssl = slice(sbi * SB, (sbi + 1) * SB)
                YS, YQ, MEANc, VARc = (GNT[hp][n] for n in ("YS", "YQ", "MEAN", "VAR"))
                cp(YS.t[:, :], PS.t[:, B2, 128:128 + SB], psk(B2), [YS.k()], eng="act")
                act(YQ.t[:, :], PS.t[:, B2, 128:128 + SB], AF.Square, psk(B2), [YQ.k()])
                yield
                mm(PS.t[:, B1, 0:SB], bones32, YS.t[:, :], True, True, KCF + [YS.k()], psk(B1))
                mm(PS.t[:, B1, SB:2 * SB], bones32, YQ.t[:, :], True, True, KCF + [YQ.k()], psk(B1))
                act(MEANc.t[:, :], PS.t[:, B1, 0:SB], AF.Identity, psk(B1), [MEANc.k()], scale=1.0 / 64)
                act(VARc.t[:, :], PS.t[:, B1, 0:SB], AF.Square, psk(B1), [VARc.k()], scale=1.0 / 64)
                stt(VARc.t[:, :], PS.t[:, B1, SB:2 * SB], 1.0 / 64, VARc.t[:, :], ALU.mult, ALU.subtract,
                    psk(B1) + [VARc.k()], [VARc.k()])
                yield
                rpow(VARc.t[:, :], VARc.t[:, :], -0.5, [VARc.k()] + KDV, [VARc.k()], bias=dv(DV_EPSGN))
                yield
                tt(YS.t[:, :], YS.t[:, :], MEANc.t[:, :], ALU.subtract, [YS.k(), MEANc.k()], [YS.k()])
                tt(YS.t[:, :], YS.t[:, :], VARc.t[:, :], ALU.mult, [YS.k(), VARc.k()], [YS.k()])
                yield
                act(YS.t[:, :], YS.t[:, :], AF.Identity, [YS.k()] + KPV, [YS.k()],
                    scale=pv(PV_GNW + hp), bias=pv(PV_GNB + hp))
                tt(YS.t[:, :], YS.t[:, :], BON[hp].t[:, :], ALU.add, [YS.k(), BON[hp].k()], [YS.k()])
                yv = Y.t[:, 2 + hp, ssl]
                tt(yv, yv, YS.t[:, :], ALU.mult, [YS.k(), Y.k(2 + hp, tbx)], [Y.k(2 + hp, tbx)])

            def warm(n, bank=0):
                if n <= 0:
                    return

                def fn(e):
                    ins = None
                    for _ in range(n):
                        ins = e.matmul(PS.t[:, bank, :], lhsT=CBT.t[:, CB_ID:CB_ID + 128],
                                       rhs=CBT.t[:, CB_MP:CB_MP + 512], start=True, stop=True)
                    return ins
                P.add("pe", fn, KCB, psk(bank), cost=0.3 * n)

            def gen_Eall(sbi):
                for _ in prologue(sbi):
                    yield
                alive = [stage_E(sbi, 0), stage_E(sbi, 1)]
                while alive:
                    for g_ in list(alive):
                        try:
                            next(g_)
                        except StopIteration:
                            alive.remove(g_)
                        yield

            def gen_PC(sbi):
                warm(WARM_BURST)
                for hp in range(2):
                    stage_P2(sbi, hp)
                    yield
                for hp in range(2):
                    stage_P1(sbi, hp)
                    yield
                for hp in range(2):
                    stage_P3(sbi, hp)
                    yield
                for r in range(6):
                    for hp in range(2):
                        stage_P4(sbi, hp, r)
                        warm(WARM_P4)
                        yield
                for hp in range(2):
                    stage_P5(sbi, hp)
                    yield
                for j in range(NCH):
                    for hp in range(2):
                        chain_step(sbi, hp, j)
                        warm(WARM_CH)
                        yield

            def run_interleaved(*gens):
                alive = list(gens)
                while alive:
                    for g_ in list(alive):
                        try:
                            next(g_)
                        except StopIteration:
                            alive.remove(g_)

            def gen_GN(sbi):
                alive = [stage_GN(sbi, 0), stage_GN(sbi, 1)]
                while alive:
                    for g_ in list(alive):
                        try:
                            next(g_)
                        except StopIteration:
                            alive.remove(g_)
                        yield

            run_interleaved(gen_Eall(0))
            gg = iter(())
            for sbi in range(NSB):
                ge = gen_Eall(sbi + 1) if sbi + 1 < NSB else iter(())
                for i_, _ in enumerate(gen_PC(sbi)):
                    if i_ < E_START:
                        for _k in range(3):
                            next(gg, None)
                    else:
                        if i_ == E_START:
                            run_interleaved(gg)
                        for _k in range(E_CHAIN if i_ >= 20 else 1):
                            next(ge, None)
                run_interleaved(ge)
                gg = gen_GN(sbi)
            run_interleaved(gg)
            for g in range(gbase + 23, gbase + 30):
                wdone(g)

        P.set_fence()
        with ExitStack() as ph:
          if stage >= 5:
            SQ = [sb("LSQ%d" % i, [128, TB], F32, ph) for i in range(4)]
            MEAN = sb("LMEAN", [128, TB], F32, ph)
            RSTD = sb("LRSTD", [128, TB], F32, ph)
            L1 = sb("L1", [128, TB], F32, ph)

            def outproj_tb(tb):
                for j in range(2):
                    wap, wk = wslot(gbase + 30 + j)
                    for mm_ in range(4):
                        m = 4 * j + mm_
                        b = next_bank()
                        for kc in range(2):
                            mm(PS.t[:, b, :], wap[:, kc * 4 + mm_, :], Y.t[:, 2 + kc, tb * TB:(tb + 1) * TB],
                               kc == 0, kc == 1, [wk, Y.k(2 + kc, tb)], psk(b))
                        xs = X32.t[:, m, tb * TB:(tb + 1) * TB]
                        tt(xs, xs, PS.t[:, b, :], ALU.add, psk(b) + [X32.k(m, tb)], [X32.k(m, tb)])

            def ln_tb(tb):
                sl = slice(tb * TB, (tb + 1) * TB)
                s1b, s2b = 2 + 2 * (tb % 2), 3 + 2 * (tb % 2)
                for m in range(KC):
                    mm(PS.t[:, s1b, :], ones32, X32.t[:, m, sl], m == 0, m == KC - 1, KCF + [X32.k(m, tb)], psk(s1b))
                for m in range(KC):
                    sq = SQ[m % 4]
                    act(sq.t[:, :], X32.t[:, m, sl], AF.Square, [X32.k(m, tb)], [sq.k()])
                    mm(PS.t[:, s2b, :], ones32, sq.t[:, :], m == 0, m == KC - 1, KCF + [sq.k()], psk(s2b))
                act(MEAN.t[:, :], PS.t[:, s1b, :], AF.Identity, psk(s1b), [MEAN.k()], scale=1.0 / D)
                act(L1.t[:, :], PS.t[:, s1b, :], AF.Square, psk(s1b), [L1.k()], scale=1.0 / D)
                stt(L1.t[:, :], PS.t[:, s2b, :], 1.0 / D, L1.t[:, :], ALU.mult, ALU.subtract, psk(s2b) + [L1.k()], [L1.k()])
                rpow(RSTD.t[:, :], L1.t[:, :], -0.5, [L1.k()] + KDV, [RSTD.k()], bias=dv(DV_EPSLN))
                tt(MEAN.t[:, :], MEAN.t[:, :], RSTD.t[:, :], ALU.mult, [MEAN.k(), RSTD.k()], [MEAN.k()])
                for m in range(KC):
                    xs = X32.t[:, m, sl]
                    tt(xs, xs, RSTD.t[:, :], ALU.mult, [X32.k(m, tb), RSTD.k()], [X32.k(m, tb)])
                    tt(xs, xs, MEAN.t[:, :], ALU.subtract, [X32.k(m, tb), MEAN.k()], [X32.k(m, tb)])
                    act(xs, xs, AF.Identity, [X32.k(m, tb)] + KPV, [X32.k(m, tb)],
                        scale=pv(PV_LNG + m), bias=pv(PV_LNB + m))
                    if li < NLY - 1:
                        cp(XBF.t[:, m, sl], xs, [X32.k(m, tb)], [XBF.k(m, tb)], eng="act")

            outproj_tb(0)
            for tb in range(NTB):
                if tb + 1 < NTB:
                    outproj_tb(tb + 1)
                ln_tb(tb)
            for j in range(2):
                wdone(gbase + 30 + j)

    outk = []
    for kc in range(KC):
        for tb in range(NTB):
            k = ("out", kc, tb)
            dma(y_out[kc * 128:(kc + 1) * 128, tb * TB:(tb + 1) * TB], X32.t[:, kc, tb * TB:(tb + 1) * TB],
                [X32.k(kc, tb)], [k])
            outk.append(k)
    P.add("sp", None, outk, (), cost=0.01)

    esems = {e: es.enter_context(nc.semaphore("sem_" + e)) for e in Prog.ENG}
    dsems = [es.enter_context(nc.semaphore("dsem%d" % i)) for i in range(NDSEM)]
    block = es.enter_context(nc.Block())
    P.emit(nc, block, esems, dsems)
    es.close()
    return nc, P


_CACHE = {}


def _get_program(layers):
    key = tuple(layers)
    if key not in _CACHE:
        _CACHE[key] = build(list(layers))
    return _CACHE[key]


def run_layers(x, inp, layers):
    ws = _prep_weights(inp)
    pvs, sws = _prep_small(inp)
    cb, cf = _consts()
    ls = list(layers)
    ws = np.ascontiguousarray(ws[ls])
    pvs = np.ascontiguousarray(pvs[ls])
    sws = np.ascontiguousarray(sws[ls])
    nc, _ = _get_program(ls)
    in_maps = []
    for b in range(8):
        in_maps.append({"xT": np.ascontiguousarray(x[b].T), "wst": ws, "pv": pvs, "sw": sws, "cb": cb, "cf": cf})
    res = run_bass_kernel_spmd(nc, in_maps, core_ids=list(range(8)))
    out = np.stack([np.asarray(res.results[b]["yT"]).T for b in range(8)])
    return np.ascontiguousarray(out.astype(np.float32))


def kernel(**inputs):
    x = np.asarray(inputs["x"], np.float32)
    return run_layers(x, inputs, range(DEPTH))
```
